# Optimizing a Trainium2 kernel written in Bass

```python
import jax, jax.numpy as jnp
from jax import lax
import numpy as np

D_MODEL = 1024
BATCH = 16
SEQ = 4096
DEPTH = 2
DEC_BATCH = 32
DEC_SEQ = 2048
PAST_LEN = 128

D_A = D_MODEL
D_B = D_MODEL
D_C = D_MODEL
CHUNK = 128
SGU_HEADS = 8
SGU_HEAD_DIM = D_A // SGU_HEADS
SHORT_CONV_W = 3
CONF_CONV_W = 31
N_BRANCH = 3
D_FF = 4 * D_MODEL
IN_WIDTH = 2 * D_A + 3 * D_B + 2 * D_C + N_BRANCH * D_MODEL
ALPHA = float((2 * DEPTH) ** 0.25)
BETA = float((8 * DEPTH) ** -0.25)
LN_EPS = 1e-5

kernel_name = "hybrid_sgu_shortconv_conformer_encoder"


def _ln(x, g, b):
    xf = x.astype(jnp.float32)
    mu = jnp.mean(xf, axis=-1, keepdims=True)
    var = jnp.mean(jnp.square(xf - mu), axis=-1, keepdims=True)
    return ((xf - mu) * lax.rsqrt(var + LN_EPS) * g + b).astype(x.dtype)


def _dwconv(x, w):
    k = w.shape[0]
    return lax.conv_general_dilated(
        x, w[:, None, :].astype(x.dtype), window_strides=(1,), padding=[(k // 2, k // 2)],
        dimension_numbers=("NWC", "WIO", "NWC"), feature_group_count=x.shape[-1])


def _spatial_gating(u, v, g, b, w_s, b_s):
    bsz, s, _ = v.shape
    v = _ln(v, g, b).reshape(bsz, s // CHUNK, CHUNK, SGU_HEADS, SGU_HEAD_DIM)
    v = jnp.einsum("hpq,bcqhd->bcphd", w_s, v) + jnp.transpose(b_s)[:, :, None]
    return u * v.reshape(bsz, s, D_A)


def _mixer(x, w_in, b_in, sgu_ln_g, sgu_ln_b, sgu_w, sgu_b, sconv_w, cconv_w, cconv_b,
           cnorm_g, cnorm_b, w_branch, w_out, b_out):
    proj = jnp.einsum("bsd,de->bse", x, w_in) + b_in
    sizes = [D_A, D_A, D_B, D_B, D_B, D_C, D_C, D_MODEL, D_MODEL]
    idx = [int(i) for i in np.cumsum(sizes)]
    a_u, a_v, bg, cg, h, glu_a, glu_b, gate_a, gate_b, gate_c = jnp.split(proj, idx, axis=-1)
    y_a = _spatial_gating(jax.nn.gelu(a_u, approximate=False), jax.nn.gelu(a_v, approximate=False),
                          sgu_ln_g, sgu_ln_b, sgu_w, sgu_b)
    y_b = bg * _dwconv(cg * h, sconv_w)
    c = glu_a * jax.nn.sigmoid(glu_b)
    c = _dwconv(c, cconv_w) + cconv_b
    y_c = jax.nn.silu(_ln(c, cnorm_g, cnorm_b))
    merged = (jax.nn.sigmoid(gate_a) * jnp.einsum("bsc,cd->bsd", y_a, w_branch[0])
              + jax.nn.sigmoid(gate_b) * jnp.einsum("bsc,cd->bsd", y_b, w_branch[1])
              + jax.nn.sigmoid(gate_c) * jnp.einsum("bsc,cd->bsd", y_c, w_branch[2]))
    return jnp.einsum("bsd,de->bse", merged, w_out) + b_out


def _layer(x, w_in, b_in, sgu_ln_g, sgu_ln_b, sgu_w, sgu_b, sconv_w, cconv_w, cconv_b,
           cnorm_g, cnorm_b, w_branch, w_out, b_out, ln1_g, ln1_b,
           w_ff1, b_ff1, w_ff2, b_ff2, ln2_g, ln2_b):
    mix = _mixer(x, w_in, b_in, sgu_ln_g, sgu_ln_b, sgu_w, sgu_b, sconv_w, cconv_w, cconv_b,
                 cnorm_g, cnorm_b, w_branch, w_out, b_out)
    x = _ln(ALPHA * x + mix, ln1_g, ln1_b)
    hid = jnp.square(jax.nn.relu(jnp.einsum("bsd,df->bsf", x, w_ff1) + b_ff1))
    ff = jnp.einsum("bsf,fd->bsd", hid, w_ff2) + b_ff2
    return _ln(ALPHA * x + ff, ln2_g, ln2_b)


def _trunk(x, ln_in_g, ln_in_b, layer_params):
    x = _ln(x, ln_in_g, ln_in_b)
    for l in range(DEPTH):
        x = _layer(x, *[p[l] for p in layer_params])
    return x


def setup_inputs(seed: int = 0) -> dict:
    key = jax.random.key(seed)
    ks = jax.random.split(key, 32)
    f32 = jnp.float32
    nrm = lambda k, shape, s: jax.random.normal(k, shape, f32) * s
    L = DEPTH
    return {
        "x_prompt": nrm(ks[0], (BATCH, SEQ, D_MODEL), 1.0),
        "x_sample": nrm(ks[1], (DEC_BATCH, DEC_SEQ, D_MODEL), 1.0),
        "ln_in_g": 1.0 + nrm(ks[2], (D_MODEL,), 0.05),
        "ln_in_b": nrm(ks[3], (D_MODEL,), 0.02),
        "w_in": nrm(ks[4], (L, D_MODEL, IN_WIDTH), D_MODEL ** -0.5),
        "b_in": nrm(ks[5], (L, IN_WIDTH), 0.02),
        "sgu_ln_g": 1.0 + nrm(ks[6], (L, D_A), 0.05),
        "sgu_ln_b": nrm(ks[7], (L, D_A), 0.02),
        "sgu_w": nrm(ks[8], (L, SGU_HEADS, CHUNK, CHUNK), CHUNK ** -0.5),
        "sgu_b": 1.0 + nrm(ks[9], (L, SGU_HEADS, CHUNK), 0.05),
        "sconv_w": nrm(ks[10], (L, SHORT_CONV_W, D_B), SHORT_CONV_W ** -0.5),
        "cconv_w": nrm(ks[11], (L, CONF_CONV_W, D_C), CONF_CONV_W ** -0.5),
        "cconv_b": nrm(ks[12], (L, D_C), 0.02),
        "cnorm_g": 1.0 + nrm(ks[13], (L, D_C), 0.05),
        "cnorm_b": nrm(ks[14], (L, D_C), 0.02),
        "w_branch": nrm(ks[15], (L, N_BRANCH, D_MODEL, D_MODEL), BETA * D_MODEL ** -0.5),
        "w_out": nrm(ks[16], (L, D_MODEL, D_MODEL), BETA * D_MODEL ** -0.5),
        "b_out": nrm(ks[17], (L, D_MODEL), 0.02),
        "ln1_g": 1.0 + nrm(ks[18], (L, D_MODEL), 0.05),
        "ln1_b": nrm(ks[19], (L, D_MODEL), 0.02),
        "w_ff1": nrm(ks[20], (L, D_MODEL, D_FF), D_MODEL ** -0.5),
        "b_ff1": nrm(ks[21], (L, D_FF), 0.02),
        "w_ff2": nrm(ks[22], (L, D_FF, D_MODEL), BETA * D_FF ** -0.5),
        "b_ff2": nrm(ks[23], (L, D_MODEL), 0.02),
        "ln2_g": 1.0 + nrm(ks[24], (L, D_MODEL), 0.05),
        "ln2_b": nrm(ks[25], (L, D_MODEL), 0.02),
    }


def reference(x_prompt, x_sample, ln_in_g, ln_in_b, w_in, b_in, sgu_ln_g, sgu_ln_b, sgu_w, sgu_b,
              sconv_w, cconv_w, cconv_b, cnorm_g, cnorm_b, w_branch, w_out, b_out,
              ln1_g, ln1_b, w_ff1, b_ff1, w_ff2, b_ff2, ln2_g, ln2_b):
    layer_params = (w_in, b_in, sgu_ln_g, sgu_ln_b, sgu_w, sgu_b, sconv_w, cconv_w, cconv_b,
                    cnorm_g, cnorm_b, w_branch, w_out, b_out, ln1_g, ln1_b,
                    w_ff1, b_ff1, w_ff2, b_ff2, ln2_g, ln2_b)
    y_prompt = _trunk(x_prompt, ln_in_g, ln_in_b, layer_params)
    y_sample = _trunk(x_sample, ln_in_g, ln_in_b, layer_params)
    return (y_prompt, y_sample)
```

```python
import contextlib
import numpy as np
import concourse.bass as bass
import concourse.mybir as mybir
from concourse.bass_utils import run_bass_kernel_spmd

F32 = mybir.dt.float32
BF16 = mybir.dt.bfloat16
AF = mybir.ActivationFunctionType
ALU = mybir.AluOpType

D = 1024
NCORES = 8
T = 512
HALO = 15
WIN = T + 2 * HALO
NBLK = 52
NSLOT = 3
ALPHA = float(4 ** 0.25)
EPS = 1e-5
TOK_PER_CORE = 16384
SEQS = [(0, 8), (4096, 8), (8192, 4), (10240, 4), (12288, 4), (14336, 4)]

SEG_U, SEG_V, SEG_BG, SEG_CG, SEG_H, SEG_GA, SEG_GB, SEG_G0 = 0, 1, 2, 3, 4, 5, 6, 7

CF_BIN = 0
CF_BFF1 = 80
CF_CCB = 112
CF_CNG = 120
CF_CNB = 128
CF_W3 = 136
CF_N = 160


def layer_schedule():
    sch = []
    cb = []
    for m in range(8):
        cb += [(SEG_CG, m), (SEG_H, m), (SEG_BG, m)]
    B = lambda kind, p: sch.append(("blk", kind, p))
    A = lambda name: sch.append(("act", name, None))
    B("u", 0); B("u", 1); B("v", 0); B("v", 1); A("haloT"); A("tail")
    B("cc", 0); B("cc", 1); A("vln0"); B("dg", 0); B("dg", 1)
    B("cc", 2); A("vln1"); B("dg", 2); B("dg", 3)
    B("cc", 3); A("vln2"); B("dg", 4); B("dg", 5)
    B("cb", cb[0:4]); A("vln3"); B("dg", 6); B("dg", 7); A("cnorm")
    for i in range(1, 6):
        B("cb", cb[4 * i:4 * i + 4])
    A("flush"); A("sgu")
    for mb in range(2):
        for n in range(3):
            B("gate", (n, mb)); B("br", (n, mb))
    B("out", 0); B("out", 1); A("ln1")
    for mb in range(8):
        B("ff1", mb)
    for h in range(2):
        for kb in range(4):
            B("ff2", (h, kb))
    A("ln2")
    return sch


SCHED = layer_schedule()
PLAN = [(k, p) for (t, k, p) in SCHED if t == "blk"]
assert len(PLAN) == NBLK


def in_groups(kind, p):
    if kind == "u":
        return [(SEG_U, 4 * p + j) for j in range(4)]
    if kind == "v":
        return [(SEG_V, 4 * p + j) for j in range(4)]
    if kind == "cc":
        return [(SEG_GB, 2 * p), (SEG_GA, 2 * p), (SEG_GB, 2 * p + 1), (SEG_GA, 2 * p + 1)]
    if kind == "cb":
        return p
    if kind == "gate":
        n, mb = p
        return [(SEG_G0 + n, 4 * mb + j) for j in range(4)]
    return None


def _pack(W):
    return np.ascontiguousarray(W.reshape(8, 128, 512).transpose(1, 0, 2)).reshape(128, 4096)


def pack_weights(w_in, w_branch, w_out, w_ff1, w_ff2, cconv_w):
    out = np.empty((2, NBLK, 128, 4096), np.float32)
    ar = np.arange(128)
    for l in range(2):
        for b, (kind, p) in enumerate(PLAN):
            grp = in_groups(kind, p)
            if grp is not None:
                cols = np.concatenate([np.arange(sg * 1024 + m * 128, sg * 1024 + m * 128 + 128) for (sg, m) in grp])
                W = w_in[l][:, cols]
            elif kind == "dg":
                blk = np.zeros((128, 32, 128), np.float32)
                for j in range(31):
                    blk[ar, j, ar] = cconv_w[l, j, p * 128:(p + 1) * 128]
                out[l, b] = blk.reshape(128, 4096)
                continue
            elif kind == "br":
                n, mb = p
                W = w_branch[l, n][:, mb * 512:(mb + 1) * 512]
            elif kind == "out":
                W = w_out[l][:, p * 512:(p + 1) * 512]
            elif kind == "ff1":
                W = w_ff1[l][:, p * 512:(p + 1) * 512]
            else:
                h, kb = p
                W = w_ff2[l][kb * 1024:(kb + 1) * 1024, h * 512:(h + 1) * 512]
            out[l, b] = _pack(W)
    return out


class Buf:
    __slots__ = ("w", "r")

    def __init__(self):
        self.w = None
        self.r = {}


class Sched:
    ENG = ("sp", "act", "dve", "pool", "pe")

    def __init__(self):
        self.streams = {k: [] for k in self.ENG}
        self.cnt = {}
        self.seen = {k: {} for k in self.ENG}
        self.lazy = []

    def _waits(self, eng, reads, writes):
        deps = {}

        def add(tok, raw):
            k, v = tok
            if k == eng and eng == "pe":
                return
            if k not in self.ENG:
                v = self.cnt[k]
            if deps.get(k, 0) < v:
                deps[k] = v
        for b in reads:
            if b.w is not None:
                add(b.w, True)
        for b in writes:
            if b.w is not None:
                add(b.w, False)
            for k, v in b.r.items():
                add((k, v), False)
        seen = self.seen[eng]
        for k, v in deps.items():
            if seen.get(k, 0) < v:
                self.streams[eng].append(("wait", k, v))
                seen[k] = v

    def _mark(self, tok, reads, writes):
        k, v = tok
        for b in reads:
            if b.r.get(k, 0) < v:
                b.r[k] = v
        for b in writes:
            b.w = tok
            b.r = {}

    def op(self, eng, fn, reads=(), writes=(), signal=True):
        self._waits(eng, reads, writes)
        c = self.cnt.get(eng, 0)
        if signal:
            c += 1
            self.cnt[eng] = c
            self.streams[eng].append(("op", fn, eng, 1))
            tok = (eng, c)
        else:
            self.streams[eng].append(("op", fn, None, 0))
            tok = (eng, c + 1)
        self._mark(tok, reads, writes)

    def dma(self, eng, fn, dsem, reads=(), writes=()):
        self._waits(eng, reads, writes)
        c = self.cnt.get(dsem, 0) + 16
        self.cnt[dsem] = c
        self.streams[eng].append(("op", fn, dsem, 16))
        self._mark((dsem, c), reads, writes)

    def defer(self, eng, fn, reads=(), writes=()):
        self.lazy.append((eng, fn, list(reads), list(writes)))

    def pump(self, n=None):
        while self.lazy and (n is None or n > 0):
            eng, fn, r, w = self.lazy.pop(0)
            self.op(eng, fn, reads=r, writes=w)
            if n is not None:
                n -= 1

    def wait_all(self, eng, keys):
        for k in keys:
            v = self.cnt.get(k, 0)
            if v and self.seen[eng].get(k, 0) < v:
                self.streams[eng].append(("wait", k, v))
                self.seen[eng][k] = v


def build_program(SEQS=SEQS):
    TOK_PER_CORE = sum(n for _, n in SEQS) * T
    nc = bass.Bass("TRN2", target_bir_lowering=False)
    xin = nc.dram_tensor("xin", [TOK_PER_CORE, D], F32, kind="ExternalInput").ap()
    wpk32 = nc.dram_tensor("wpk32", [2, NBLK, 128, 4096], F32, kind="ExternalInput").ap()
    cfm = nc.dram_tensor("cfm", [128, 2 * CF_N], F32, kind="ExternalInput").ap()
    brow32 = nc.dram_tensor("brow32", [2, 4, D], F32, kind="ExternalInput").ap()
    lnp = nc.dram_tensor("lnp", [14, 128, D], F32, kind="ExternalInput").ap()
    wst32 = nc.dram_tensor("wst32", [2, 128, 8 * 128], F32, kind="ExternalInput").ap()
    ident = nc.dram_tensor("ident", [128, 128], F32, kind="ExternalInput").ap()
    yout = nc.dram_tensor("yout", [TOK_PER_CORE, D], F32, kind="ExternalOutput").ap()
    wpk = nc.dram_tensor("wpk", [2, NBLK, 128, 4096], BF16, kind="Internal").ap()

    S = Sched()
    es = contextlib.ExitStack()

    def sb(name, shape, dt):
        return es.enter_context(nc.sbuf_tensor(name, shape, dt))

    with es:
        XT = [sb(f"xt{i}", [128, 4, D], F32) for i in range(3)]
        XF = [sb(f"xf{i}", [128, 8, WIN], BF16) for i in range(2)]
        WR = [sb(f"wr{i}", [128, 8, 512], BF16) for i in range(NSLOT)]
        R = sb("rbig", [128, 32, 512], BF16)
        CP = sb("cp", [128, 8, 512], BF16)
        SQ = sb("sq", [128, 8, 512], BF16)
        SGT = [sb(f"sgt{i}", [128, 4, 512], BF16) for i in range(2)]
        MG = sb("mg", [128, 4, 512], F32)
        CG = [sb(f"cg{i}", [128, WIN], F32) for i in range(2)]
        CGH = [sb(f"cgh{i}", [128, WIN], F32) for i in range(2)]
        SG = CG
        CCB = [sb(f"ccb{i}", [128, WIN], BF16) for i in range(4)]
        ACC = [sb(f"acc{i}", [128, 512], F32) for i in range(2)]
        TM1 = [sb(f"tm1{i}", [128, 512], F32) for i in range(2)]
        RR = [sb(f"rr{i}", [128, 512], F32) for i in range(2)]
        ZT = RR
        MEAN = TM1[0]
        E2 = sb("e2", [128, 512], F32)
        RSTD = TM1[1]
        NEGH = sb("negh", [128, 1], F32)
        GBT = [[sb(f"gb{i}{j}", [128, D], F32) for j in range(2)] for i in range(1)]
        XB = sb("xb", [128, 2, D], BF16)
        LNT = XB[:].rearrange("p a c -> p (a c)").bitcast(F32)
        assert tuple(LNT.shape) == (128, D), LNT.shape
        IDB = sb("idb", [128, 128], BF16)
        HB = sb("hb", [128, 8, 2 * HALO], BF16)
        XH = MG[:, 0:2, :].rearrange("p a c -> p (a c)")
        CF = sb("cf", [128, 2 * CF_N], F32)
        IDT = sb("idt", [128, 128], F32)
        ONES = sb("ones", [128, 128], BF16)
        BROW = sb("brow", [128, 2, 2 * D], BF16)
        WST = sb("wst", [128, 2, 8 * 128], BF16)
        NLN = 8
        ST = [sb(f"st{i}", [128, 12], F32) for i in range(NLN)]
        MV = [sb(f"mv{i}", [128, 2], F32) for i in range(NLN)]
        VE = [sb(f"ve{i}", [128, 1], F32) for i in range(NLN)]
        RS = [sb(f"rs{i}", [128, 1], F32) for i in range(NLN)]
        EPST = sb("epst", [128, 1], F32)
        PP = [es.enter_context(nc.psum_tensor(f"pp{i}", [128, 1024], F32)) for i in range(4)]

        keys = ["pe", "act", "dve", "pool", "misc", "xh", "gb0", "gb1"] + [f"w{i}" for i in range(NSLOT)] + \
               [f"c{l}_{q}" for l in range(2) for q in range(NBLK // 4)] + \
               [f"{a}{s_}{g}" for a in "xo" for s_ in range(3) for g in range(4)]
        sems = {k: es.enter_context(nc.semaphore(k)) for k in keys}

        bXT = [[Buf() for _ in range(4)] for _ in range(3)]
        bXFc = [Buf() for _ in range(2)]
        bXFh = [Buf() for _ in range(2)]
        bWR = [Buf() for _ in range(NSLOT)]
        bR = [Buf() for _ in range(32)]
        bCP = [Buf() for _ in range(8)]
        bSQ = [Buf() for _ in range(8)]
        bSGT = [Buf() for _ in range(2)]
        bMG = [Buf() for _ in range(4)]
        bXHL = [bMG[0], bMG[1]]
        bCG = [Buf() for _ in range(2)]
        bCGH = [Buf() for _ in range(2)]
        bSG = bCG
        bCCB = [Buf() for _ in range(4)]
        bACC = [Buf() for _ in range(2)]
        bTM1 = [Buf() for _ in range(2)]
        bRR = [Buf() for _ in range(2)]
        bZT = bRR
        bE2 = Buf()
        bSTATL = [bTM1[0], bTM1[1], bE2]
        bGB = [Buf() for _ in range(2)]
        bXB = [Buf(), Buf()]
        bHB = Buf()
        bXH = None
        bCONST = Buf()
        bLN = [Buf() for _ in range(8)]
        bPB = [[Buf(), Buf()] for _ in range(4)]
        bWPK = [[Buf() for _ in range(NBLK // 4)] for _ in range(2)]

        st = {"pp": 0, "misc": 0, "ln": 0, "gb": 0, "tmp": 0, "gbcur": -1}

        def next_bank():
            p = st["pp"]
            st["pp"] = (p + 1) % 8
            return p // 2, p % 2

        def next_pair():
            p = st["pp"]
            if p % 2:
                p = (p + 1) % 8
            st["pp"] = (p + 2) % 8
            return p // 2

        def misc_bank():
            h = st["misc"]
            st["misc"] = 1 - h
            return 3, h

        def bank_ap(i, h, n=512):
            return PP[i][:, h * 512:h * 512 + n]

        castq = [(l, q) for l in range(2) for q in range(NBLK // 4)]

        def pump_cast(n=None, upto=None):
            while castq and (n is None or n > 0):
                if upto is not None and castq[0] > upto:
                    break
                l, q = castq.pop(0)
                for b in range(4 * q, 4 * q + 4):
                    S.dma("pool", lambda e, l=l, b=b: e.dma_start(
                        out=wpk[l, b].rearrange("(a r) c -> a (r c)", a=16),
                        in_=wpk32[l, b].rearrange("(a r) c -> a (r c)", a=16)), f"c{l}_{q}", writes=[bWPK[l][q]])
                if n is not None:
                    n -= 1
        S.dma("sp", lambda e: e.dma_start(out=CF[:], in_=cfm[:, :]), "misc", writes=[bCONST])
        S.dma("sp", lambda e: e.dma_start(out=IDT[:], in_=ident[:, :]), "misc", writes=[bCONST])
        S.wait_all("sp", ["misc"])
        for l in range(2):
            for r in range(3):
                S.dma("sp", lambda e, l=l, r=r: e.dma_start(out=XH[32 * r:32 * r + 1, :], in_=brow32[l, r:r + 1, :]),
                      "misc", writes=bXHL)
            S.wait_all("sp", ["misc"])
            for r in range(3):
                S.op("dve", lambda e, l=l, r=r: e.tensor_copy(out=BROW[32 * r:32 * r + 1, l, 0:D],
                                                              in_=XH[32 * r:32 * r + 1, :]),
                     reads=bXHL, writes=[bCONST])
            S.dma("sp", lambda e, l=l: e.dma_start(out=XH[0:1, :], in_=brow32[l, 3:4, :]), "misc", writes=bXHL)
            S.wait_all("sp", ["misc"])
            S.op("dve", lambda e, l=l: e.tensor_copy(out=BROW[0:1, l, D:2 * D], in_=XH[0:1, :]),
                 reads=bXHL, writes=[bCONST])
            S.dma("sp", lambda e, l=l: e.dma_start(out=LNT, in_=wst32[l]), "misc", writes=bXB)
            S.wait_all("sp", ["misc"])
            S.op("dve", lambda e, l=l: e.tensor_copy(out=WST[:, l, :], in_=LNT), reads=bXB, writes=[bCONST])
        S.op("dve", lambda e: e.memset(ONES[:], 1.0), writes=[bCONST])
        S.op("dve", lambda e: e.tensor_copy(out=IDB[:], in_=IDT[:]), reads=[bCONST], writes=[bCONST])
        S.op("dve", lambda e: e.memset(NEGH[:], -0.5), writes=[bCONST])
        S.op("dve", lambda e: e.memset(EPST[:], EPS), writes=[bCONST])
        S.op("dve", lambda e: e.memset(XH[:], 0.0), reads=[], writes=bXHL)

        tile_layers = []
        for si, (row0, ntile) in enumerate(SEQS):
            for j in range(ntile):
                tile_layers.append(("L1", si, j))
                if j >= 1:
                    tile_layers.append(("L2", si, j - 1))
            tile_layers.append(("L2", si, ntile - 1))
        gblocks = []
        for (kind, si, j) in tile_layers:
            l = 0 if kind == "L1" else 1
            for b in range(NBLK):
                gblocks.append((l, b))
        wst_ = {"issued": 0, "used": 0}

        def issue_loads(upto):
            while wst_["issued"] < min(upto, len(gblocks)):
                i = wst_["issued"]
                l, b = gblocks[i]
                slot = i % NSLOT
                pump_cast(upto=(l, b // 4))
                S.dma("sp", lambda e, l=l, b=b, slot=slot: e.dma_start(
                    out=WR[slot][:].rearrange("p k c -> p (k c)"), in_=wpk[l, b]),
                    f"w{slot}", reads=[bWPK[l][b // 4]], writes=[bWR[slot]])
                wst_["issued"] += 1

        def next_block(prefetch=True):
            i = wst_["used"]
            issue_loads(i + NSLOT if prefetch else i + 1)
            if i % 2 == 0:
                pump_cast(1)
            wst_["used"] += 1
            return i % NSLOT

        def load_gb(idx):
            s = 0
            st["gbcur"] = idx
            S.dma("pool", lambda e: e.dma_start(out=GBT[s][0][:], in_=lnp[idx]), f"gb{s}", writes=[bGB[s]])
            S.dma("pool", lambda e: e.dma_start(out=GBT[s][1][:], in_=lnp[idx + 1]), f"gb{s}", writes=[bGB[s]])
            return s

        def ln_stats(src, rbufs, npart=128):
            k = st["ln"]
            st["ln"] = (k + 1) % NLN
            P = slice(0, npart)
            S.op("dve", lambda e: e.bn_stats(out=ST[k][P, 0:6], in_=src[:, 0:512]), reads=rbufs, writes=[bLN[k]])
            S.op("dve", lambda e: e.bn_stats(out=ST[k][P, 6:12], in_=src[:, 512:1024]), reads=rbufs, writes=[bLN[k]])
            S.op("dve", lambda e: e.bn_aggr(out=MV[k][P, :], in_=ST[k][P, :]), reads=[bLN[k]], writes=[bLN[k]])
            S.op("pool", lambda e: e.tensor_tensor(out=VE[k][P, :], in0=MV[k][P, 1:2], in1=EPST[P, :], op=ALU.add),
                 reads=[bLN[k], bCONST], writes=[bLN[k]])
            S.op("pool", lambda e: e.tensor_tensor(out=RS[k][P, :], in0=VE[k][P, :], in1=NEGH[P, 0:1], op=ALU.pow),
                 reads=[bLN[k], bCONST], writes=[bLN[k]])
            return k

        def ln_apply(k, src, dst, rbufs, wbufs, gs, npart=128, tmp=None, tbufs=None):
            P = slice(0, npart)
            if tmp is None:
                tmp, tbufs = src, list(rbufs)
            S.op("dve", lambda e: e.scalar_tensor_tensor(out=tmp, in0=src, scalar=MV[k][P, 0:1],
                                                         in1=GBT[gs][0][P, :], op0=ALU.subtract, op1=ALU.mult),
                 reads=list(rbufs) + [bLN[k], bGB[gs]], writes=tbufs)
            S.op("dve", lambda e: e.scalar_tensor_tensor(out=dst, in0=tmp, scalar=RS[k][P, 0:1],
                                                         in1=GBT[gs][1][P, :], op0=ALU.mult, op1=ALU.add),
                 reads=list(tbufs) + [bLN[k], bGB[gs]], writes=wbufs)

        def ln_tm(src, dst, rbufs, wbufs, gs, npart=128, tmp=None, tbufs=None):
            k = ln_stats(src, rbufs, npart)
            ln_apply(k, src, dst, rbufs, wbufs, gs, npart, tmp, tbufs)

        def ln_groups(s, gs, pre=None):
            ks = []
            for g in range(4):
                if pre is not None:
                    pre(g)
                ks.append(ln_stats(XT[s][:, g, :], [bXT[s][g]]))
                if g >= 1:
                    ln_apply(ks[g - 1], XT[s][:, g - 1, :], XT[s][:, g - 1, :], [bXT[s][g - 1]], [bXT[s][g - 1]], gs)
            ln_apply(ks[3], XT[s][:, 3, :], XT[s][:, 3, :], [bXT[s][3]], [bXT[s][3]], gs)

        PPB = [p_.bitcast(BF16) for p_ in PP]

        def to_fm(s, xs, dstF, dst_bufs, col0):
            def cast(g):
                xb = g % 2
                S.op("pool", lambda e, g=g, xb=xb: e.tensor_copy(out=XB[:, xb, :], in_=XT[xs][:, g, :]),
                     reads=[bXT[xs][g]], writes=[bXB[xb]])
            cast(0)
            for g in range(4):
                xb = g % 2
                if g + 1 < 4 and g == 0:
                    cast(1)
                for mq in range(2):
                    i, h = misc_bank()
                    for mm in range(4):
                        m = mq * 4 + mm
                        S.op("pe", lambda e, i=i, h=h, xb=xb, m=m, mm=mm: e.transpose(
                            out=PPB[i][:, h * 1024 + mm * 128:h * 1024 + (mm + 1) * 128],
                            in_=XB[:, xb, m * 128:(m + 1) * 128], identity=IDB[:]),
                            reads=[bXB[xb], bCONST], writes=[bPB[i][h]], signal=(mm == 3))
                    S.op("act", lambda e, i=i, h=h, g=g, mq=mq: e.activation(
                        out=dstF[:, mq * 4:mq * 4 + 4, col0 + g * 128:col0 + (g + 1) * 128],
                        in_=PPB[i][:, h * 1024:h * 1024 + 512].rearrange("p (m c) -> p m c", c=128), func=AF.Copy),
                        reads=[bPB[i][h]], writes=dst_bufs)
                if 1 <= g and g + 1 < 4:
                    pass
                if g + 2 < 4:
                    cast(g + 2)

        tcount = {"n": 0}
        slot_of = {}

        def cfc(l, base, idx):
            c = l * CF_N + base + idx
            return CF[:, c:c + 1]

        tiles = [(si, j) for si, (row0, ntile) in enumerate(SEQS) for j in range(ntile)]
        gidx = {t: i for i, t in enumerate(tiles)}

        def stage_a(si, j):
            row0, ntile = SEQS[si]
            r0 = row0 + j * T
            s = gidx[(si, j)] % 3
            first, last = (j == 0), (j == ntile - 1)
            for g in range(4):
                S.dma("sp", lambda e, g=g: e.dma_start(out=XT[s][:, g, :], in_=xin[r0 + g * 128:r0 + (g + 1) * 128, :]),
                      f"x{s}{g}", writes=[bXT[s][g]])
            if not first:
                S.dma("sp", lambda e: e.dma_start(out=XH[0:HALO, :], in_=xin[r0 - HALO:r0, :]), "xh", writes=bXHL)
            if not last:
                S.dma("sp", lambda e: e.dma_start(out=XH[HALO:2 * HALO, :], in_=xin[r0 + T:r0 + T + HALO, :]), "xh",
                      writes=bXHL)
            gs = load_gb(0)
            NP = 2 * HALO
            ln_tm(XH[0:NP, :], XH[0:NP, :], bXHL, bXHL, gs, npart=NP)
            ln_groups(s, gs)
            pendingH.append(1)

        def halo_t():
            if not pendingH:
                return
            pendingH.pop()
            NP = 2 * HALO
            i, h = misc_bank()
            for m in range(8):
                S.op("pe", lambda e, i=i, h=h, m=m: e.transpose(out=PP[i][:, h * 512 + m * NP:h * 512 + (m + 1) * NP],
                                                                  in_=XH[0:NP, m * 128:(m + 1) * 128],
                                                                  identity=IDT[0:NP, 0:NP]),
                     reads=bXHL + [bCONST], writes=[bPB[i][h]], signal=(m == 7))
            pv = PP[i][:, h * 512:h * 512 + 8 * NP].rearrange("p (m c) -> p m c", c=NP)
            S.op("act", lambda e: e.activation(out=HB[:], in_=pv, func=AF.Copy), reads=[bPB[i][h]], writes=[bHB])

        def stage_b(si, j):
            s = gidx[(si, j)] % 3
            sf = gidx[(si, j)] % 2
            halo_t()
            to_fm(s, s, XF[sf], [bXFc[sf]], HALO)
            S.op("pool", lambda e: e.tensor_copy(out=XF[sf][:, :, 0:HALO], in_=HB[:, :, 0:HALO]),
                 reads=[bHB], writes=[bXFh[sf]])
            S.op("pool", lambda e: e.tensor_copy(out=XF[sf][:, :, HALO + T:WIN], in_=HB[:, :, HALO:2 * HALO]),
                 reads=[bHB], writes=[bXFh[sf]])

        pendingB = []
        pendingT = []
        pendingH = []

        def layer(l, si, j):
            row0, ntile = SEQS[si]
            s = gidx[(si, j)] % 3
            sf = gidx[(si, j)] % 2
            first, last = (j == 0), (j == ntile - 1)
            xf, bxf, bxfh = XF[sf], bXFc[sf], bXFh[sf]
            GU = lambda m: R[:, m, :]
            YA = lambda m: R[:, 8 + m, :]
            YB = lambda m: R[:, 16 + m, :]
            VNH = lambda g, hh: R[:, 24 + 2 * g + hh, :]
            bGU, bYA, bYB = bR[0:8], bR[8:16], bR[16:24]
            bVN = lambda g: bR[24 + 2 * g:26 + 2 * g]
            LNB = 2 + 6 * l

            def fm_mm(slot, cgi, rhs_of_k, rbufs, halo):
                if halo:
                    i = next_pair()
                    for k in range(8):
                        S.op("pe", lambda e, k=k, i=i: e.matmul(PP[i][:, 0:512], lhsT=WR[slot][:, k, cgi * 128:(cgi + 1) * 128],
                                                                  rhs=xf[:, k, 0:512], start=(k == 0), stop=(k == 7)),
                             reads=[bWR[slot]] + rbufs, writes=[bPB[i][0]], signal=False)
                    for k in range(8):
                        S.op("pe", lambda e, k=k, i=i: e.matmul(PP[i][:, 512:WIN], lhsT=WR[slot][:, k, cgi * 128:(cgi + 1) * 128],
                                                                  rhs=xf[:, k, 512:WIN], start=(k == 0), stop=(k == 7)),
                             reads=[bWR[slot]] + rbufs, writes=[bPB[i][1]], signal=(k == 7))
                    return PP[i][:, 0:WIN], [bPB[i][0], bPB[i][1]]
                i, h = next_bank()
                for k in range(8):
                    S.op("pe", lambda e, k=k, i=i, h=h: e.matmul(bank_ap(i, h), lhsT=WR[slot][:, k, cgi * 128:(cgi + 1) * 128],
                                                                  rhs=rhs_of_k(k), start=(k == 0), stop=(k == 7)),
                         reads=[bWR[slot]] + rbufs, writes=[bPB[i][h]], signal=(k == 7))
                return bank_ap(i, h), [bPB[i][h]]

            def tm_mm(slot, lhs_of_k, lbufs, brow_r, col0, nk=8):
                outs = []
                for g in range(4):
                    i, h = next_bank()
                    S.op("pe", lambda e, i=i, h=h: e.matmul(bank_ap(i, h), lhsT=ONES[32 * brow_r:32 * brow_r + 1, :],
                                                             rhs=BROW[32 * brow_r:32 * brow_r + 1, l, col0:col0 + 512],
                                                             start=True, stop=False, skip_group_check=True),
                         reads=[bCONST], writes=[bPB[i][h]], signal=False)
                    for k in range(nk):
                        S.op("pe", lambda e, i=i, h=h, k=k, g=g: e.matmul(bank_ap(i, h), lhsT=lhs_of_k(k, g),
                                                                           rhs=WR[slot][:, k, :], start=False, stop=(k == nk - 1),
                                                                           skip_group_check=True),
                             reads=[bWR[slot]] + lbufs(k), writes=[bPB[i][h]], signal=(k == nk - 1))
                    outs.append((i, h))
                return outs

            tmpi = {"c": 0}

            def rot():
                tmpi["c"] ^= 1
                return tmpi["c"]

            WRF = lambda slot: WR[slot][:].rearrange("p k c -> p (k c)")
            acc_of = {}
            cstate = {}

            def blk_u(p, slot):
                for cgi, (seg, m) in enumerate(in_groups("u", p)):
                    ps, pbufs = fm_mm(slot, cgi, lambda k: xf[:, k, HALO:HALO + T], [bxf], False)
                    S.op("act", lambda e, ps=ps, m=m: e.activation(out=GU(m), in_=ps, func=AF.Gelu,
                                                                    bias=cfc(l, CF_BIN, SEG_U * 8 + m), scale=1.0),
                         reads=pbufs + [bCONST], writes=[bGU[m]])

            def blk_cc(p, slot):
                for cgi, (seg, m) in enumerate(in_groups("cc", p)):
                    ps, pbufs = fm_mm(slot, cgi, None, [bxf, bxfh], True)
                    if seg == SEG_GB:
                        q = rot()
                        S.op("act", lambda e, ps=ps, q=q, m=m: e.activation(out=SG[q][:], in_=ps, func=AF.Sigmoid,
                                                                             bias=cfc(l, CF_BIN, SEG_GB * 8 + m), scale=1.0),
                             reads=pbufs + [bCONST], writes=[bSG[q]])
                        cstate["sgq"] = q
                    else:
                        sgq = cstate["sgq"]
                        c4 = m % 4
                        S.op("dve", lambda e, ps=ps, c4=c4, m=m, sgq=sgq: e.scalar_tensor_tensor(
                            out=CCB[c4][:], in0=ps, scalar=cfc(l, CF_BIN, SEG_GA * 8 + m), in1=SG[sgq][:],
                            op0=ALU.add, op1=ALU.mult), reads=pbufs + [bSG[sgq], bCONST], writes=[bCCB[c4]])
                        if first:
                            S.op("pool", lambda e, c4=c4: e.memset(CCB[c4][:, 0:HALO], 0.0), writes=[bCCB[c4]])
                        if last:
                            S.op("pool", lambda e, c4=c4: e.memset(CCB[c4][:, HALO + T:WIN], 0.0), writes=[bCCB[c4]])

            def blk_dg(m, slot):
                c4 = m % 4
                i, h = next_bank()
                for jt in range(31):
                    S.op("pe", lambda e, i=i, h=h, jt=jt, c4=c4, slot=slot: e.matmul(
                        bank_ap(i, h), lhsT=WRF(slot)[:, jt * 128:(jt + 1) * 128], rhs=CCB[c4][:, jt:jt + T],
                        start=(jt == 0), stop=(jt == 30)), reads=[bWR[slot], bCCB[c4]], writes=[bPB[i][h]], signal=(jt == 30))
                S.op("act", lambda e, i=i, h=h, m=m: e.activation(out=CP[:, m, :], in_=bank_ap(i, h), func=AF.Identity,
                                                                   bias=cfc(l, CF_CCB, m), scale=1.0),
                     reads=[bPB[i][h], bCONST], writes=[bCP[m]])
                S.op("act", lambda e, m=m: e.activation(out=SQ[:, m, :], in_=CP[:, m, :], func=AF.Square),
                     reads=[bCP[m]], writes=[bSQ[m]])

            def blk_v(hh, slot):
                outs = tm_mm(slot, lambda k, g: xf[:, k, HALO + g * 128:HALO + (g + 1) * 128], lambda k: [bxf], 0, hh * 512)
                for g, (i, h) in enumerate(outs):
                    S.op("act", lambda e, i=i, h=h, g=g, hh=hh: e.activation(out=VNH(g, hh), in_=bank_ap(i, h), func=AF.Gelu),
                         reads=[bPB[i][h]], writes=[bVN(g)[hh]])

            def act_vln(g):
                if g == 0:
                    cstate["vgs"] = load_gb(LNB + 0)
                vfull = R[:, 24 + 2 * g:26 + 2 * g, :].rearrange("p a c -> p (a c)")
                ln_tm(vfull, vfull, bVN(g), bVN(g), cstate["vgs"], tmp=LNT, tbufs=bXB)

            def act_sgu():
                for hd in range(8):
                    i, h = misc_bank()
                    for g in range(4):
                        S.op("pe", lambda e, i=i, h=h, g=g, hd=hd: e.matmul(
                            PP[i][:, h * 512 + g * 128:h * 512 + (g + 1) * 128], lhsT=ONES[0:1, :],
                            rhs=BROW[0:1, l, D + hd * 128:D + (hd + 1) * 128], start=(g == 0), stop=False, skip_group_check=True),
                            reads=[bCONST], writes=[bPB[i][h]], signal=False)
                    for g in range(4):
                        S.op("pe", lambda e, i=i, h=h, g=g, hd=hd: e.matmul(
                            PP[i][:, h * 512 + g * 128:h * 512 + (g + 1) * 128],
                            lhsT=R[:, 24 + 2 * g + hd // 4, (hd % 4) * 128:(hd % 4 + 1) * 128],
                            rhs=WST[:, l, hd * 128:(hd + 1) * 128], start=False, stop=(g == 3), skip_group_check=True),
                            reads=[bCONST] + bVN(g), writes=[bPB[i][h]], signal=(g == 3))
                    S.op("dve", lambda e, i=i, h=h, hd=hd: e.tensor_tensor(out=YA(hd), in0=bank_ap(i, h), in1=GU(hd), op=ALU.mult),
                         reads=[bPB[i][h], bGU[hd]], writes=[bYA[hd]])

            def blk_cb(groups, slot):
                for cgi, (seg, m) in enumerate(groups):
                    if seg == SEG_CG:
                        ps, pbufs = fm_mm(slot, cgi, None, [bxf, bxfh], True)
                        q = rot()
                        S.op("act", lambda e, ps=ps, q=q, m=m: e.activation(out=CG[q][:], in_=ps, func=AF.Identity,
                                                                             bias=cfc(l, CF_BIN, SEG_CG * 8 + m), scale=1.0),
                             reads=pbufs + [bCONST], writes=[bCG[q]])
                        cstate["cgq"] = q
                    elif seg == SEG_H:
                        cgq = cstate["cgq"]
                        ps, pbufs = fm_mm(slot, cgi, None, [bxf, bxfh], True)
                        q = rot()
                        S.op("dve", lambda e, ps=ps, q=q, m=m, cgq=cgq: e.scalar_tensor_tensor(
                            out=CGH[q][:], in0=ps, scalar=cfc(l, CF_BIN, SEG_H * 8 + m), in1=CG[cgq][:],
                            op0=ALU.add, op1=ALU.mult), reads=pbufs + [bCG[cgq], bCONST], writes=[bCGH[q]])
                        if first:
                            S.op("pool", lambda e, q=q: e.memset(CGH[q][:, 0:HALO], 0.0), writes=[bCGH[q]])
                        if last:
                            S.op("pool", lambda e, q=q: e.memset(CGH[q][:, HALO + T:WIN], 0.0), writes=[bCGH[q]])
                        a = rot()
                        S.op("dve", lambda e, q=q, a=a, m=m: e.tensor_scalar(
                            out=ACC[a][:], in0=CGH[q][:, HALO - 1:HALO - 1 + T], scalar1=cfc(l, CF_W3, m), scalar2=None,
                            op0=ALU.mult), reads=[bCGH[q], bCONST], writes=[bACC[a]])
                        for jt in (1, 2):
                            S.op("dve", lambda e, q=q, a=a, m=m, jt=jt: e.scalar_tensor_tensor(
                                out=ACC[a][:], in0=CGH[q][:, HALO - 1 + jt:HALO - 1 + jt + T],
                                scalar=cfc(l, CF_W3, jt * 8 + m), in1=ACC[a][:], op0=ALU.mult, op1=ALU.add),
                                reads=[bCGH[q], bACC[a], bCONST], writes=[bACC[a]])
                        acc_of[m] = a
                        S.pump(3)
                    else:
                        ps, pbufs = fm_mm(slot, cgi, lambda k: xf[:, k, HALO:HALO + T], [bxf], False)
                        a = acc_of[m]
                        S.op("dve", lambda e, ps=ps, a=a, m=m: e.scalar_tensor_tensor(
                            out=YB(m), in0=ps, scalar=cfc(l, CF_BIN, SEG_BG * 8 + m), in1=ACC[a][:],
                            op0=ALU.add, op1=ALU.mult), reads=pbufs + [bACC[a], bCONST], writes=[bYB[m]])

            def act_cnorm():
                i1, h1 = misc_bank()
                i2, h2 = misc_bank()
                for k in range(8):
                    S.op("pe", lambda e, k=k: e.matmul(bank_ap(i1, h1), lhsT=ONES[:, :], rhs=CP[:, k, :], start=(k == 0), stop=(k == 7)),
                         reads=[bCONST, bCP[k]], writes=[bPB[i1][h1]], signal=(k == 7))
                for k in range(8):
                    S.op("pe", lambda e, k=k: e.matmul(bank_ap(i2, h2), lhsT=ONES[:, :], rhs=SQ[:, k, :], start=(k == 0), stop=(k == 7)),
                         reads=[bCONST, bSQ[k]], writes=[bPB[i2][h2]], signal=(k == 7))
                S.op("act", lambda e: e.activation(out=MEAN[:], in_=bank_ap(i1, h1), func=AF.Identity, scale=1.0 / D),
                     reads=[bPB[i1][h1]], writes=bSTATL)
                S.op("act", lambda e: e.activation(out=E2[:], in_=bank_ap(i2, h2), func=AF.Identity, scale=1.0 / D),
                     reads=[bPB[i2][h2]], writes=bSTATL)
                S.op("dve", lambda e: e.scalar_tensor_tensor(out=RSTD[:], in0=MEAN[:], scalar=-1.0, in1=MEAN[:],
                                                             op0=ALU.mult, op1=ALU.mult), reads=bSTATL, writes=bSTATL)
                S.op("dve", lambda e: e.scalar_tensor_tensor(out=E2[:], in0=E2[:], scalar=EPS, in1=RSTD[:],
                                                             op0=ALU.add, op1=ALU.add), reads=bSTATL, writes=bSTATL)
                S.op("act", lambda e: e.activation(out=RSTD[:], in_=E2[:], func=AF.Sqrt), reads=bSTATL, writes=bSTATL)
                S.op("dve", lambda e: e.reciprocal(out=RSTD[:], in_=RSTD[:]), reads=bSTATL, writes=bSTATL)
                for m in range(8):
                    q = m % 2
                    S.defer("pool", lambda e, m=m, q=q: e.tensor_tensor(out=ZT[q][:], in0=CP[:, m, :], in1=MEAN[:], op=ALU.subtract),
                            reads=[bCP[m]] + bSTATL, writes=[bZT[q]])
                    S.defer("pool", lambda e, m=m, q=q: e.tensor_tensor(out=ZT[q][:], in0=ZT[q][:], in1=RSTD[:], op=ALU.mult),
                            reads=[bZT[q]] + bSTATL, writes=[bZT[q]])
                    S.defer("act", lambda e, m=m, q=q: e.activation(out=CP[:, m, :], in_=ZT[q][:], func=AF.Silu,
                                                                     bias=cfc(l, CF_CNB, m), scale=cfc(l, CF_CNG, m)),
                            reads=[bZT[q], bCONST], writes=[bCP[m]])

            Ysrc = [(YA, bYA), (YB, bYB), (lambda m: CP[:, m, :], bCP)]

            def blk_gate(p, slot):
                n, mb = p
                r = (mb * 3 + n) % 2
                for cgi, (seg, m) in enumerate(in_groups("gate", p)):
                    ps, pbufs = fm_mm(slot, cgi, lambda k: xf[:, k, HALO:HALO + T], [bxf], False)
                    S.op("act", lambda e, ps=ps, m=m, cgi=cgi, r=r, seg=seg: e.activation(
                        out=SGT[r][:, cgi, :], in_=ps, func=AF.Sigmoid, bias=cfc(l, CF_BIN, seg * 8 + m), scale=1.0),
                        reads=pbufs + [bCONST], writes=[bSGT[r]])

            def blk_br(p, slot):
                n, mb = p
                r = (mb * 3 + n) % 2
                Yn, bYn = Ysrc[n]
                for cgi in range(4):
                    m = mb * 4 + cgi
                    ps, pbufs = fm_mm(slot, cgi, lambda k, Yn=Yn: Yn(k), list(bYn), False)
                    if n == 0:
                        S.op("dve", lambda e, ps=ps, cgi=cgi, r=r: e.tensor_tensor(out=MG[:, cgi, :], in0=ps, in1=SGT[r][:, cgi, :], op=ALU.mult),
                             reads=pbufs + [bSGT[r]], writes=[bMG[cgi]])
                    else:
                        q = rot()
                        S.op("dve", lambda e, ps=ps, q=q, cgi=cgi, r=r: e.tensor_tensor(out=TM1[q][:], in0=ps, in1=SGT[r][:, cgi, :], op=ALU.mult),
                             reads=pbufs + [bSGT[r]], writes=[bTM1[q]])
                        if n == 1:
                            S.op("pool", lambda e, q=q, cgi=cgi: e.tensor_tensor(out=MG[:, cgi, :], in0=MG[:, cgi, :], in1=TM1[q][:], op=ALU.add),
                                 reads=[bMG[cgi], bTM1[q]], writes=[bMG[cgi]])
                        else:
                            S.op("pool", lambda e, q=q, m=m, cgi=cgi: e.tensor_tensor(out=SQ[:, m, :], in0=MG[:, cgi, :], in1=TM1[q][:], op=ALU.add),
                                 reads=[bMG[cgi], bTM1[q]], writes=[bSQ[m]])

            if pendingT and pendingT[0][0] == gidx[(si, j)]:
                while pendingT:
                    pendingT.pop(0)[1]()
            handlers = {"u": blk_u, "cc": blk_cc, "dg": blk_dg, "v": blk_v, "cb": blk_cb, "gate": blk_gate, "br": blk_br}
            def run_tails():
                while pendingT:
                    pendingT.pop(0)[1]()

            def act_stageb():
                while pendingB:
                    stage_b(*pendingB.pop(0))
            acts = {"vln0": lambda: act_vln(0), "vln1": lambda: act_vln(1), "vln2": lambda: act_vln(2),
                    "vln3": lambda: act_vln(3), "sgu": act_sgu, "cnorm": act_cnorm, "stageB": act_stageb,
                    "flush": lambda: S.pump(), "haloT": halo_t, "tail": run_tails}
            si_ = 0
            while SCHED[si_][1] != "out":
                t_, k_, p_ = SCHED[si_]
                if t_ == "blk":
                    handlers[k_](p_, next_block())
                else:
                    acts[k_]()
                si_ += 1
            MGB, bMGB = SQ, bSQ
            gs = load_gb(LNB + 2)
            wslots = [next_block(), next_block(prefetch=False)]

            def wout_group(g):
                banks = []
                for hh in range(2):
                    i, h = next_bank()
                    banks.append((i, h))
                    S.op("pe", lambda e, i=i, h=h, hh=hh: e.matmul(bank_ap(i, h), lhsT=ONES[32:33, :],
                                                                    rhs=BROW[32:33, l, hh * 512:(hh + 1) * 512],
                                                                    start=True, stop=False, skip_group_check=True),
                         reads=[bCONST], writes=[bPB[i][h]], signal=False)
                    for k in range(8):
                        S.op("pe", lambda e, i=i, h=h, k=k, g=g, hh=hh: e.matmul(
                            bank_ap(i, h), lhsT=MGB[:, k, g * 128:(g + 1) * 128], rhs=WR[wslots[hh]][:, k, :],
                            start=False, stop=(k == 7), skip_group_check=True),
                            reads=[bWR[wslots[hh]], bMGB[k]], writes=[bPB[i][h]], signal=(k == 7))
                for hh, (i, h) in enumerate(banks):
                    S.op("dve", lambda e, i=i, h=h, g=g, hh=hh: e.scalar_tensor_tensor(
                        out=XT[s][:, g, hh * 512:(hh + 1) * 512], in0=XT[s][:, g, hh * 512:(hh + 1) * 512], scalar=ALPHA,
                        in1=bank_ap(i, h), op0=ALU.mult, op1=ALU.add), reads=[bPB[i][h], bXT[s][g]], writes=[bXT[s][g]])
            ln_groups(s, gs, pre=wout_group)
            act_stageb()
            X1F, bX1F = CP, bCP
            to_fm(s, s, X1F, list(bX1F), 0)
            for mb in range(8):
                slot = next_block()
                for cgi in range(4):
                    c = mb * 4 + cgi
                    ps, pbufs = fm_mm(slot, cgi, lambda k: X1F[:, k, :], list(bX1F), False)
                    q = rot()
                    S.op("act", lambda e, ps=ps, q=q, c=c: e.activation(out=RR[q][:], in_=ps, func=AF.Relu,
                                                                         bias=cfc(l, CF_BFF1, c), scale=1.0),
                         reads=pbufs + [bCONST], writes=[bRR[q]])
                    S.op("pool", lambda e, q=q, c=c: e.tensor_tensor(out=R[:, c, :], in0=RR[q][:], in1=RR[q][:], op=ALU.mult),
                         reads=[bRR[q]], writes=[bR[c]])
            gs = load_gb(LNB + 4)
            ln_ks = []
            for hh in range(2):
                banks = [next_bank() for _ in range(4)]
                for kb in range(4):
                    slot = next_block()
                    for g in range(4):
                        i, h = banks[g]
                        if kb == 0:
                            S.op("pe", lambda e, i=i, h=h, hh=hh: e.matmul(bank_ap(i, h), lhsT=ONES[64:65, :],
                                                                            rhs=BROW[64:65, l, hh * 512:(hh + 1) * 512],
                                                                            start=True, stop=False, skip_group_check=True),
                                 reads=[bCONST], writes=[bPB[i][h]], signal=False)
                        for k in range(8):
                            kk = kb * 8 + k
                            lastmm = (kb == 3 and k == 7)
                            S.op("pe", lambda e, i=i, h=h, g=g, k=k, kk=kk, lastmm=lastmm, slot=slot: e.matmul(
                                bank_ap(i, h), lhsT=R[:, kk, g * 128:(g + 1) * 128], rhs=WR[slot][:, k, :],
                                start=False, stop=lastmm, skip_group_check=True),
                                reads=[bWR[slot], bR[kk]], writes=[bPB[i][h]], signal=(k == 7))
                        if kb == 3:
                            S.op("dve", lambda e, i=i, h=h, g=g, hh=hh: e.scalar_tensor_tensor(
                                out=XT[s][:, g, hh * 512:(hh + 1) * 512], in0=XT[s][:, g, hh * 512:(hh + 1) * 512], scalar=ALPHA,
                                in1=bank_ap(i, h), op0=ALU.mult, op1=ALU.add), reads=[bPB[i][h], bXT[s][g]], writes=[bXT[s][g]])
                            if hh == 1:
                                ln_ks.append(ln_stats(XT[s][:, g, :], [bXT[s][g]]))
                                if g >= 1:
                                    ln_apply(ln_ks[g - 1], XT[s][:, g - 1, :], XT[s][:, g - 1, :], [bXT[s][g - 1]], [bXT[s][g - 1]], gs)
            ln_apply(ln_ks[3], XT[s][:, 3, :], XT[s][:, 3, :], [bXT[s][3]], [bXT[s][3]], gs)
            if l == 0:
                def tail():
                    to_fm(s, s, XF[sf], [bXFc[sf]], HALO)
                    if not first:
                        sp_ = 1 - sf
                        S.op("pool", lambda e: e.tensor_copy(out=XF[sf][:, :, 0:HALO], in_=XF[sp_][:, :, T:T + HALO]),
                             reads=[bXFc[sp_]], writes=[bXFh[sf]])
                        S.op("pool", lambda e: e.tensor_copy(out=XF[sp_][:, :, HALO + T:WIN], in_=XF[sf][:, :, HALO:2 * HALO]),
                             reads=[bXFc[sf]], writes=[bXFh[sp_]])
                pendingT.append((gidx[(si, j)], tail))
            else:
                r0 = row0 + j * T
                for g in range(4):
                    S.dma("pool", lambda e, g=g: e.dma_start(out=yout[r0 + g * 128:r0 + (g + 1) * 128, :], in_=XT[s][:, g, :]),
                          f"o{s}{g}", reads=[bXT[s][g]])

        stage_a(*tiles[0])
        stage_b(*tiles[0])
        pump_cast(3)
        for n_, (kind, si, j) in enumerate(tile_layers):
            if kind == "L1":
                layer(0, si, j)
                nxt = gidx[(si, j)] + 1
                if nxt < len(tiles):
                    stage_a(*tiles[nxt])
                    pendingB.append(tiles[nxt])
                    if not (n_ + 1 < len(tile_layers) and tile_layers[n_ + 1][0] == "L2"):
                        stage_b(*pendingB.pop(0))
            else:
                layer(1, si, j)
        S.wait_all("sp", [f"o{s_}{g}" for s_ in range(3) for g in range(4)] + ["pe", "act", "dve", "pool"])

        engs = {"sp": "sync", "act": "scalar", "dve": "vector", "pool": "gpsimd", "pe": "tensor"}
        with nc.Block() as block:
            def make(key):
                def body(e):
                    for it in S.streams[key]:
                        if it[0] == "wait":
                            e.wait_ge(sems[it[1]], it[2])
                        else:
                            ins = it[1](e)
                            if it[3]:
                                ins.then_inc(sems[it[2]], it[3])
                return body
            for key, attr in engs.items():
                getattr(block, attr)(make(key))
    return nc


def pack_aux(ln_in_g, ln_in_b, b_in, sgu_ln_g, sgu_ln_b, sgu_w, sgu_b, sconv_w, cconv_w, cconv_b,
             cnorm_g, cnorm_b, b_out, ln1_g, ln1_b, b_ff1, b_ff2, ln2_g, ln2_b):
    f = lambda a: np.asarray(a, dtype=np.float32)
    cfm = np.empty((128, 2 * CF_N), np.float32)
    for l in range(2):
        o = l * CF_N
        cfm[:, o + CF_BIN:o + CF_BIN + 80] = f(b_in)[l].reshape(80, 128).T
        cfm[:, o + CF_BFF1:o + CF_BFF1 + 32] = f(b_ff1)[l].reshape(32, 128).T
        cfm[:, o + CF_CCB:o + CF_CCB + 8] = f(cconv_b)[l].reshape(8, 128).T
        cfm[:, o + CF_CNG:o + CF_CNG + 8] = f(cnorm_g)[l].reshape(8, 128).T
        cfm[:, o + CF_CNB:o + CF_CNB + 8] = f(cnorm_b)[l].reshape(8, 128).T
        cfm[:, o + CF_W3:o + CF_W3 + 24] = f(sconv_w)[l].reshape(3, 8, 128).transpose(2, 0, 1).reshape(128, 24)
    brow32 = np.empty((2, 4, D), np.float32)
    for l in range(2):
        brow32[l, 0] = f(b_in)[l, SEG_V * 1024:(SEG_V + 1) * 1024]
        brow32[l, 1] = f(b_out)[l]
        brow32[l, 2] = f(b_ff2)[l]
        brow32[l, 3] = f(sgu_b)[l].reshape(-1)
    lnp = np.empty((14, 128, D), np.float32)
    vecs = [f(ln_in_g), f(ln_in_b)]
    for l in range(2):
        vecs += [f(sgu_ln_g)[l], f(sgu_ln_b)[l], f(ln1_g)[l], f(ln1_b)[l], f(ln2_g)[l], f(ln2_b)[l]]
    for i, v in enumerate(vecs):
        lnp[i] = np.broadcast_to(v[None, :], (128, D))
    wst32 = np.ascontiguousarray(f(sgu_w).transpose(0, 3, 1, 2)).reshape(2, 128, 8 * 128)
    ident = np.eye(128, dtype=np.float32)

    return {"cfm": cfm, "brow32": brow32, "lnp": lnp, "wst32": wst32, "ident": ident}


_NC_CACHE = {}


def kernel(x_prompt, x_sample, ln_in_g, ln_in_b, w_in, b_in, sgu_ln_g, sgu_ln_b, sgu_w, sgu_b,
           sconv_w, cconv_w, cconv_b, cnorm_g, cnorm_b, w_branch, w_out, b_out,
           ln1_g, ln1_b, w_ff1, b_ff1, w_ff2, b_ff2, ln2_g, ln2_b):
    f = lambda a: np.asarray(a, dtype=np.float32)
    x_prompt, x_sample = f(x_prompt), f(x_sample)
    w_in, w_branch, w_out, w_ff1, w_ff2 = f(w_in), f(w_branch), f(w_out), f(w_ff1), f(w_ff2)
    wpk32 = pack_weights(w_in, w_branch, w_out, w_ff1, w_ff2, f(cconv_w))
    aux = pack_aux(ln_in_g, ln_in_b, b_in, sgu_ln_g, sgu_ln_b, sgu_w, sgu_b, sconv_w, cconv_w, cconv_b,
                   cnorm_g, cnorm_b, b_out, ln1_g, ln1_b, b_ff1, b_ff2, ln2_g, ln2_b)
    in_maps = []
    for c in range(NCORES):
        xin = np.concatenate([x_prompt[2 * c:2 * c + 2].reshape(-1, D), x_sample[4 * c:4 * c + 4].reshape(-1, D)], axis=0)
        in_maps.append(dict(aux, xin=np.ascontiguousarray(xin), wpk32=wpk32))
    if "nc" not in _NC_CACHE:
        _NC_CACHE["nc"] = build_program()
    res = run_bass_kernel_spmd(_NC_CACHE["nc"], in_maps, core_ids=list(range(NCORES)))
    y_prompt = np.empty((16, 4096, D), np.float32)
    y_sample = np.empty((32, 2048, D), np.float32)
    for c in range(NCORES):
        y = np.asarray(res.results[c]["yout"])
        y_prompt[2 * c:2 * c + 2] = y[:8192].reshape(2, 4096, D)
        y_sample[4 * c:4 * c + 4] = y[8192:].reshape(4, 2048, D)
    return (y_prompt, y_sample)
```

```python
import contextlib
import numpy as np
import concourse.bass as bass
import concourse.mybir as mybir
from concourse.bass_utils import run_bass_kernel_spmd

F32 = mybir.dt.float32
BF16 = mybir.dt.bfloat16
AF = mybir.ActivationFunctionType
ALU = mybir.AluOpType

D = 1024
NCORES = 8
T = 512
HALO = 15
WIN = T + 2 * HALO
NBLK = 52
NSLOT = 3
ALPHA = float(4 ** 0.25)
EPS = 1e-5
TOK_PER_CORE = 16384
SEQS = [(0, 8), (4096, 8), (8192, 4), (10240, 4), (12288, 4), (14336, 4)]

SEG_U, SEG_V, SEG_BG, SEG_CG, SEG_H, SEG_GA, SEG_GB, SEG_G0 = 0, 1, 2, 3, 4, 5, 6, 7

CF_BIN = 0
CF_BFF1 = 80
CF_CCB = 112
CF_CNG = 120
CF_CNB = 128
CF_W3 = 136
CF_N = 160


def layer_schedule():
    sch = []
    cb = []
    for m in range(8):
        cb += [(SEG_CG, m), (SEG_H, m), (SEG_BG, m)]
    B = lambda kind, p: sch.append(("blk", kind, p))
    A = lambda name: sch.append(("act", name, None))
    B("u", 0); B("u", 1); B("v", 0); B("v", 1); A("haloT"); A("tail")
    B("cc", 0); B("cc", 1); A("vln0"); B("dg", 0); B("dg", 1)
    B("cc", 2); A("vln1"); B("dg", 2); B("dg", 3)
    B("cc", 3); A("vln2"); B("dg", 4); B("dg", 5)
    B("cb", cb[0:4]); A("vln3"); B("dg", 6); B("dg", 7); A("cnorm")
    for i in range(1, 6):
        B("cb", cb[4 * i:4 * i + 4])
    A("flush"); A("sgu")
    for mb in range(2):
        for n in range(3):
            B("gate", (n, mb)); B("br", (n, mb))
    B("out", 0); B("out", 1); A("ln1")
    for mb in range(8):
        B("ff1", mb)
    for h in range(2):
        for kb in range(4):
            B("ff2", (h, kb))
    A("ln2")
    return sch


SCHED = layer_schedule()
PLAN = [(k, p) for (t, k, p) in SCHED if t == "blk"]
assert len(PLAN) == NBLK


def in_groups(kind, p):
    if kind == "u":
        return [(SEG_U, 4 * p + j) for j in range(4)]
    if kind == "v":
        return [(SEG_V, 4 * p + j) for j in range(4)]
    if kind == "cc":
        return [(SEG_GB, 2 * p), (SEG_GA, 2 * p), (SEG_GB, 2 * p + 1), (SEG_GA, 2 * p + 1)]
    if kind == "cb":
        return p
    if kind == "gate":
        n, mb = p
        return [(SEG_G0 + n, 4 * mb + j) for j in range(4)]
    return None


def _pack(W):
    return np.ascontiguousarray(W.reshape(8, 128, 512).transpose(1, 0, 2)).reshape(128, 4096)


def pack_weights(w_in, w_branch, w_out, w_ff1, w_ff2, cconv_w):
    out = np.empty((2, NBLK, 128, 4096), np.float32)
    ar = np.arange(128)
    for l in range(2):
        for b, (kind, p) in enumerate(PLAN):
            grp = in_groups(kind, p)
            if grp is not None:
                cols = np.concatenate([np.arange(sg * 1024 + m * 128, sg * 1024 + m * 128 + 128) for (sg, m) in grp])
                W = w_in[l][:, cols]
            elif kind == "dg":
                blk = np.zeros((128, 32, 128), np.float32)
                for j in range(31):
                    blk[ar, j, ar] = cconv_w[l, j, p * 128:(p + 1) * 128]
                out[l, b] = blk.reshape(128, 4096)
                continue
            elif kind == "br":
                n, mb = p
                W = w_branch[l, n][:, mb * 512:(mb + 1) * 512]
            elif kind == "out":
                W = w_out[l][:, p * 512:(p + 1) * 512]
            elif kind == "ff1":
                W = w_ff1[l][:, p * 512:(p + 1) * 512]
            else:
                h, kb = p
                W = w_ff2[l][kb * 1024:(kb + 1) * 1024, h * 512:(h + 1) * 512]
            out[l, b] = _pack(W)
    return out


class Buf:
    __slots__ = ("w", "r")

    def __init__(self):
        self.w = None
        self.r = {}


class Sched:
    ENG = ("sp", "act", "dve", "pool", "pe")

    def __init__(self):
        self.streams = {k: [] for k in self.ENG}
        self.cnt = {}
        self.seen = {k: {} for k in self.ENG}
        self.lazy = []

    def _waits(self, eng, reads, writes):
        deps = {}

        def add(tok, raw):
            k, v = tok
            if k == eng and eng == "pe":
                return
            if k not in self.ENG:
                v = self.cnt[k]
            if deps.get(k, 0) < v:
                deps[k] = v
        for b in reads:
            if b.w is not None:
                add(b.w, True)
        for b in writes:
            if b.w is not None:
                add(b.w, False)
            for k, v in b.r.items():
                add((k, v), False)
        seen = self.seen[eng]
        for k, v in deps.items():
            if seen.get(k, 0) < v:
                self.streams[eng].append(("wait", k, v))
                seen[k] = v

    def _mark(self, tok, reads, writes):
        k, v = tok
        for b in reads:
            if b.r.get(k, 0) < v:
                b.r[k] = v
        for b in writes:
            b.w = tok
            b.r = {}

    def op(self, eng, fn, reads=(), writes=(), signal=True):
        self._waits(eng, reads, writes)
        c = self.cnt.get(eng, 0)
        if signal:
            c += 1
            self.cnt[eng] = c
            self.streams[eng].append(("op", fn, eng, 1))
            tok = (eng, c)
        else:
            self.streams[eng].append(("op", fn, None, 0))
            tok = (eng, c + 1)
        self._mark(tok, reads, writes)

    def dma(self, eng, fn, dsem, reads=(), writes=()):
        self._waits(eng, reads, writes)
        c = self.cnt.get(dsem, 0) + 16
        self.cnt[dsem] = c
        self.streams[eng].append(("op", fn, dsem, 16))
        self._mark((dsem, c), reads, writes)

    def defer(self, eng, fn, reads=(), writes=()):
        self.lazy.append((eng, fn, list(reads), list(writes)))

    def pump(self, n=None):
        while self.lazy and (n is None or n > 0):
            eng, fn, r, w = self.lazy.pop(0)
            self.op(eng, fn, reads=r, writes=w)
            if n is not None:
                n -= 1

    def wait_all(self, eng, keys):
        for k in keys:
            v = self.cnt.get(k, 0)
            if v and self.seen[eng].get(k, 0) < v:
                self.streams[eng].append(("wait", k, v))
                self.seen[eng][k] = v


def build_program(SEQS=SEQS):
    TOK_PER_CORE = sum(n for _, n in SEQS) * T
    nc = bass.Bass("TRN2", target_bir_lowering=False)
    xin = nc.dram_tensor("xin", [TOK_PER_CORE, D], F32, kind="ExternalInput").ap()
    wpk32 = nc.dram_tensor("wpk32", [2, NBLK, 128, 4096], F32, kind="ExternalInput").ap()
    cfm = nc.dram_tensor("cfm", [128, 2 * CF_N], F32, kind="ExternalInput").ap()
    brow32 = nc.dram_tensor("brow32", [2, 4, D], F32, kind="ExternalInput").ap()
    lnp = nc.dram_tensor("lnp", [14, 128, D], F32, kind="ExternalInput").ap()
    wst32 = nc.dram_tensor("wst32", [2, 128, 8 * 128], F32, kind="ExternalInput").ap()
    ident = nc.dram_tensor("ident", [128, 128], F32, kind="ExternalInput").ap()
    yout = nc.dram_tensor("yout", [TOK_PER_CORE, D], F32, kind="ExternalOutput").ap()
    wpk = nc.dram_tensor("wpk", [2, NBLK, 128, 4096], BF16, kind="Internal").ap()

    S = Sched()
    es = contextlib.ExitStack()

    def sb(name, shape, dt):
        return es.enter_context(nc.sbuf_tensor(name, shape, dt))

    with es:
        XT = [sb(f"xt{i}", [128, 4, D], F32) for i in range(3)]
        XF = [sb(f"xf{i}", [128, 8, WIN], BF16) for i in range(2)]
        WR = [sb(f"wr{i}", [128, 8, 512], BF16) for i in range(NSLOT)]
        R = sb("rbig", [128, 32, 512], BF16)
        CP = sb("cp", [128, 8, 512], BF16)
        SQ = sb("sq", [128, 8, 512], BF16)
        SGT = [sb(f"sgt{i}", [128, 4, 512], BF16) for i in range(2)]
        MG = sb("mg", [128, 4, 512], F32)
        CG = [sb(f"cg{i}", [128, WIN], F32) for i in range(2)]
        CGH = [sb(f"cgh{i}", [128, WIN], F32) for i in range(2)]
        SG = CG
        CCB = [sb(f"ccb{i}", [128, WIN], BF16) for i in range(4)]
        ACC = [sb(f"acc{i}", [128, 512], F32) for i in range(2)]
        TM1 = [sb(f"tm1{i}", [128, 512], F32) for i in range(2)]
        RR = [sb(f"rr{i}", [128, 512], F32) for i in range(2)]
        ZT = RR
        MEAN = TM1[0]
        E2 = sb("e2", [128, 512], F32)
        RSTD = TM1[1]
        NEGH = sb("negh", [128, 1], F32)
        GBT = [[sb(f"gb{i}{j}", [128, D], F32) for j in range(2)] for i in range(1)]
        LNT = sb("lnt", [128, D], F32)
        HB = sb("hb", [128, 8, 2 * HALO], BF16)
        XH = MG[:, 0:2, :].rearrange("p a c -> p (a c)")
        CF = sb("cf", [128, 2 * CF_N], F32)
        IDT = sb("idt", [128, 128], F32)
        ONES = sb("ones", [128, 128], BF16)
        BROW = sb("brow", [128, 2, 2 * D], BF16)
        WST = sb("wst", [128, 2, 8 * 128], BF16)
        NLN = 8
        ST = [sb(f"st{i}", [128, 12], F32) for i in range(NLN)]
        MV = [sb(f"mv{i}", [128, 2], F32) for i in range(NLN)]
        VE = [sb(f"ve{i}", [128, 1], F32) for i in range(NLN)]
        RS = [sb(f"rs{i}", [128, 1], F32) for i in range(NLN)]
        EPST = sb("epst", [128, 1], F32)
        PP = [es.enter_context(nc.psum_tensor(f"pp{i}", [128, 1024], F32)) for i in range(4)]

        keys = ["pe", "act", "dve", "pool", "misc", "xh", "gb0", "gb1"] + [f"w{i}" for i in range(NSLOT)] + \
               [f"c{l}_{q}" for l in range(2) for q in range(NBLK // 4)] + \
               [f"{a}{s_}{g}" for a in "xo" for s_ in range(3) for g in range(4)]
        sems = {k: es.enter_context(nc.semaphore(k)) for k in keys}

        bXT = [[Buf() for _ in range(4)] for _ in range(3)]
        bXFc = [Buf() for _ in range(2)]
        bXFh = [Buf() for _ in range(2)]
        bWR = [Buf() for _ in range(NSLOT)]
        bR = [Buf() for _ in range(32)]
        bCP = [Buf() for _ in range(8)]
        bSQ = [Buf() for _ in range(8)]
        bSGT = [Buf() for _ in range(2)]
        bMG = [Buf() for _ in range(4)]
        bXHL = [bMG[0], bMG[1]]
        bCG = [Buf() for _ in range(2)]
        bCGH = [Buf() for _ in range(2)]
        bSG = bCG
        bCCB = [Buf() for _ in range(4)]
        bACC = [Buf() for _ in range(2)]
        bTM1 = [Buf() for _ in range(2)]
        bRR = [Buf() for _ in range(2)]
        bZT = bRR
        bE2 = Buf()
        bSTATL = [bTM1[0], bTM1[1], bE2]
        bGB = [Buf() for _ in range(2)]
        bLNT = Buf()
        bHB = Buf()
        bXH = None
        bCONST = Buf()
        bLN = [Buf() for _ in range(8)]
        bPB = [[Buf(), Buf()] for _ in range(4)]
        bWPK = [[Buf() for _ in range(NBLK // 4)] for _ in range(2)]

        st = {"pp": 0, "misc": 0, "ln": 0, "gb": 0, "tmp": 0, "gbcur": -1}

        def next_bank():
            p = st["pp"]
            st["pp"] = (p + 1) % 8
            return p // 2, p % 2

        def next_pair():
            p = st["pp"]
            if p % 2:
                p = (p + 1) % 8
            st["pp"] = (p + 2) % 8
            return p // 2

        def misc_bank():
            h = st["misc"]
            st["misc"] = 1 - h
            return 3, h

        def bank_ap(i, h, n=512):
            return PP[i][:, h * 512:h * 512 + n]

        castq = [(l, q) for l in range(2) for q in range(NBLK // 4)]

        def pump_cast(n=None, upto=None):
            while castq and (n is None or n > 0):
                if upto is not None and castq[0] > upto:
                    break
                l, q = castq.pop(0)
                for b in range(4 * q, 4 * q + 4):
                    S.dma("pool", lambda e, l=l, b=b: e.dma_start(
                        out=wpk[l, b].rearrange("(a r) c -> a (r c)", a=16),
                        in_=wpk32[l, b].rearrange("(a r) c -> a (r c)", a=16)), f"c{l}_{q}", writes=[bWPK[l][q]])
                if n is not None:
                    n -= 1
        S.dma("sp", lambda e: e.dma_start(out=CF[:], in_=cfm[:, :]), "misc", writes=[bCONST])
        S.dma("sp", lambda e: e.dma_start(out=IDT[:], in_=ident[:, :]), "misc", writes=[bCONST])
        S.wait_all("sp", ["misc"])
        for l in range(2):
            for r in range(3):
                S.dma("sp", lambda e, l=l, r=r: e.dma_start(out=XH[32 * r:32 * r + 1, :], in_=brow32[l, r:r + 1, :]),
                      "misc", writes=bXHL)
            S.wait_all("sp", ["misc"])
            for r in range(3):
                S.op("dve", lambda e, l=l, r=r: e.tensor_copy(out=BROW[32 * r:32 * r + 1, l, 0:D],
                                                              in_=XH[32 * r:32 * r + 1, :]),
                     reads=bXHL, writes=[bCONST])
            S.dma("sp", lambda e, l=l: e.dma_start(out=XH[0:1, :], in_=brow32[l, 3:4, :]), "misc", writes=bXHL)
            S.wait_all("sp", ["misc"])
            S.op("dve", lambda e, l=l: e.tensor_copy(out=BROW[0:1, l, D:2 * D], in_=XH[0:1, :]),
                 reads=bXHL, writes=[bCONST])
            S.dma("sp", lambda e, l=l: e.dma_start(out=LNT[:], in_=wst32[l]), "misc", writes=[bLNT])
            S.wait_all("sp", ["misc"])
            S.op("dve", lambda e, l=l: e.tensor_copy(out=WST[:, l, :], in_=LNT[:]), reads=[bLNT], writes=[bCONST])
        S.op("dve", lambda e: e.memset(ONES[:], 1.0), writes=[bCONST])
        S.op("dve", lambda e: e.memset(NEGH[:], -0.5), writes=[bCONST])
        S.op("dve", lambda e: e.memset(EPST[:], EPS), writes=[bCONST])
        S.op("dve", lambda e: e.memset(XH[:], 0.0), reads=[], writes=bXHL)

        tile_layers = []
        for si, (row0, ntile) in enumerate(SEQS):
            for j in range(ntile):
                tile_layers.append(("L1", si, j))
                if j >= 1:
                    tile_layers.append(("L2", si, j - 1))
            tile_layers.append(("L2", si, ntile - 1))
        gblocks = []
        for (kind, si, j) in tile_layers:
            l = 0 if kind == "L1" else 1
            for b in range(NBLK):
                gblocks.append((l, b))
        wst_ = {"issued": 0, "used": 0}

        def issue_loads(upto):
            while wst_["issued"] < min(upto, len(gblocks)):
                i = wst_["issued"]
                l, b = gblocks[i]
                slot = i % NSLOT
                pump_cast(upto=(l, b // 4))
                S.dma("sp", lambda e, l=l, b=b, slot=slot: e.dma_start(
                    out=WR[slot][:].rearrange("p k c -> p (k c)"), in_=wpk[l, b]),
                    f"w{slot}", reads=[bWPK[l][b // 4]], writes=[bWR[slot]])
                wst_["issued"] += 1

        def next_block(prefetch=True):
            i = wst_["used"]
            issue_loads(i + NSLOT if prefetch else i + 1)
            if castq and i % (2 if castq[0][0] == 0 else 4) == 0:
                pump_cast(1)
            wst_["used"] += 1
            return i % NSLOT

        def load_gb(idx):
            s = 0
            st["gbcur"] = idx
            S.dma("pool", lambda e: e.dma_start(out=GBT[s][0][:], in_=lnp[idx]), f"gb{s}", writes=[bGB[s]])
            S.dma("pool", lambda e: e.dma_start(out=GBT[s][1][:], in_=lnp[idx + 1]), f"gb{s}", writes=[bGB[s]])
            return s

        def ln_stats(src, rbufs, npart=128):
            k = st["ln"]
            st["ln"] = (k + 1) % NLN
            P = slice(0, npart)
            S.op("dve", lambda e: e.bn_stats(out=ST[k][P, 0:6], in_=src[:, 0:512]), reads=rbufs, writes=[bLN[k]])
            S.op("dve", lambda e: e.bn_stats(out=ST[k][P, 6:12], in_=src[:, 512:1024]), reads=rbufs, writes=[bLN[k]])
            S.op("dve", lambda e: e.bn_aggr(out=MV[k][P, :], in_=ST[k][P, :]), reads=[bLN[k]], writes=[bLN[k]])
            S.op("pool", lambda e: e.tensor_tensor(out=VE[k][P, :], in0=MV[k][P, 1:2], in1=EPST[P, :], op=ALU.add),
                 reads=[bLN[k], bCONST], writes=[bLN[k]])
            S.op("pool", lambda e: e.tensor_tensor(out=RS[k][P, :], in0=VE[k][P, :], in1=NEGH[P, 0:1], op=ALU.pow),
                 reads=[bLN[k], bCONST], writes=[bLN[k]])
            return k

        def ln_apply(k, src, dst, rbufs, wbufs, gs, npart=128):
            P = slice(0, npart)
            S.op("dve", lambda e: e.scalar_tensor_tensor(out=LNT[P, :], in0=src, scalar=MV[k][P, 0:1],
                                                         in1=GBT[gs][0][P, :], op0=ALU.subtract, op1=ALU.mult),
                 reads=list(rbufs) + [bLN[k], bGB[gs]], writes=[bLNT])
            S.op("dve", lambda e: e.scalar_tensor_tensor(out=dst, in0=LNT[P, :], scalar=RS[k][P, 0:1],
                                                         in1=GBT[gs][1][P, :], op0=ALU.mult, op1=ALU.add),
                 reads=[bLNT, bLN[k], bGB[gs]], writes=wbufs)

        def ln_tm(src, dst, rbufs, wbufs, gs, npart=128):
            k = ln_stats(src, rbufs, npart)
            ln_apply(k, src, dst, rbufs, wbufs, gs, npart)

        def ln_groups(s, gs, pre=None):
            ks = []
            for g in range(4):
                if pre is not None:
                    pre(g)
                ks.append(ln_stats(XT[s][:, g, :], [bXT[s][g]]))
                if g >= 1:
                    ln_apply(ks[g - 1], XT[s][:, g - 1, :], XT[s][:, g - 1, :], [bXT[s][g - 1]], [bXT[s][g - 1]], gs)
            ln_apply(ks[3], XT[s][:, 3, :], XT[s][:, 3, :], [bXT[s][3]], [bXT[s][3]], gs)

        def to_fm(s, xs, dstF, dst_bufs, col0):
            for g in range(4):
                for mq in range(2):
                    i, h = misc_bank()
                    for mm in range(4):
                        m = mq * 4 + mm
                        S.op("pe", lambda e, i=i, h=h, g=g, m=m, mm=mm: e.transpose(
                            out=PP[i][:, h * 512 + mm * 128:h * 512 + (mm + 1) * 128],
                            in_=XT[xs][:, g, m * 128:(m + 1) * 128], identity=IDT[:]),
                            reads=[bXT[xs][g], bCONST], writes=[bPB[i][h]], signal=(mm == 3))
                    S.op("act", lambda e, i=i, h=h, g=g, mq=mq: e.activation(
                        out=dstF[:, mq * 4:mq * 4 + 4, col0 + g * 128:col0 + (g + 1) * 128],
                        in_=bank_ap(i, h).rearrange("p (m c) -> p m c", c=128), func=AF.Copy),
                        reads=[bPB[i][h]], writes=dst_bufs)

        tcount = {"n": 0}
        slot_of = {}

        def cfc(l, base, idx):
            c = l * CF_N + base + idx
            return CF[:, c:c + 1]

        tiles = [(si, j) for si, (row0, ntile) in enumerate(SEQS) for j in range(ntile)]
        gidx = {t: i for i, t in enumerate(tiles)}

        def stage_a(si, j):
            row0, ntile = SEQS[si]
            r0 = row0 + j * T
            s = gidx[(si, j)] % 3
            first, last = (j == 0), (j == ntile - 1)
            for g in range(4):
                S.dma("sp", lambda e, g=g: e.dma_start(out=XT[s][:, g, :], in_=xin[r0 + g * 128:r0 + (g + 1) * 128, :]),
                      f"x{s}{g}", writes=[bXT[s][g]])
            if not first:
                S.dma("sp", lambda e: e.dma_start(out=XH[0:HALO, :], in_=xin[r0 - HALO:r0, :]), "xh", writes=bXHL)
            if not last:
                S.dma("sp", lambda e: e.dma_start(out=XH[HALO:2 * HALO, :], in_=xin[r0 + T:r0 + T + HALO, :]), "xh",
                      writes=bXHL)
            gs = load_gb(0)
            NP = 2 * HALO
            ln_tm(XH[0:NP, :], XH[0:NP, :], bXHL, bXHL, gs, npart=NP)
            ln_groups(s, gs)
            pendingH.append(1)

        def halo_t():
            if not pendingH:
                return
            pendingH.pop()
            NP = 2 * HALO
            i, h = misc_bank()
            for m in range(8):
                S.op("pe", lambda e, i=i, h=h, m=m: e.transpose(out=PP[i][:, h * 512 + m * NP:h * 512 + (m + 1) * NP],
                                                                  in_=XH[0:NP, m * 128:(m + 1) * 128],
                                                                  identity=IDT[0:NP, 0:NP]),
                     reads=bXHL + [bCONST], writes=[bPB[i][h]], signal=(m == 7))
            pv = PP[i][:, h * 512:h * 512 + 8 * NP].rearrange("p (m c) -> p m c", c=NP)
            S.op("act", lambda e: e.activation(out=HB[:], in_=pv, func=AF.Copy), reads=[bPB[i][h]], writes=[bHB])

        def stage_b(si, j):
            s = gidx[(si, j)] % 3
            sf = gidx[(si, j)] % 2
            halo_t()
            to_fm(s, s, XF[sf], [bXFc[sf]], HALO)
            S.op("pool", lambda e: e.tensor_copy(out=XF[sf][:, :, 0:HALO], in_=HB[:, :, 0:HALO]),
                 reads=[bHB], writes=[bXFh[sf]])
            S.op("pool", lambda e: e.tensor_copy(out=XF[sf][:, :, HALO + T:WIN], in_=HB[:, :, HALO:2 * HALO]),
                 reads=[bHB], writes=[bXFh[sf]])

        pendingB = []
        pendingT = []
        pendingH = []

        def layer(l, si, j):
            row0, ntile = SEQS[si]
            s = gidx[(si, j)] % 3
            sf = gidx[(si, j)] % 2
            first, last = (j == 0), (j == ntile - 1)
            xf, bxf, bxfh = XF[sf], bXFc[sf], bXFh[sf]
            GU = lambda m: R[:, m, :]
            YA = lambda m: R[:, 8 + m, :]
            YB = lambda m: R[:, 16 + m, :]
            VNH = lambda g, hh: R[:, 24 + 2 * g + hh, :]
            bGU, bYA, bYB = bR[0:8], bR[8:16], bR[16:24]
            bVN = lambda g: bR[24 + 2 * g:26 + 2 * g]
            LNB = 2 + 6 * l

            def fm_mm(slot, cgi, rhs_of_k, rbufs, halo):
                if halo:
                    i = next_pair()
                    for k in range(8):
                        S.op("pe", lambda e, k=k, i=i: e.matmul(PP[i][:, 0:512], lhsT=WR[slot][:, k, cgi * 128:(cgi + 1) * 128],
                                                                  rhs=xf[:, k, 0:512], start=(k == 0), stop=(k == 7)),
                             reads=[bWR[slot]] + rbufs, writes=[bPB[i][0]], signal=False)
                    for k in range(8):
                        S.op("pe", lambda e, k=k, i=i: e.matmul(PP[i][:, 512:WIN], lhsT=WR[slot][:, k, cgi * 128:(cgi + 1) * 128],
                                                                  rhs=xf[:, k, 512:WIN], start=(k == 0), stop=(k == 7)),
                             reads=[bWR[slot]] + rbufs, writes=[bPB[i][1]], signal=(k == 7))
                    return PP[i][:, 0:WIN], [bPB[i][0], bPB[i][1]]
                i, h = next_bank()
                for k in range(8):
                    S.op("pe", lambda e, k=k, i=i, h=h: e.matmul(bank_ap(i, h), lhsT=WR[slot][:, k, cgi * 128:(cgi + 1) * 128],
                                                                  rhs=rhs_of_k(k), start=(k == 0), stop=(k == 7)),
                         reads=[bWR[slot]] + rbufs, writes=[bPB[i][h]], signal=(k == 7))
                return bank_ap(i, h), [bPB[i][h]]

            def tm_mm(slot, lhs_of_k, lbufs, brow_r, col0, nk=8):
                outs = []
                for g in range(4):
                    i, h = next_bank()
                    S.op("pe", lambda e, i=i, h=h: e.matmul(bank_ap(i, h), lhsT=ONES[32 * brow_r:32 * brow_r + 1, :],
                                                             rhs=BROW[32 * brow_r:32 * brow_r + 1, l, col0:col0 + 512],
                                                             start=True, stop=False, skip_group_check=True),
                         reads=[bCONST], writes=[bPB[i][h]], signal=False)
                    for k in range(nk):
                        S.op("pe", lambda e, i=i, h=h, k=k, g=g: e.matmul(bank_ap(i, h), lhsT=lhs_of_k(k, g),
                                                                           rhs=WR[slot][:, k, :], start=False, stop=(k == nk - 1),
                                                                           skip_group_check=True),
                             reads=[bWR[slot]] + lbufs(k), writes=[bPB[i][h]], signal=(k == nk - 1))
                    outs.append((i, h))
                return outs

            tmpi = {"c": 0}

            def rot():
                tmpi["c"] ^= 1
                return tmpi["c"]

            WRF = lambda slot: WR[slot][:].rearrange("p k c -> p (k c)")
            acc_of = {}
            cstate = {}

            def blk_u(p, slot):
                for cgi, (seg, m) in enumerate(in_groups("u", p)):
                    ps, pbufs = fm_mm(slot, cgi, lambda k: xf[:, k, HALO:HALO + T], [bxf], False)
                    S.op("act", lambda e, ps=ps, m=m: e.activation(out=GU(m), in_=ps, func=AF.Gelu,
                                                                    bias=cfc(l, CF_BIN, SEG_U * 8 + m), scale=1.0),
                         reads=pbufs + [bCONST], writes=[bGU[m]])

            def blk_cc(p, slot):
                for cgi, (seg, m) in enumerate(in_groups("cc", p)):
                    ps, pbufs = fm_mm(slot, cgi, None, [bxf, bxfh], True)
                    if seg == SEG_GB:
                        q = rot()
                        S.op("act", lambda e, ps=ps, q=q, m=m: e.activation(out=SG[q][:], in_=ps, func=AF.Sigmoid,
                                                                             bias=cfc(l, CF_BIN, SEG_GB * 8 + m), scale=1.0),
                             reads=pbufs + [bCONST], writes=[bSG[q]])
                        cstate["sgq"] = q
                    else:
                        sgq = cstate["sgq"]
                        c4 = m % 4
                        S.op("dve", lambda e, ps=ps, c4=c4, m=m, sgq=sgq: e.scalar_tensor_tensor(
                            out=CCB[c4][:], in0=ps, scalar=cfc(l, CF_BIN, SEG_GA * 8 + m), in1=SG[sgq][:],
                            op0=ALU.add, op1=ALU.mult), reads=pbufs + [bSG[sgq], bCONST], writes=[bCCB[c4]])
                        if first:
                            S.op("pool", lambda e, c4=c4: e.memset(CCB[c4][:, 0:HALO], 0.0), writes=[bCCB[c4]])
                        if last:
                            S.op("pool", lambda e, c4=c4: e.memset(CCB[c4][:, HALO + T:WIN], 0.0), writes=[bCCB[c4]])

            def blk_dg(m, slot):
                c4 = m % 4
                i, h = next_bank()
                for jt in range(31):
                    S.op("pe", lambda e, i=i, h=h, jt=jt, c4=c4, slot=slot: e.matmul(
                        bank_ap(i, h), lhsT=WRF(slot)[:, jt * 128:(jt + 1) * 128], rhs=CCB[c4][:, jt:jt + T],
                        start=(jt == 0), stop=(jt == 30)), reads=[bWR[slot], bCCB[c4]], writes=[bPB[i][h]], signal=(jt == 30))
                S.op("act", lambda e, i=i, h=h, m=m: e.activation(out=CP[:, m, :], in_=bank_ap(i, h), func=AF.Identity,
                                                                   bias=cfc(l, CF_CCB, m), scale=1.0),
                     reads=[bPB[i][h], bCONST], writes=[bCP[m]])
                S.op("act", lambda e, m=m: e.activation(out=SQ[:, m, :], in_=CP[:, m, :], func=AF.Square),
                     reads=[bCP[m]], writes=[bSQ[m]])

            def blk_v(hh, slot):
                outs = tm_mm(slot, lambda k, g: xf[:, k, HALO + g * 128:HALO + (g + 1) * 128], lambda k: [bxf], 0, hh * 512)
                for g, (i, h) in enumerate(outs):
                    S.op("act", lambda e, i=i, h=h, g=g, hh=hh: e.activation(out=VNH(g, hh), in_=bank_ap(i, h), func=AF.Gelu),
                         reads=[bPB[i][h]], writes=[bVN(g)[hh]])

            def act_vln(g):
                if g == 0:
                    cstate["vgs"] = load_gb(LNB + 0)
                vfull = R[:, 24 + 2 * g:26 + 2 * g, :].rearrange("p a c -> p (a c)")
                ln_tm(vfull, vfull, bVN(g), bVN(g), cstate["vgs"])

            def act_sgu():
                for hd in range(8):
                    i, h = misc_bank()
                    for g in range(4):
                        S.op("pe", lambda e, i=i, h=h, g=g, hd=hd: e.matmul(
                            PP[i][:, h * 512 + g * 128:h * 512 + (g + 1) * 128], lhsT=ONES[0:1, :],
                            rhs=BROW[0:1, l, D + hd * 128:D + (hd + 1) * 128], start=(g == 0), stop=False, skip_group_check=True),
                            reads=[bCONST], writes=[bPB[i][h]], signal=False)
                    for g in range(4):
                        S.op("pe", lambda e, i=i, h=h, g=g, hd=hd: e.matmul(
                            PP[i][:, h * 512 + g * 128:h * 512 + (g + 1) * 128],
                            lhsT=R[:, 24 + 2 * g + hd // 4, (hd % 4) * 128:(hd % 4 + 1) * 128],
                            rhs=WST[:, l, hd * 128:(hd + 1) * 128], start=False, stop=(g == 3), skip_group_check=True),
                            reads=[bCONST] + bVN(g), writes=[bPB[i][h]], signal=(g == 3))
                    S.op("dve", lambda e, i=i, h=h, hd=hd: e.tensor_tensor(out=YA(hd), in0=bank_ap(i, h), in1=GU(hd), op=ALU.mult),
                         reads=[bPB[i][h], bGU[hd]], writes=[bYA[hd]])

            def blk_cb(groups, slot):
                for cgi, (seg, m) in enumerate(groups):
                    if seg == SEG_CG:
                        ps, pbufs = fm_mm(slot, cgi, None, [bxf, bxfh], True)
                        q = rot()
                        S.op("act", lambda e, ps=ps, q=q, m=m: e.activation(out=CG[q][:], in_=ps, func=AF.Identity,
                                                                             bias=cfc(l, CF_BIN, SEG_CG * 8 + m), scale=1.0),
                             reads=pbufs + [bCONST], writes=[bCG[q]])
                        cstate["cgq"] = q
                    elif seg == SEG_H:
                        cgq = cstate["cgq"]
                        ps, pbufs = fm_mm(slot, cgi, None, [bxf, bxfh], True)
                        q = rot()
                        S.op("dve", lambda e, ps=ps, q=q, m=m, cgq=cgq: e.scalar_tensor_tensor(
                            out=CGH[q][:], in0=ps, scalar=cfc(l, CF_BIN, SEG_H * 8 + m), in1=CG[cgq][:],
                            op0=ALU.add, op1=ALU.mult), reads=pbufs + [bCG[cgq], bCONST], writes=[bCGH[q]])
                        if first:
                            S.op("pool", lambda e, q=q: e.memset(CGH[q][:, 0:HALO], 0.0), writes=[bCGH[q]])
                        if last:
                            S.op("pool", lambda e, q=q: e.memset(CGH[q][:, HALO + T:WIN], 0.0), writes=[bCGH[q]])
                        a = rot()
                        S.op("dve", lambda e, q=q, a=a, m=m: e.tensor_scalar(
                            out=ACC[a][:], in0=CGH[q][:, HALO - 1:HALO - 1 + T], scalar1=cfc(l, CF_W3, m), scalar2=None,
                            op0=ALU.mult), reads=[bCGH[q], bCONST], writes=[bACC[a]])
                        for jt in (1, 2):
                            S.op("dve", lambda e, q=q, a=a, m=m, jt=jt: e.scalar_tensor_tensor(
                                out=ACC[a][:], in0=CGH[q][:, HALO - 1 + jt:HALO - 1 + jt + T],
                                scalar=cfc(l, CF_W3, jt * 8 + m), in1=ACC[a][:], op0=ALU.mult, op1=ALU.add),
                                reads=[bCGH[q], bACC[a], bCONST], writes=[bACC[a]])
                        acc_of[m] = a
                        S.pump(3)
                    else:
                        ps, pbufs = fm_mm(slot, cgi, lambda k: xf[:, k, HALO:HALO + T], [bxf], False)
                        a = acc_of[m]
                        S.op("dve", lambda e, ps=ps, a=a, m=m: e.scalar_tensor_tensor(
                            out=YB(m), in0=ps, scalar=cfc(l, CF_BIN, SEG_BG * 8 + m), in1=ACC[a][:],
                            op0=ALU.add, op1=ALU.mult), reads=pbufs + [bACC[a], bCONST], writes=[bYB[m]])

            def act_cnorm():
                i1, h1 = misc_bank()
                i2, h2 = misc_bank()
                for k in range(8):
                    S.op("pe", lambda e, k=k: e.matmul(bank_ap(i1, h1), lhsT=ONES[:, :], rhs=CP[:, k, :], start=(k == 0), stop=(k == 7)),
                         reads=[bCONST, bCP[k]], writes=[bPB[i1][h1]], signal=(k == 7))
                for k in range(8):
                    S.op("pe", lambda e, k=k: e.matmul(bank_ap(i2, h2), lhsT=ONES[:, :], rhs=SQ[:, k, :], start=(k == 0), stop=(k == 7)),
                         reads=[bCONST, bSQ[k]], writes=[bPB[i2][h2]], signal=(k == 7))
                S.op("act", lambda e: e.activation(out=MEAN[:], in_=bank_ap(i1, h1), func=AF.Identity, scale=1.0 / D),
                     reads=[bPB[i1][h1]], writes=bSTATL)
                S.op("act", lambda e: e.activation(out=E2[:], in_=bank_ap(i2, h2), func=AF.Identity, scale=1.0 / D),
                     reads=[bPB[i2][h2]], writes=bSTATL)
                S.op("dve", lambda e: e.scalar_tensor_tensor(out=RSTD[:], in0=MEAN[:], scalar=-1.0, in1=MEAN[:],
                                                             op0=ALU.mult, op1=ALU.mult), reads=bSTATL, writes=bSTATL)
                S.op("dve", lambda e: e.scalar_tensor_tensor(out=E2[:], in0=E2[:], scalar=EPS, in1=RSTD[:],
                                                             op0=ALU.add, op1=ALU.add), reads=bSTATL, writes=bSTATL)
                S.op("act", lambda e: e.activation(out=RSTD[:], in_=E2[:], func=AF.Sqrt), reads=bSTATL, writes=bSTATL)
                S.op("dve", lambda e: e.reciprocal(out=RSTD[:], in_=RSTD[:]), reads=bSTATL, writes=bSTATL)
                for m in range(8):
                    q = m % 2
                    S.defer("pool", lambda e, m=m, q=q: e.tensor_tensor(out=ZT[q][:], in0=CP[:, m, :], in1=MEAN[:], op=ALU.subtract),
                            reads=[bCP[m]] + bSTATL, writes=[bZT[q]])
                    S.defer("pool", lambda e, m=m, q=q: e.tensor_tensor(out=ZT[q][:], in0=ZT[q][:], in1=RSTD[:], op=ALU.mult),
                            reads=[bZT[q]] + bSTATL, writes=[bZT[q]])
                    S.defer("act", lambda e, m=m, q=q: e.activation(out=CP[:, m, :], in_=ZT[q][:], func=AF.Silu,
                                                                     bias=cfc(l, CF_CNB, m), scale=cfc(l, CF_CNG, m)),
                            reads=[bZT[q], bCONST], writes=[bCP[m]])

            Ysrc = [(YA, bYA), (YB, bYB), (lambda m: CP[:, m, :], bCP)]

            def blk_gate(p, slot):
                n, mb = p
                r = (mb * 3 + n) % 2
                for cgi, (seg, m) in enumerate(in_groups("gate", p)):
                    ps, pbufs = fm_mm(slot, cgi, lambda k: xf[:, k, HALO:HALO + T], [bxf], False)
                    S.op("act", lambda e, ps=ps, m=m, cgi=cgi, r=r, seg=seg: e.activation(
                        out=SGT[r][:, cgi, :], in_=ps, func=AF.Sigmoid, bias=cfc(l, CF_BIN, seg * 8 + m), scale=1.0),
                        reads=pbufs + [bCONST], writes=[bSGT[r]])

            def blk_br(p, slot):
                n, mb = p
                r = (mb * 3 + n) % 2
                Yn, bYn = Ysrc[n]
                for cgi in range(4):
                    m = mb * 4 + cgi
                    ps, pbufs = fm_mm(slot, cgi, lambda k, Yn=Yn: Yn(k), list(bYn), False)
                    if n == 0:
                        S.op("dve", lambda e, ps=ps, cgi=cgi, r=r: e.tensor_tensor(out=MG[:, cgi, :], in0=ps, in1=SGT[r][:, cgi, :], op=ALU.mult),
                             reads=pbufs + [bSGT[r]], writes=[bMG[cgi]])
                    else:
                        q = rot()
                        S.op("dve", lambda e, ps=ps, q=q, cgi=cgi, r=r: e.tensor_tensor(out=TM1[q][:], in0=ps, in1=SGT[r][:, cgi, :], op=ALU.mult),
                             reads=pbufs + [bSGT[r]], writes=[bTM1[q]])
                        if n == 1:
                            S.op("pool", lambda e, q=q, cgi=cgi: e.tensor_tensor(out=MG[:, cgi, :], in0=MG[:, cgi, :], in1=TM1[q][:], op=ALU.add),
                                 reads=[bMG[cgi], bTM1[q]], writes=[bMG[cgi]])
                        else:
                            S.op("pool", lambda e, q=q, m=m, cgi=cgi: e.tensor_tensor(out=SQ[:, m, :], in0=MG[:, cgi, :], in1=TM1[q][:], op=ALU.add),
                                 reads=[bMG[cgi], bTM1[q]], writes=[bSQ[m]])

            if pendingT and pendingT[0][0] == gidx[(si, j)]:
                while pendingT:
                    pendingT.pop(0)[1]()
            handlers = {"u": blk_u, "cc": blk_cc, "dg": blk_dg, "v": blk_v, "cb": blk_cb, "gate": blk_gate, "br": blk_br}
            def run_tails():
                while pendingT:
                    pendingT.pop(0)[1]()

            def act_stageb():
                while pendingB:
                    stage_b(*pendingB.pop(0))
            acts = {"vln0": lambda: act_vln(0), "vln1": lambda: act_vln(1), "vln2": lambda: act_vln(2),
                    "vln3": lambda: act_vln(3), "sgu": act_sgu, "cnorm": act_cnorm, "stageB": act_stageb,
                    "flush": lambda: S.pump(), "haloT": halo_t, "tail": run_tails}
            si_ = 0
            while SCHED[si_][1] != "out":
                t_, k_, p_ = SCHED[si_]
                if t_ == "blk":
                    handlers[k_](p_, next_block())
                else:
                    acts[k_]()
                si_ += 1
            MGB, bMGB = SQ, bSQ
            gs = load_gb(LNB + 2)
            wslots = [next_block(), next_block(prefetch=False)]

            def wout_group(g):
                banks = []
                for hh in range(2):
                    i, h = next_bank()
                    banks.append((i, h))
                    S.op("pe", lambda e, i=i, h=h, hh=hh: e.matmul(bank_ap(i, h), lhsT=ONES[32:33, :],
                                                                    rhs=BROW[32:33, l, hh * 512:(hh + 1) * 512],
                                                                    start=True, stop=False, skip_group_check=True),
                         reads=[bCONST], writes=[bPB[i][h]], signal=False)
                    for k in range(8):
                        S.op("pe", lambda e, i=i, h=h, k=k, g=g, hh=hh: e.matmul(
                            bank_ap(i, h), lhsT=MGB[:, k, g * 128:(g + 1) * 128], rhs=WR[wslots[hh]][:, k, :],
                            start=False, stop=(k == 7), skip_group_check=True),
                            reads=[bWR[wslots[hh]], bMGB[k]], writes=[bPB[i][h]], signal=(k == 7))
                for hh, (i, h) in enumerate(banks):
                    S.op("dve", lambda e, i=i, h=h, g=g, hh=hh: e.scalar_tensor_tensor(
                        out=XT[s][:, g, hh * 512:(hh + 1) * 512], in0=XT[s][:, g, hh * 512:(hh + 1) * 512], scalar=ALPHA,
                        in1=bank_ap(i, h), op0=ALU.mult, op1=ALU.add), reads=[bPB[i][h], bXT[s][g]], writes=[bXT[s][g]])
            ln_groups(s, gs, pre=wout_group)
            act_stageb()
            X1F, bX1F = CP, bCP
            to_fm(s, s, X1F, list(bX1F), 0)
            for mb in range(8):
                slot = next_block()
                for cgi in range(4):
                    c = mb * 4 + cgi
                    ps, pbufs = fm_mm(slot, cgi, lambda k: X1F[:, k, :], list(bX1F), False)
                    q = rot()
                    S.op("act", lambda e, ps=ps, q=q, c=c: e.activation(out=RR[q][:], in_=ps, func=AF.Relu,
                                                                         bias=cfc(l, CF_BFF1, c), scale=1.0),
                         reads=pbufs + [bCONST], writes=[bRR[q]])
                    S.op("pool", lambda e, q=q, c=c: e.tensor_tensor(out=R[:, c, :], in0=RR[q][:], in1=RR[q][:], op=ALU.mult),
                         reads=[bRR[q]], writes=[bR[c]])
            gs = load_gb(LNB + 4)
            ln_ks = []
            for hh in range(2):
                banks = [next_bank() for _ in range(4)]
                for kb in range(4):
                    slot = next_block()
                    for g in range(4):
                        i, h = banks[g]
                        if kb == 0:
                            S.op("pe", lambda e, i=i, h=h, hh=hh: e.matmul(bank_ap(i, h), lhsT=ONES[64:65, :],
                                                                            rhs=BROW[64:65, l, hh * 512:(hh + 1) * 512],
                                                                            start=True, stop=False, skip_group_check=True),
                                 reads=[bCONST], writes=[bPB[i][h]], signal=False)
                        for k in range(8):
                            kk = kb * 8 + k
                            lastmm = (kb == 3 and k == 7)
                            S.op("pe", lambda e, i=i, h=h, g=g, k=k, kk=kk, lastmm=lastmm, slot=slot: e.matmul(
                                bank_ap(i, h), lhsT=R[:, kk, g * 128:(g + 1) * 128], rhs=WR[slot][:, k, :],
                                start=False, stop=lastmm, skip_group_check=True),
                                reads=[bWR[slot], bR[kk]], writes=[bPB[i][h]], signal=(k == 7))
                        if kb == 3:
                            S.op("dve", lambda e, i=i, h=h, g=g, hh=hh: e.scalar_tensor_tensor(
                                out=XT[s][:, g, hh * 512:(hh + 1) * 512], in0=XT[s][:, g, hh * 512:(hh + 1) * 512], scalar=ALPHA,
                                in1=bank_ap(i, h), op0=ALU.mult, op1=ALU.add), reads=[bPB[i][h], bXT[s][g]], writes=[bXT[s][g]])
                            if hh == 1:
                                ln_ks.append(ln_stats(XT[s][:, g, :], [bXT[s][g]]))
                                if g >= 1:
                                    ln_apply(ln_ks[g - 1], XT[s][:, g - 1, :], XT[s][:, g - 1, :], [bXT[s][g - 1]], [bXT[s][g - 1]], gs)
            ln_apply(ln_ks[3], XT[s][:, 3, :], XT[s][:, 3, :], [bXT[s][3]], [bXT[s][3]], gs)
            if l == 0:
                def tail():
                    to_fm(s, s, XF[sf], [bXFc[sf]], HALO)
                    if not first:
                        sp_ = 1 - sf
                        S.op("pool", lambda e: e.tensor_copy(out=XF[sf][:, :, 0:HALO], in_=XF[sp_][:, :, T:T + HALO]),
                             reads=[bXFc[sp_]], writes=[bXFh[sf]])
                        S.op("pool", lambda e: e.tensor_copy(out=XF[sp_][:, :, HALO + T:WIN], in_=XF[sf][:, :, HALO:2 * HALO]),
                             reads=[bXFc[sf]], writes=[bXFh[sp_]])
                pendingT.append((gidx[(si, j)], tail))
            else:
                r0 = row0 + j * T
                for g in range(4):
                    S.dma("pool", lambda e, g=g: e.dma_start(out=yout[r0 + g * 128:r0 + (g + 1) * 128, :], in_=XT[s][:, g, :]),
                          f"o{s}{g}", reads=[bXT[s][g]])

        stage_a(*tiles[0])
        stage_b(*tiles[0])
        pump_cast(3)
        for n_, (kind, si, j) in enumerate(tile_layers):
            if kind == "L1":
                layer(0, si, j)
                nxt = gidx[(si, j)] + 1
                if nxt < len(tiles):
                    stage_a(*tiles[nxt])
                    pendingB.append(tiles[nxt])
                    if not (n_ + 1 < len(tile_layers) and tile_layers[n_ + 1][0] == "L2"):
                        stage_b(*pendingB.pop(0))
            else:
                layer(1, si, j)
        S.wait_all("sp", [f"o{s_}{g}" for s_ in range(3) for g in range(4)] + ["pe", "act", "dve", "pool"])

        engs = {"sp": "sync", "act": "scalar", "dve": "vector", "pool": "gpsimd", "pe": "tensor"}
        with nc.Block() as block:
            def make(key):
                def body(e):
                    for it in S.streams[key]:
                        if it[0] == "wait":
                            e.wait_ge(sems[it[1]], it[2])
                        else:
                            ins = it[1](e)
                            if it[3]:
                                ins.then_inc(sems[it[2]], it[3])
                return body
            for key, attr in engs.items():
                getattr(block, attr)(make(key))
    return nc


def pack_aux(ln_in_g, ln_in_b, b_in, sgu_ln_g, sgu_ln_b, sgu_w, sgu_b, sconv_w, cconv_w, cconv_b,
             cnorm_g, cnorm_b, b_out, ln1_g, ln1_b, b_ff1, b_ff2, ln2_g, ln2_b):
    f = lambda a: np.asarray(a, dtype=np.float32)
    cfm = np.empty((128, 2 * CF_N), np.float32)
    for l in range(2):
        o = l * CF_N
        cfm[:, o + CF_BIN:o + CF_BIN + 80] = f(b_in)[l].reshape(80, 128).T
        cfm[:, o + CF_BFF1:o + CF_BFF1 + 32] = f(b_ff1)[l].reshape(32, 128).T
        cfm[:, o + CF_CCB:o + CF_CCB + 8] = f(cconv_b)[l].reshape(8, 128).T
        cfm[:, o + CF_CNG:o + CF_CNG + 8] = f(cnorm_g)[l].reshape(8, 128).T
        cfm[:, o + CF_CNB:o + CF_CNB + 8] = f(cnorm_b)[l].reshape(8, 128).T
        cfm[:, o + CF_W3:o + CF_W3 + 24] = f(sconv_w)[l].reshape(3, 8, 128).transpose(2, 0, 1).reshape(128, 24)
    brow32 = np.empty((2, 4, D), np.float32)
    for l in range(2):
        brow32[l, 0] = f(b_in)[l, SEG_V * 1024:(SEG_V + 1) * 1024]
        brow32[l, 1] = f(b_out)[l]
        brow32[l, 2] = f(b_ff2)[l]
        brow32[l, 3] = f(sgu_b)[l].reshape(-1)
    lnp = np.empty((14, 128, D), np.float32)
    vecs = [f(ln_in_g), f(ln_in_b)]
    for l in range(2):
        vecs += [f(sgu_ln_g)[l], f(sgu_ln_b)[l], f(ln1_g)[l], f(ln1_b)[l], f(ln2_g)[l], f(ln2_b)[l]]
    for i, v in enumerate(vecs):
        lnp[i] = np.broadcast_to(v[None, :], (128, D))
    wst32 = np.ascontiguousarray(f(sgu_w).transpose(0, 3, 1, 2)).reshape(2, 128, 8 * 128)
    ident = np.eye(128, dtype=np.float32)

    return {"cfm": cfm, "brow32": brow32, "lnp": lnp, "wst32": wst32, "ident": ident}


_NC_CACHE = {}


def kernel(x_prompt, x_sample, ln_in_g, ln_in_b, w_in, b_in, sgu_ln_g, sgu_ln_b, sgu_w, sgu_b,
           sconv_w, cconv_w, cconv_b, cnorm_g, cnorm_b, w_branch, w_out, b_out,
           ln1_g, ln1_b, w_ff1, b_ff1, w_ff2, b_ff2, ln2_g, ln2_b):
    f = lambda a: np.asarray(a, dtype=np.float32)
    x_prompt, x_sample = f(x_prompt), f(x_sample)
    w_in, w_branch, w_out, w_ff1, w_ff2 = f(w_in), f(w_branch), f(w_out), f(w_ff1), f(w_ff2)
    wpk32 = pack_weights(w_in, w_branch, w_out, w_ff1, w_ff2, f(cconv_w))
    aux = pack_aux(ln_in_g, ln_in_b, b_in, sgu_ln_g, sgu_ln_b, sgu_w, sgu_b, sconv_w, cconv_w, cconv_b,
                   cnorm_g, cnorm_b, b_out, ln1_g, ln1_b, b_ff1, b_ff2, ln2_g, ln2_b)
    in_maps = []
    for c in range(NCORES):
        xin = np.concatenate([x_prompt[2 * c:2 * c + 2].reshape(-1, D), x_sample[4 * c:4 * c + 4].reshape(-1, D)], axis=0)
        in_maps.append(dict(aux, xin=np.ascontiguousarray(xin), wpk32=wpk32))
    if "nc" not in _NC_CACHE:
        _NC_CACHE["nc"] = build_program()
    res = run_bass_kernel_spmd(_NC_CACHE["nc"], in_maps, core_ids=list(range(NCORES)))
    y_prompt = np.empty((16, 4096, D), np.float32)
    y_sample = np.empty((32, 2048, D), np.float32)
    for c in range(NCORES):
        y = np.asarray(res.results[c]["yout"])
        y_prompt[2 * c:2 * c + 2] = y[:8192].reshape(2, 4096, D)
        y_sample[4 * c:4 * c + 4] = y[8192:].reshape(4, 2048, D)
    return (y_prompt, y_sample)
```

```python
import contextlib
import numpy as np
import concourse.bass as bass
import concourse.mybir as mybir
from concourse.bass_utils import run_bass_kernel_spmd

F32 = mybir.dt.float32
BF16 = mybir.dt.bfloat16
AF = mybir.ActivationFunctionType
ALU = mybir.AluOpType

D = 1024
NCORES = 8
T = 512
HALO = 15
WIN = T + 2 * HALO
NBLK = 52
NSLOT = 3
ALPHA = float(4 ** 0.25)
EPS = 1e-5
TOK_PER_CORE = 16384
SEQS = [(0, 8), (4096, 8), (8192, 4), (10240, 4), (12288, 4), (14336, 4)]

SEG_U, SEG_V, SEG_BG, SEG_CG, SEG_H, SEG_GA, SEG_GB, SEG_G0 = 0, 1, 2, 3, 4, 5, 6, 7

CF_BIN = 0
CF_BFF1 = 80
CF_CCB = 112
CF_CNG = 120
CF_CNB = 128
CF_W3 = 136
CF_N = 160


def layer_schedule():
    sch = []
    cb = []
    for m in range(8):
        cb += [(SEG_CG, m), (SEG_H, m), (SEG_BG, m)]
    B = lambda kind, p: sch.append(("blk", kind, p))
    A = lambda name: sch.append(("act", name, None))
    B("u", 0); B("u", 1); B("v", 0); B("v", 1); A("haloT"); A("tail")
    B("cc", 0); B("cc", 1); A("vln0"); B("dg", 0); B("dg", 1)
    B("cc", 2); A("vln1"); B("dg", 2); B("dg", 3)
    B("cc", 3); A("vln2"); B("dg", 4); B("dg", 5)
    B("cb", cb[0:4]); A("vln3"); B("dg", 6); B("dg", 7); A("cnorm")
    for i in range(1, 6):
        B("cb", cb[4 * i:4 * i + 4])
    A("flush"); A("sgu")
    for mb in range(2):
        for n in range(3):
            B("gate", (n, mb)); B("br", (n, mb))
    B("out", 0); B("out", 1); A("ln1")
    for mb in range(8):
        B("ff1", mb)
    for h in range(2):
        for kb in range(4):
            B("ff2", (h, kb))
    A("ln2")
    return sch


SCHED = layer_schedule()
PLAN = [(k, p) for (t, k, p) in SCHED if t == "blk"]
assert len(PLAN) == NBLK


def in_groups(kind, p):
    if kind == "u":
        return [(SEG_U, 4 * p + j) for j in range(4)]
    if kind == "v":
        return [(SEG_V, 4 * p + j) for j in range(4)]
    if kind == "cc":
        return [(SEG_GB, 2 * p), (SEG_GA, 2 * p), (SEG_GB, 2 * p + 1), (SEG_GA, 2 * p + 1)]
    if kind == "cb":
        return p
    if kind == "gate":
        n, mb = p
        return [(SEG_G0 + n, 4 * mb + j) for j in range(4)]
    return None


def _pack(W):
    return np.ascontiguousarray(W.reshape(8, 128, 512).transpose(1, 0, 2)).reshape(128, 4096)


def pack_weights(w_in, w_branch, w_out, w_ff1, w_ff2, cconv_w):
    out = np.empty((2, NBLK, 128, 4096), np.float32)
    ar = np.arange(128)
    for l in range(2):
        for b, (kind, p) in enumerate(PLAN):
            grp = in_groups(kind, p)
            if grp is not None:
                cols = np.concatenate([np.arange(sg * 1024 + m * 128, sg * 1024 + m * 128 + 128) for (sg, m) in grp])
                W = w_in[l][:, cols]
            elif kind == "dg":
                blk = np.zeros((128, 32, 128), np.float32)
                for j in range(31):
                    blk[ar, j, ar] = cconv_w[l, j, p * 128:(p + 1) * 128]
                out[l, b] = blk.reshape(128, 4096)
                continue
            elif kind == "br":
                n, mb = p
                W = w_branch[l, n][:, mb * 512:(mb + 1) * 512]
            elif kind == "out":
                W = w_out[l][:, p * 512:(p + 1) * 512]
            elif kind == "ff1":
                W = w_ff1[l][:, p * 512:(p + 1) * 512]
            else:
                h, kb = p
                W = w_ff2[l][kb * 1024:(kb + 1) * 1024, h * 512:(h + 1) * 512]
            out[l, b] = _pack(W)
    return out


class Buf:
    __slots__ = ("w", "r")

    def __init__(self):
        self.w = None
        self.r = {}


class Sched:
    ENG = ("sp", "act", "dve", "pool", "pe")

    def __init__(self):
        self.streams = {k: [] for k in self.ENG}
        self.cnt = {}
        self.seen = {k: {} for k in self.ENG}
        self.lazy = []

    def _waits(self, eng, reads, writes):
        deps = {}

        def add(tok, raw):
            k, v = tok
            if k == eng and eng == "pe":
                return
            if k not in self.ENG:
                v = self.cnt[k]
            if deps.get(k, 0) < v:
                deps[k] = v
        for b in reads:
            if b.w is not None:
                add(b.w, True)
        for b in writes:
            if b.w is not None:
                add(b.w, False)
            for k, v in b.r.items():
                add((k, v), False)
        seen = self.seen[eng]
        for k, v in deps.items():
            if seen.get(k, 0) < v:
                self.streams[eng].append(("wait", k, v))
                seen[k] = v

    def _mark(self, tok, reads, writes):
        k, v = tok
        for b in reads:
            if b.r.get(k, 0) < v:
                b.r[k] = v
        for b in writes:
            b.w = tok
            b.r = {}

    def op(self, eng, fn, reads=(), writes=(), signal=True):
        self._waits(eng, reads, writes)
        c = self.cnt.get(eng, 0)
        if signal:
            c += 1
            self.cnt[eng] = c
            self.streams[eng].append(("op", fn, eng, 1))
            tok = (eng, c)
        else:
            self.streams[eng].append(("op", fn, None, 0))
            tok = (eng, c + 1)
        self._mark(tok, reads, writes)

    def dma(self, eng, fn, dsem, reads=(), writes=()):
        self._waits(eng, reads, writes)
        c = self.cnt.get(dsem, 0) + 16
        self.cnt[dsem] = c
        self.streams[eng].append(("op", fn, dsem, 16))
        self._mark((dsem, c), reads, writes)

    def defer(self, eng, fn, reads=(), writes=()):
        self.lazy.append((eng, fn, list(reads), list(writes)))

    def pump(self, n=None):
        while self.lazy and (n is None or n > 0):
            eng, fn, r, w = self.lazy.pop(0)
            self.op(eng, fn, reads=r, writes=w)
            if n is not None:
                n -= 1

    def wait_all(self, eng, keys):
        for k in keys:
            v = self.cnt.get(k, 0)
            if v and self.seen[eng].get(k, 0) < v:
                self.streams[eng].append(("wait", k, v))
                self.seen[eng][k] = v


def build_program(SEQS=SEQS):
    TOK_PER_CORE = sum(n for _, n in SEQS) * T
    nc = bass.Bass("TRN2", target_bir_lowering=False)
    xin = nc.dram_tensor("xin", [TOK_PER_CORE, D], F32, kind="ExternalInput").ap()
    wpk32 = nc.dram_tensor("wpk32", [2, NBLK, 128, 4096], F32, kind="ExternalInput").ap()
    cfm = nc.dram_tensor("cfm", [128, 2 * CF_N], F32, kind="ExternalInput").ap()
    brow32 = nc.dram_tensor("brow32", [2, 4, D], F32, kind="ExternalInput").ap()
    lnp = nc.dram_tensor("lnp", [14, 128, D], F32, kind="ExternalInput").ap()
    wst32 = nc.dram_tensor("wst32", [2, 128, 8 * 128], F32, kind="ExternalInput").ap()
    ident = nc.dram_tensor("ident", [128, 128], F32, kind="ExternalInput").ap()
    yout = nc.dram_tensor("yout", [TOK_PER_CORE, D], F32, kind="ExternalOutput").ap()
    wpk = nc.dram_tensor("wpk", [2, NBLK, 128, 4096], BF16, kind="Internal").ap()

    S = Sched()
    es = contextlib.ExitStack()

    def sb(name, shape, dt):
        return es.enter_context(nc.sbuf_tensor(name, shape, dt))

    with es:
        XT = [sb(f"xt{i}", [128, 4, D], F32) for i in range(3)]
        XF = [sb(f"xf{i}", [128, 8, WIN], BF16) for i in range(2)]
        WR = [sb(f"wr{i}", [128, 8, 512], BF16) for i in range(NSLOT)]
        R = sb("rbig", [128, 32, 512], BF16)
        CP = sb("cp", [128, 8, 512], BF16)
        SQ = sb("sq", [128, 8, 512], BF16)
        SGT = [sb(f"sgt{i}", [128, 4, 512], BF16) for i in range(2)]
        MG = sb("mg", [128, 4, 512], F32)
        CG = [sb(f"cg{i}", [128, WIN], F32) for i in range(2)]
        CGH = [sb(f"cgh{i}", [128, WIN], F32) for i in range(2)]
        SG = CG
        CCB = [sb(f"ccb{i}", [128, WIN], BF16) for i in range(4)]
        ACC = [sb(f"acc{i}", [128, 512], F32) for i in range(2)]
        TM1 = [sb(f"tm1{i}", [128, 512], F32) for i in range(2)]
        RR = [sb(f"rr{i}", [128, 512], F32) for i in range(2)]
        ZT = RR
        MEAN = TM1[0]
        E2 = sb("e2", [128, 512], F32)
        RSTD = TM1[1]
        NEGH = sb("negh", [128, 1], F32)
        GBT = [[sb(f"gb{i}{j}", [128, D], F32) for j in range(2)] for i in range(1)]
        LNT = sb("lnt", [128, D], F32)
        HB = sb("hb", [128, 8, 2 * HALO], BF16)
        XH = MG[:, 0:2, :].rearrange("p a c -> p (a c)")
        CF = sb("cf", [128, 2 * CF_N], F32)
        IDT = sb("idt", [128, 128], F32)
        ONES = sb("ones", [128, 128], BF16)
        BROW = sb("brow", [128, 2, 2 * D], BF16)
        WST = sb("wst", [128, 2, 8 * 128], BF16)
        NLN = 8
        ST = [sb(f"st{i}", [128, 12], F32) for i in range(NLN)]
        MV = [sb(f"mv{i}", [128, 2], F32) for i in range(NLN)]
        VE = [sb(f"ve{i}", [128, 1], F32) for i in range(NLN)]
        RS = [sb(f"rs{i}", [128, 1], F32) for i in range(NLN)]
        EPST = sb("epst", [128, 1], F32)
        PP = [es.enter_context(nc.psum_tensor(f"pp{i}", [128, 1024], F32)) for i in range(4)]

        keys = ["pe", "act", "dve", "pool", "misc", "xh", "gb0", "gb1"] + [f"w{i}" for i in range(NSLOT)] + \
               [f"c{l}_{q}" for l in range(2) for q in range(NBLK // 4)] + \
               [f"{a}{s_}{g}" for a in "xo" for s_ in range(3) for g in range(4)]
        sems = {k: es.enter_context(nc.semaphore(k)) for k in keys}

        bXT = [[Buf() for _ in range(4)] for _ in range(3)]
        bXFc = [Buf() for _ in range(2)]
        bXFh = [Buf() for _ in range(2)]
        bWR = [Buf() for _ in range(NSLOT)]
        bR = [Buf() for _ in range(32)]
        bCP = [Buf() for _ in range(8)]
        bSQ = [Buf() for _ in range(8)]
        bSGT = [Buf() for _ in range(2)]
        bMG = [Buf() for _ in range(4)]
        bXHL = [bMG[0], bMG[1]]
        bCG = [Buf() for _ in range(2)]
        bCGH = [Buf() for _ in range(2)]
        bSG = bCG
        bCCB = [Buf() for _ in range(4)]
        bACC = [Buf() for _ in range(2)]
        bTM1 = [Buf() for _ in range(2)]
        bRR = [Buf() for _ in range(2)]
        bZT = bRR
        bE2 = Buf()
        bSTATL = [bTM1[0], bTM1[1], bE2]
        bGB = [Buf() for _ in range(2)]
        bLNT = Buf()
        bHB = Buf()
        bXH = None
        bCONST = Buf()
        bLN = [Buf() for _ in range(8)]
        bPB = [[Buf(), Buf()] for _ in range(4)]
        bWPK = [[Buf() for _ in range(NBLK // 4)] for _ in range(2)]

        st = {"pp": 0, "misc": 0, "ln": 0, "gb": 0, "tmp": 0, "gbcur": -1}

        def next_bank():
            p = st["pp"]
            st["pp"] = (p + 1) % 8
            return p // 2, p % 2

        def next_pair():
            p = st["pp"]
            if p % 2:
                p = (p + 1) % 8
            st["pp"] = (p + 2) % 8
            return p // 2

        def misc_bank():
            h = st["misc"]
            st["misc"] = 1 - h
            return 3, h

        def bank_ap(i, h, n=512):
            return PP[i][:, h * 512:h * 512 + n]

        castq = [(l, q) for l in range(2) for q in range(NBLK // 4)]

        def pump_cast(n=None, upto=None):
            while castq and (n is None or n > 0):
                if upto is not None and castq[0] > upto:
                    break
                l, q = castq.pop(0)
                for b in range(4 * q, 4 * q + 4):
                    S.dma("pool", lambda e, l=l, b=b: e.dma_start(
                        out=wpk[l, b].rearrange("(a r) c -> a (r c)", a=16),
                        in_=wpk32[l, b].rearrange("(a r) c -> a (r c)", a=16)), f"c{l}_{q}", writes=[bWPK[l][q]])
                if n is not None:
                    n -= 1
        S.dma("sp", lambda e: e.dma_start(out=CF[:], in_=cfm[:, :]), "misc", writes=[bCONST])
        S.dma("sp", lambda e: e.dma_start(out=IDT[:], in_=ident[:, :]), "misc", writes=[bCONST])
        S.wait_all("sp", ["misc"])
        for l in range(2):
            for r in range(3):
                S.dma("sp", lambda e, l=l, r=r: e.dma_start(out=XH[32 * r:32 * r + 1, :], in_=brow32[l, r:r + 1, :]),
                      "misc", writes=bXHL)
            S.wait_all("sp", ["misc"])
            for r in range(3):
                S.op("dve", lambda e, l=l, r=r: e.tensor_copy(out=BROW[32 * r:32 * r + 1, l, 0:D],
                                                              in_=XH[32 * r:32 * r + 1, :]),
                     reads=bXHL, writes=[bCONST])
            S.dma("sp", lambda e, l=l: e.dma_start(out=XH[0:1, :], in_=brow32[l, 3:4, :]), "misc", writes=bXHL)
            S.wait_all("sp", ["misc"])
            S.op("dve", lambda e, l=l: e.tensor_copy(out=BROW[0:1, l, D:2 * D], in_=XH[0:1, :]),
                 reads=bXHL, writes=[bCONST])
            S.dma("sp", lambda e, l=l: e.dma_start(out=LNT[:], in_=wst32[l]), "misc", writes=[bLNT])
            S.wait_all("sp", ["misc"])
            S.op("dve", lambda e, l=l: e.tensor_copy(out=WST[:, l, :], in_=LNT[:]), reads=[bLNT], writes=[bCONST])
        S.op("dve", lambda e: e.memset(ONES[:], 1.0), writes=[bCONST])
        S.op("dve", lambda e: e.memset(NEGH[:], -0.5), writes=[bCONST])
        S.op("dve", lambda e: e.memset(EPST[:], EPS), writes=[bCONST])
        S.op("dve", lambda e: e.memset(XH[:], 0.0), reads=[], writes=bXHL)

        tile_layers = []
        for si, (row0, ntile) in enumerate(SEQS):
            for j in range(ntile):
                tile_layers.append(("L1", si, j))
                if j >= 1:
                    tile_layers.append(("L2", si, j - 1))
            tile_layers.append(("L2", si, ntile - 1))
        gblocks = []
        for (kind, si, j) in tile_layers:
            l = 0 if kind == "L1" else 1
            for b in range(NBLK):
                gblocks.append((l, b))
        wst_ = {"issued": 0, "used": 0}

        def issue_loads(upto):
            while wst_["issued"] < min(upto, len(gblocks)):
                i = wst_["issued"]
                l, b = gblocks[i]
                slot = i % NSLOT
                pump_cast(upto=(l, b // 4))
                S.dma("sp", lambda e, l=l, b=b, slot=slot: e.dma_start(
                    out=WR[slot][:].rearrange("p k c -> p (k c)"), in_=wpk[l, b]),
                    f"w{slot}", reads=[bWPK[l][b // 4]], writes=[bWR[slot]])
                wst_["issued"] += 1

        def next_block(prefetch=True):
            i = wst_["used"]
            issue_loads(i + NSLOT if prefetch else i + 1)
            if castq and i % (2 if castq[0][0] == 0 else 4) == 0:
                pump_cast(1)
            wst_["used"] += 1
            return i % NSLOT

        def load_gb(idx):
            s = 0
            st["gbcur"] = idx
            S.dma("pool", lambda e: e.dma_start(out=GBT[s][0][:], in_=lnp[idx]), f"gb{s}", writes=[bGB[s]])
            S.dma("pool", lambda e: e.dma_start(out=GBT[s][1][:], in_=lnp[idx + 1]), f"gb{s}", writes=[bGB[s]])
            return s

        def ln_stats(src, rbufs, npart=128):
            k = st["ln"]
            st["ln"] = (k + 1) % NLN
            P = slice(0, npart)
            S.op("dve", lambda e: e.bn_stats(out=ST[k][P, 0:6], in_=src[:, 0:512]), reads=rbufs, writes=[bLN[k]])
            S.op("dve", lambda e: e.bn_stats(out=ST[k][P, 6:12], in_=src[:, 512:1024]), reads=rbufs, writes=[bLN[k]])
            S.op("dve", lambda e: e.bn_aggr(out=MV[k][P, :], in_=ST[k][P, :]), reads=[bLN[k]], writes=[bLN[k]])
            S.op("pool", lambda e: e.tensor_tensor(out=VE[k][P, :], in0=MV[k][P, 1:2], in1=EPST[P, :], op=ALU.add),
                 reads=[bLN[k], bCONST], writes=[bLN[k]])
            S.op("pool", lambda e: e.tensor_tensor(out=RS[k][P, :], in0=VE[k][P, :], in1=NEGH[P, 0:1], op=ALU.pow),
                 reads=[bLN[k], bCONST], writes=[bLN[k]])
            return k

        def ln_apply(k, src, dst, rbufs, wbufs, gs, npart=128):
            P = slice(0, npart)
            S.op("dve", lambda e: e.scalar_tensor_tensor(out=LNT[P, :], in0=src, scalar=MV[k][P, 0:1],
                                                         in1=GBT[gs][0][P, :], op0=ALU.subtract, op1=ALU.mult),
                 reads=list(rbufs) + [bLN[k], bGB[gs]], writes=[bLNT])
            S.op("dve", lambda e: e.scalar_tensor_tensor(out=dst, in0=LNT[P, :], scalar=RS[k][P, 0:1],
                                                         in1=GBT[gs][1][P, :], op0=ALU.mult, op1=ALU.add),
                 reads=[bLNT, bLN[k], bGB[gs]], writes=wbufs)

        def ln_tm(src, dst, rbufs, wbufs, gs, npart=128):
            k = ln_stats(src, rbufs, npart)
            ln_apply(k, src, dst, rbufs, wbufs, gs, npart)

        def ln_groups(s, gs, pre=None):
            ks = []
            for g in range(4):
                if pre is not None:
                    pre(g)
                ks.append(ln_stats(XT[s][:, g, :], [bXT[s][g]]))
                if g >= 1:
                    ln_apply(ks[g - 1], XT[s][:, g - 1, :], XT[s][:, g - 1, :], [bXT[s][g - 1]], [bXT[s][g - 1]], gs)
            ln_apply(ks[3], XT[s][:, 3, :], XT[s][:, 3, :], [bXT[s][3]], [bXT[s][3]], gs)

        def to_fm(s, xs, dstF, dst_bufs, col0):
            for g in range(4):
                for mq in range(2):
                    i, h = misc_bank()
                    for mm in range(4):
                        m = mq * 4 + mm
                        S.op("pe", lambda e, i=i, h=h, g=g, m=m, mm=mm: e.transpose(
                            out=PP[i][:, h * 512 + mm * 128:h * 512 + (mm + 1) * 128],
                            in_=XT[xs][:, g, m * 128:(m + 1) * 128], identity=IDT[:]),
                            reads=[bXT[xs][g], bCONST], writes=[bPB[i][h]], signal=(mm == 3))
                    S.op("act", lambda e, i=i, h=h, g=g, mq=mq: e.activation(
                        out=dstF[:, mq * 4:mq * 4 + 4, col0 + g * 128:col0 + (g + 1) * 128],
                        in_=bank_ap(i, h).rearrange("p (m c) -> p m c", c=128), func=AF.Copy),
                        reads=[bPB[i][h]], writes=dst_bufs)

        tcount = {"n": 0}
        slot_of = {}

        def cfc(l, base, idx):
            c = l * CF_N + base + idx
            return CF[:, c:c + 1]

        tiles = [(si, j) for si, (row0, ntile) in enumerate(SEQS) for j in range(ntile)]
        gidx = {t: i for i, t in enumerate(tiles)}

        def stage_a(si, j):
            row0, ntile = SEQS[si]
            r0 = row0 + j * T
            s = gidx[(si, j)] % 3
            first, last = (j == 0), (j == ntile - 1)
            for g in range(4):
                S.dma("sp", lambda e, g=g: e.dma_start(out=XT[s][:, g, :], in_=xin[r0 + g * 128:r0 + (g + 1) * 128, :]),
                      f"x{s}{g}", writes=[bXT[s][g]])
            if not first:
                S.dma("sp", lambda e: e.dma_start(out=XH[0:HALO, :], in_=xin[r0 - HALO:r0, :]), "xh", writes=bXHL)
            if not last:
                S.dma("sp", lambda e: e.dma_start(out=XH[HALO:2 * HALO, :], in_=xin[r0 + T:r0 + T + HALO, :]), "xh",
                      writes=bXHL)
            gs = load_gb(0)
            NP = 2 * HALO
            ln_tm(XH[0:NP, :], XH[0:NP, :], bXHL, bXHL, gs, npart=NP)
            ln_groups(s, gs)
            pendingH.append(1)

        def halo_t():
            if not pendingH:
                return
            pendingH.pop()
            NP = 2 * HALO
            i, h = misc_bank()
            for m in range(8):
                S.op("pe", lambda e, i=i, h=h, m=m: e.transpose(out=PP[i][:, h * 512 + m * NP:h * 512 + (m + 1) * NP],
                                                                  in_=XH[0:NP, m * 128:(m + 1) * 128],
                                                                  identity=IDT[0:NP, 0:NP]),
                     reads=bXHL + [bCONST], writes=[bPB[i][h]], signal=(m == 7))
            pv = PP[i][:, h * 512:h * 512 + 8 * NP].rearrange("p (m c) -> p m c", c=NP)
            S.op("act", lambda e: e.activation(out=HB[:], in_=pv, func=AF.Copy), reads=[bPB[i][h]], writes=[bHB])

        def stage_b(si, j):
            s = gidx[(si, j)] % 3
            sf = gidx[(si, j)] % 2
            halo_t()
            to_fm(s, s, XF[sf], [bXFc[sf]], HALO)
            S.op("pool", lambda e: e.tensor_copy(out=XF[sf][:, :, 0:HALO], in_=HB[:, :, 0:HALO]),
                 reads=[bHB], writes=[bXFh[sf]])
            S.op("pool", lambda e: e.tensor_copy(out=XF[sf][:, :, HALO + T:WIN], in_=HB[:, :, HALO:2 * HALO]),
                 reads=[bHB], writes=[bXFh[sf]])

        pendingB = []
        pendingT = []
        pendingH = []

        def layer(l, si, j):
            row0, ntile = SEQS[si]
            s = gidx[(si, j)] % 3
            sf = gidx[(si, j)] % 2
            first, last = (j == 0), (j == ntile - 1)
            xf, bxf, bxfh = XF[sf], bXFc[sf], bXFh[sf]
            GU = lambda m: R[:, m, :]
            YA = lambda m: R[:, 8 + m, :]
            YB = lambda m: R[:, 16 + m, :]
            VNH = lambda g, hh: R[:, 24 + 2 * g + hh, :]
            bGU, bYA, bYB = bR[0:8], bR[8:16], bR[16:24]
            bVN = lambda g: bR[24 + 2 * g:26 + 2 * g]
            LNB = 2 + 6 * l

            def fm_mm(slot, cgi, rhs_of_k, rbufs, halo):
                if halo:
                    i = next_pair()
                    for k in range(8):
                        S.op("pe", lambda e, k=k, i=i: e.matmul(PP[i][:, 0:512], lhsT=WR[slot][:, k, cgi * 128:(cgi + 1) * 128],
                                                                  rhs=xf[:, k, 0:512], start=(k == 0), stop=(k == 7)),
                             reads=[bWR[slot]] + rbufs, writes=[bPB[i][0]], signal=False)
                    for k in range(8):
                        S.op("pe", lambda e, k=k, i=i: e.matmul(PP[i][:, 512:WIN], lhsT=WR[slot][:, k, cgi * 128:(cgi + 1) * 128],
                                                                  rhs=xf[:, k, 512:WIN], start=(k == 0), stop=(k == 7)),
                             reads=[bWR[slot]] + rbufs, writes=[bPB[i][1]], signal=(k == 7))
                    return PP[i][:, 0:WIN], [bPB[i][0], bPB[i][1]]
                i, h = next_bank()
                for k in range(8):
                    S.op("pe", lambda e, k=k, i=i, h=h: e.matmul(bank_ap(i, h), lhsT=WR[slot][:, k, cgi * 128:(cgi + 1) * 128],
                                                                  rhs=rhs_of_k(k), start=(k == 0), stop=(k == 7)),
                         reads=[bWR[slot]] + rbufs, writes=[bPB[i][h]], signal=(k == 7))
                return bank_ap(i, h), [bPB[i][h]]

            def tm_mm(slot, lhs_of_k, lbufs, brow_r, col0, nk=8):
                outs = []
                for g in range(4):
                    i, h = next_bank()
                    S.op("pe", lambda e, i=i, h=h: e.matmul(bank_ap(i, h), lhsT=ONES[32 * brow_r:32 * brow_r + 1, :],
                                                             rhs=BROW[32 * brow_r:32 * brow_r + 1, l, col0:col0 + 512],
                                                             start=True, stop=False, skip_group_check=True),
                         reads=[bCONST], writes=[bPB[i][h]], signal=False)
                    for k in range(nk):
                        S.op("pe", lambda e, i=i, h=h, k=k, g=g: e.matmul(bank_ap(i, h), lhsT=lhs_of_k(k, g),
                                                                           rhs=WR[slot][:, k, :], start=False, stop=(k == nk - 1),
                                                                           skip_group_check=True),
                             reads=[bWR[slot]] + lbufs(k), writes=[bPB[i][h]], signal=(k == nk - 1))
                    outs.append((i, h))
                return outs

            tmpi = {"c": 0}

            def rot():
                tmpi["c"] ^= 1
                return tmpi["c"]

            WRF = lambda slot: WR[slot][:].rearrange("p k c -> p (k c)")
            acc_of = {}
            cstate = {}

            def blk_u(p, slot):
                for cgi, (seg, m) in enumerate(in_groups("u", p)):
                    ps, pbufs = fm_mm(slot, cgi, lambda k: xf[:, k, HALO:HALO + T], [bxf], False)
                    S.op("act", lambda e, ps=ps, m=m: e.activation(out=GU(m), in_=ps, func=AF.Gelu,
                                                                    bias=cfc(l, CF_BIN, SEG_U * 8 + m), scale=1.0),
                         reads=pbufs + [bCONST], writes=[bGU[m]])

            def blk_cc(p, slot):
                for cgi, (seg, m) in enumerate(in_groups("cc", p)):
                    ps, pbufs = fm_mm(slot, cgi, None, [bxf, bxfh], True)
                    if seg == SEG_GB:
                        q = rot()
                        S.op("act", lambda e, ps=ps, q=q, m=m: e.activation(out=SG[q][:], in_=ps, func=AF.Sigmoid,
                                                                             bias=cfc(l, CF_BIN, SEG_GB * 8 + m), scale=1.0),
                             reads=pbufs + [bCONST], writes=[bSG[q]])
                        cstate["sgq"] = q
                    else:
                        sgq = cstate["sgq"]
                        c4 = m % 4
                        S.op("dve", lambda e, ps=ps, c4=c4, m=m, sgq=sgq: e.scalar_tensor_tensor(
                            out=CCB[c4][:], in0=ps, scalar=cfc(l, CF_BIN, SEG_GA * 8 + m), in1=SG[sgq][:],
                            op0=ALU.add, op1=ALU.mult), reads=pbufs + [bSG[sgq], bCONST], writes=[bCCB[c4]])
                        if first:
                            S.op("dve", lambda e, c4=c4: e.memset(CCB[c4][:, 0:HALO], 0.0), writes=[bCCB[c4]])
                        if last:
                            S.op("dve", lambda e, c4=c4: e.memset(CCB[c4][:, HALO + T:WIN], 0.0), writes=[bCCB[c4]])

            def blk_dg(m, slot):
                c4 = m % 4
                i, h = next_bank()
                for jt in range(31):
                    S.op("pe", lambda e, i=i, h=h, jt=jt, c4=c4, slot=slot: e.matmul(
                        bank_ap(i, h), lhsT=WRF(slot)[:, jt * 128:(jt + 1) * 128], rhs=CCB[c4][:, jt:jt + T],
                        start=(jt == 0), stop=(jt == 30)), reads=[bWR[slot], bCCB[c4]], writes=[bPB[i][h]], signal=(jt == 30))
                S.op("act", lambda e, i=i, h=h, m=m: e.activation(out=CP[:, m, :], in_=bank_ap(i, h), func=AF.Identity,
                                                                   bias=cfc(l, CF_CCB, m), scale=1.0),
                     reads=[bPB[i][h], bCONST], writes=[bCP[m]])
                S.op("act", lambda e, m=m: e.activation(out=SQ[:, m, :], in_=CP[:, m, :], func=AF.Square),
                     reads=[bCP[m]], writes=[bSQ[m]])

            def blk_v(hh, slot):
                outs = tm_mm(slot, lambda k, g: xf[:, k, HALO + g * 128:HALO + (g + 1) * 128], lambda k: [bxf], 0, hh * 512)
                for g, (i, h) in enumerate(outs):
                    S.op("act", lambda e, i=i, h=h, g=g, hh=hh: e.activation(out=VNH(g, hh), in_=bank_ap(i, h), func=AF.Gelu),
                         reads=[bPB[i][h]], writes=[bVN(g)[hh]])

            def act_vln(g):
                if g == 0:
                    cstate["vgs"] = load_gb(LNB + 0)
                vfull = R[:, 24 + 2 * g:26 + 2 * g, :].rearrange("p a c -> p (a c)")
                ln_tm(vfull, vfull, bVN(g), bVN(g), cstate["vgs"])

            def act_sgu():
                for hd in range(8):
                    i, h = misc_bank()
                    for g in range(4):
                        S.op("pe", lambda e, i=i, h=h, g=g, hd=hd: e.matmul(
                            PP[i][:, h * 512 + g * 128:h * 512 + (g + 1) * 128], lhsT=ONES[0:1, :],
                            rhs=BROW[0:1, l, D + hd * 128:D + (hd + 1) * 128], start=(g == 0), stop=False, skip_group_check=True),
                            reads=[bCONST], writes=[bPB[i][h]], signal=False)
                    for g in range(4):
                        S.op("pe", lambda e, i=i, h=h, g=g, hd=hd: e.matmul(
                            PP[i][:, h * 512 + g * 128:h * 512 + (g + 1) * 128],
                            lhsT=R[:, 24 + 2 * g + hd // 4, (hd % 4) * 128:(hd % 4 + 1) * 128],
                            rhs=WST[:, l, hd * 128:(hd + 1) * 128], start=False, stop=(g == 3), skip_group_check=True),
                            reads=[bCONST] + bVN(g), writes=[bPB[i][h]], signal=(g == 3))
                    S.op("dve", lambda e, i=i, h=h, hd=hd: e.tensor_tensor(out=YA(hd), in0=bank_ap(i, h), in1=GU(hd), op=ALU.mult),
                         reads=[bPB[i][h], bGU[hd]], writes=[bYA[hd]])

            def blk_cb(groups, slot):
                for cgi, (seg, m) in enumerate(groups):
                    if seg == SEG_CG:
                        ps, pbufs = fm_mm(slot, cgi, None, [bxf, bxfh], True)
                        q = rot()
                        S.op("act", lambda e, ps=ps, q=q, m=m: e.activation(out=CG[q][:], in_=ps, func=AF.Identity,
                                                                             bias=cfc(l, CF_BIN, SEG_CG * 8 + m), scale=1.0),
                             reads=pbufs + [bCONST], writes=[bCG[q]])
                        cstate["cgq"] = q
                    elif seg == SEG_H:
                        cgq = cstate["cgq"]
                        ps, pbufs = fm_mm(slot, cgi, None, [bxf, bxfh], True)
                        q = rot()
                        S.op("dve", lambda e, ps=ps, q=q, m=m, cgq=cgq: e.scalar_tensor_tensor(
                            out=CGH[q][:], in0=ps, scalar=cfc(l, CF_BIN, SEG_H * 8 + m), in1=CG[cgq][:],
                            op0=ALU.add, op1=ALU.mult), reads=pbufs + [bCG[cgq], bCONST], writes=[bCGH[q]])
                        if first:
                            S.op("dve", lambda e, q=q: e.memset(CGH[q][:, 0:HALO], 0.0), writes=[bCGH[q]])
                        if last:
                            S.op("dve", lambda e, q=q: e.memset(CGH[q][:, HALO + T:WIN], 0.0), writes=[bCGH[q]])
                        a = rot()
                        S.op("dve", lambda e, q=q, a=a, m=m: e.tensor_scalar(
                            out=ACC[a][:], in0=CGH[q][:, HALO - 1:HALO - 1 + T], scalar1=cfc(l, CF_W3, m), scalar2=None,
                            op0=ALU.mult), reads=[bCGH[q], bCONST], writes=[bACC[a]])
                        for jt in (1, 2):
                            S.op("dve", lambda e, q=q, a=a, m=m, jt=jt: e.scalar_tensor_tensor(
                                out=ACC[a][:], in0=CGH[q][:, HALO - 1 + jt:HALO - 1 + jt + T],
                                scalar=cfc(l, CF_W3, jt * 8 + m), in1=ACC[a][:], op0=ALU.mult, op1=ALU.add),
                                reads=[bCGH[q], bACC[a], bCONST], writes=[bACC[a]])
                        acc_of[m] = a
                        S.pump(3)
                    else:
                        ps, pbufs = fm_mm(slot, cgi, lambda k: xf[:, k, HALO:HALO + T], [bxf], False)
                        a = acc_of[m]
                        S.op("dve", lambda e, ps=ps, a=a, m=m: e.scalar_tensor_tensor(
                            out=YB(m), in0=ps, scalar=cfc(l, CF_BIN, SEG_BG * 8 + m), in1=ACC[a][:],
                            op0=ALU.add, op1=ALU.mult), reads=pbufs + [bACC[a], bCONST], writes=[bYB[m]])

            def act_cnorm():
                i1, h1 = misc_bank()
                i2, h2 = misc_bank()
                for k in range(8):
                    S.op("pe", lambda e, k=k: e.matmul(bank_ap(i1, h1), lhsT=ONES[:, :], rhs=CP[:, k, :], start=(k == 0), stop=(k == 7)),
                         reads=[bCONST, bCP[k]], writes=[bPB[i1][h1]], signal=(k == 7))
                for k in range(8):
                    S.op("pe", lambda e, k=k: e.matmul(bank_ap(i2, h2), lhsT=ONES[:, :], rhs=SQ[:, k, :], start=(k == 0), stop=(k == 7)),
                         reads=[bCONST, bSQ[k]], writes=[bPB[i2][h2]], signal=(k == 7))
                S.op("act", lambda e: e.activation(out=MEAN[:], in_=bank_ap(i1, h1), func=AF.Identity, scale=1.0 / D),
                     reads=[bPB[i1][h1]], writes=bSTATL)
                S.op("act", lambda e: e.activation(out=E2[:], in_=bank_ap(i2, h2), func=AF.Identity, scale=1.0 / D),
                     reads=[bPB[i2][h2]], writes=bSTATL)
                S.op("dve", lambda e: e.scalar_tensor_tensor(out=RSTD[:], in0=MEAN[:], scalar=-1.0, in1=MEAN[:],
                                                             op0=ALU.mult, op1=ALU.mult), reads=bSTATL, writes=bSTATL)
                S.op("dve", lambda e: e.scalar_tensor_tensor(out=E2[:], in0=E2[:], scalar=EPS, in1=RSTD[:],
                                                             op0=ALU.add, op1=ALU.add), reads=bSTATL, writes=bSTATL)
                S.op("act", lambda e: e.activation(out=RSTD[:], in_=E2[:], func=AF.Sqrt), reads=bSTATL, writes=bSTATL)
                S.op("dve", lambda e: e.reciprocal(out=RSTD[:], in_=RSTD[:]), reads=bSTATL, writes=bSTATL)
                for m in range(8):
                    q = m % 2
                    S.defer("pool", lambda e, m=m, q=q: e.tensor_tensor(out=ZT[q][:], in0=CP[:, m, :], in1=MEAN[:], op=ALU.subtract),
                            reads=[bCP[m]] + bSTATL, writes=[bZT[q]])
                    S.defer("pool", lambda e, m=m, q=q: e.tensor_tensor(out=ZT[q][:], in0=ZT[q][:], in1=RSTD[:], op=ALU.mult),
                            reads=[bZT[q]] + bSTATL, writes=[bZT[q]])
                    S.defer("act", lambda e, m=m, q=q: e.activation(out=CP[:, m, :], in_=ZT[q][:], func=AF.Silu,
                                                                     bias=cfc(l, CF_CNB, m), scale=cfc(l, CF_CNG, m)),
                            reads=[bZT[q], bCONST], writes=[bCP[m]])

            Ysrc = [(YA, bYA), (YB, bYB), (lambda m: CP[:, m, :], bCP)]

            def blk_gate(p, slot):
                n, mb = p
                r = (mb * 3 + n) % 2
                for cgi, (seg, m) in enumerate(in_groups("gate", p)):
                    ps, pbufs = fm_mm(slot, cgi, lambda k: xf[:, k, HALO:HALO + T], [bxf], False)
                    S.op("act", lambda e, ps=ps, m=m, cgi=cgi, r=r, seg=seg: e.activation(
                        out=SGT[r][:, cgi, :], in_=ps, func=AF.Sigmoid, bias=cfc(l, CF_BIN, seg * 8 + m), scale=1.0),
                        reads=pbufs + [bCONST], writes=[bSGT[r]])

            def blk_br(p, slot):
                n, mb = p
                r = (mb * 3 + n) % 2
                Yn, bYn = Ysrc[n]
                for cgi in range(4):
                    m = mb * 4 + cgi
                    ps, pbufs = fm_mm(slot, cgi, lambda k, Yn=Yn: Yn(k), list(bYn), False)
                    if n == 0:
                        S.op("dve", lambda e, ps=ps, cgi=cgi, r=r: e.tensor_tensor(out=MG[:, cgi, :], in0=ps, in1=SGT[r][:, cgi, :], op=ALU.mult),
                             reads=pbufs + [bSGT[r]], writes=[bMG[cgi]])
                    else:
                        q = rot()
                        S.op("dve", lambda e, ps=ps, q=q, cgi=cgi, r=r: e.tensor_tensor(out=TM1[q][:], in0=ps, in1=SGT[r][:, cgi, :], op=ALU.mult),
                             reads=pbufs + [bSGT[r]], writes=[bTM1[q]])
                        if n == 1:
                            S.op("pool", lambda e, q=q, cgi=cgi: e.tensor_tensor(out=MG[:, cgi, :], in0=MG[:, cgi, :], in1=TM1[q][:], op=ALU.add),
                                 reads=[bMG[cgi], bTM1[q]], writes=[bMG[cgi]])
                        else:
                            S.op("pool", lambda e, q=q, m=m, cgi=cgi: e.tensor_tensor(out=SQ[:, m, :], in0=MG[:, cgi, :], in1=TM1[q][:], op=ALU.add),
                                 reads=[bMG[cgi], bTM1[q]], writes=[bSQ[m]])

            if pendingT and pendingT[0][0] == gidx[(si, j)]:
                while pendingT:
                    pendingT.pop(0)[1]()
            handlers = {"u": blk_u, "cc": blk_cc, "dg": blk_dg, "v": blk_v, "cb": blk_cb, "gate": blk_gate, "br": blk_br}
            def run_tails():
                while pendingT:
                    pendingT.pop(0)[1]()

            def act_stageb():
                while pendingB:
                    stage_b(*pendingB.pop(0))
            acts = {"vln0": lambda: act_vln(0), "vln1": lambda: act_vln(1), "vln2": lambda: act_vln(2),
                    "vln3": lambda: act_vln(3), "sgu": act_sgu, "cnorm": act_cnorm, "stageB": act_stageb,
                    "flush": lambda: S.pump(), "haloT": halo_t, "tail": run_tails}
            si_ = 0
            while SCHED[si_][1] != "out":
                t_, k_, p_ = SCHED[si_]
                if t_ == "blk":
                    handlers[k_](p_, next_block())
                else:
                    acts[k_]()
                si_ += 1
            MGB, bMGB = SQ, bSQ
            gs = load_gb(LNB + 2)
            wslots = [next_block(), next_block(prefetch=False)]

            def wout_group(g):
                banks = []
                for hh in range(2):
                    i, h = next_bank()
                    banks.append((i, h))
                    S.op("pe", lambda e, i=i, h=h, hh=hh: e.matmul(bank_ap(i, h), lhsT=ONES[32:33, :],
                                                                    rhs=BROW[32:33, l, hh * 512:(hh + 1) * 512],
                                                                    start=True, stop=False, skip_group_check=True),
                         reads=[bCONST], writes=[bPB[i][h]], signal=False)
                    for k in range(8):
                        S.op("pe", lambda e, i=i, h=h, k=k, g=g, hh=hh: e.matmul(
                            bank_ap(i, h), lhsT=MGB[:, k, g * 128:(g + 1) * 128], rhs=WR[wslots[hh]][:, k, :],
                            start=False, stop=(k == 7), skip_group_check=True),
                            reads=[bWR[wslots[hh]], bMGB[k]], writes=[bPB[i][h]], signal=(k == 7))
                for hh, (i, h) in enumerate(banks):
                    S.op("dve", lambda e, i=i, h=h, g=g, hh=hh: e.scalar_tensor_tensor(
                        out=XT[s][:, g, hh * 512:(hh + 1) * 512], in0=XT[s][:, g, hh * 512:(hh + 1) * 512], scalar=ALPHA,
                        in1=bank_ap(i, h), op0=ALU.mult, op1=ALU.add), reads=[bPB[i][h], bXT[s][g]], writes=[bXT[s][g]])
            ln_groups(s, gs, pre=wout_group)
            act_stageb()
            X1F, bX1F = CP, bCP
            to_fm(s, s, X1F, list(bX1F), 0)
            for mb in range(8):
                slot = next_block()
                for cgi in range(4):
                    c = mb * 4 + cgi
                    ps, pbufs = fm_mm(slot, cgi, lambda k: X1F[:, k, :], list(bX1F), False)
                    q = rot()
                    S.op("act", lambda e, ps=ps, q=q, c=c: e.activation(out=RR[q][:], in_=ps, func=AF.Relu,
                                                                         bias=cfc(l, CF_BFF1, c), scale=1.0),
                         reads=pbufs + [bCONST], writes=[bRR[q]])
                    S.op("pool", lambda e, q=q, c=c: e.tensor_tensor(out=R[:, c, :], in0=RR[q][:], in1=RR[q][:], op=ALU.mult),
                         reads=[bRR[q]], writes=[bR[c]])
            gs = load_gb(LNB + 4)
            ln_ks = []
            for hh in range(2):
                banks = [next_bank() for _ in range(4)]
                for kb in range(4):
                    slot = next_block()
                    for g in range(4):
                        i, h = banks[g]
                        if kb == 0:
                            S.op("pe", lambda e, i=i, h=h, hh=hh: e.matmul(bank_ap(i, h), lhsT=ONES[64:65, :],
                                                                            rhs=BROW[64:65, l, hh * 512:(hh + 1) * 512],
                                                                            start=True, stop=False, skip_group_check=True),
                                 reads=[bCONST], writes=[bPB[i][h]], signal=False)
                        for k in range(8):
                            kk = kb * 8 + k
                            lastmm = (kb == 3 and k == 7)
                            S.op("pe", lambda e, i=i, h=h, g=g, k=k, kk=kk, lastmm=lastmm, slot=slot: e.matmul(
                                bank_ap(i, h), lhsT=R[:, kk, g * 128:(g + 1) * 128], rhs=WR[slot][:, k, :],
                                start=False, stop=lastmm, skip_group_check=True),
                                reads=[bWR[slot], bR[kk]], writes=[bPB[i][h]], signal=(k == 7))
                        if kb == 3:
                            S.op("dve", lambda e, i=i, h=h, g=g, hh=hh: e.scalar_tensor_tensor(
                                out=XT[s][:, g, hh * 512:(hh + 1) * 512], in0=XT[s][:, g, hh * 512:(hh + 1) * 512], scalar=ALPHA,
                                in1=bank_ap(i, h), op0=ALU.mult, op1=ALU.add), reads=[bPB[i][h], bXT[s][g]], writes=[bXT[s][g]])
                            if hh == 1:
                                ln_ks.append(ln_stats(XT[s][:, g, :], [bXT[s][g]]))
                                if g >= 1:
                                    ln_apply(ln_ks[g - 1], XT[s][:, g - 1, :], XT[s][:, g - 1, :], [bXT[s][g - 1]], [bXT[s][g - 1]], gs)
            ln_apply(ln_ks[3], XT[s][:, 3, :], XT[s][:, 3, :], [bXT[s][3]], [bXT[s][3]], gs)
            if l == 0:
                def tail():
                    to_fm(s, s, XF[sf], [bXFc[sf]], HALO)
                    if not first:
                        sp_ = 1 - sf
                        S.op("pool", lambda e: e.tensor_copy(out=XF[sf][:, :, 0:HALO], in_=XF[sp_][:, :, T:T + HALO]),
                             reads=[bXFc[sp_]], writes=[bXFh[sf]])
                        S.op("pool", lambda e: e.tensor_copy(out=XF[sp_][:, :, HALO + T:WIN], in_=XF[sf][:, :, HALO:2 * HALO]),
                             reads=[bXFc[sf]], writes=[bXFh[sp_]])
                pendingT.append((gidx[(si, j)], tail))
            else:
                r0 = row0 + j * T
                for g in range(4):
                    S.dma("pool", lambda e, g=g: e.dma_start(out=yout[r0 + g * 128:r0 + (g + 1) * 128, :], in_=XT[s][:, g, :]),
                          f"o{s}{g}", reads=[bXT[s][g]])

        stage_a(*tiles[0])
        stage_b(*tiles[0])
        pump_cast(3)
        for n_, (kind, si, j) in enumerate(tile_layers):
            if kind == "L1":
                layer(0, si, j)
                nxt = gidx[(si, j)] + 1
                if nxt < len(tiles):
                    stage_a(*tiles[nxt])
                    pendingB.append(tiles[nxt])
                    if not (n_ + 1 < len(tile_layers) and tile_layers[n_ + 1][0] == "L2"):
                        stage_b(*pendingB.pop(0))
            else:
                layer(1, si, j)
        S.wait_all("sp", [f"o{s_}{g}" for s_ in range(3) for g in range(4)] + ["pe", "act", "dve", "pool"])

        engs = {"sp": "sync", "act": "scalar", "dve": "vector", "pool": "gpsimd", "pe": "tensor"}
        with nc.Block() as block:
            def make(key):
                def body(e):
                    for it in S.streams[key]:
                        if it[0] == "wait":
                            e.wait_ge(sems[it[1]], it[2])
                        else:
                            ins = it[1](e)
                            if it[3]:
                                ins.then_inc(sems[it[2]], it[3])
                return body
            for key, attr in engs.items():
                getattr(block, attr)(make(key))
    return nc


def pack_aux(ln_in_g, ln_in_b, b_in, sgu_ln_g, sgu_ln_b, sgu_w, sgu_b, sconv_w, cconv_w, cconv_b,
             cnorm_g, cnorm_b, b_out, ln1_g, ln1_b, b_ff1, b_ff2, ln2_g, ln2_b):
    f = lambda a: np.asarray(a, dtype=np.float32)
    cfm = np.empty((128, 2 * CF_N), np.float32)
    for l in range(2):
        o = l * CF_N
        cfm[:, o + CF_BIN:o + CF_BIN + 80] = f(b_in)[l].reshape(80, 128).T
        cfm[:, o + CF_BFF1:o + CF_BFF1 + 32] = f(b_ff1)[l].reshape(32, 128).T
        cfm[:, o + CF_CCB:o + CF_CCB + 8] = f(cconv_b)[l].reshape(8, 128).T
        cfm[:, o + CF_CNG:o + CF_CNG + 8] = f(cnorm_g)[l].reshape(8, 128).T
        cfm[:, o + CF_CNB:o + CF_CNB + 8] = f(cnorm_b)[l].reshape(8, 128).T
        cfm[:, o + CF_W3:o + CF_W3 + 24] = f(sconv_w)[l].reshape(3, 8, 128).transpose(2, 0, 1).reshape(128, 24)
    brow32 = np.empty((2, 4, D), np.float32)
    for l in range(2):
        brow32[l, 0] = f(b_in)[l, SEG_V * 1024:(SEG_V + 1) * 1024]
        brow32[l, 1] = f(b_out)[l]
        brow32[l, 2] = f(b_ff2)[l]
        brow32[l, 3] = f(sgu_b)[l].reshape(-1)
    lnp = np.empty((14, 128, D), np.float32)
    vecs = [f(ln_in_g), f(ln_in_b)]
    for l in range(2):
        vecs += [f(sgu_ln_g)[l], f(sgu_ln_b)[l], f(ln1_g)[l], f(ln1_b)[l], f(ln2_g)[l], f(ln2_b)[l]]
    for i, v in enumerate(vecs):
        lnp[i] = np.broadcast_to(v[None, :], (128, D))
    wst32 = np.ascontiguousarray(f(sgu_w).transpose(0, 3, 1, 2)).reshape(2, 128, 8 * 128)
    ident = np.eye(128, dtype=np.float32)

    return {"cfm": cfm, "brow32": brow32, "lnp": lnp, "wst32": wst32, "ident": ident}


_NC_CACHE = {}


def kernel(x_prompt, x_sample, ln_in_g, ln_in_b, w_in, b_in, sgu_ln_g, sgu_ln_b, sgu_w, sgu_b,
           sconv_w, cconv_w, cconv_b, cnorm_g, cnorm_b, w_branch, w_out, b_out,
           ln1_g, ln1_b, w_ff1, b_ff1, w_ff2, b_ff2, ln2_g, ln2_b):
    f = lambda a: np.asarray(a, dtype=np.float32)
    x_prompt, x_sample = f(x_prompt), f(x_sample)
    w_in, w_branch, w_out, w_ff1, w_ff2 = f(w_in), f(w_branch), f(w_out), f(w_ff1), f(w_ff2)
    wpk32 = pack_weights(w_in, w_branch, w_out, w_ff1, w_ff2, f(cconv_w))
    aux = pack_aux(ln_in_g, ln_in_b, b_in, sgu_ln_g, sgu_ln_b, sgu_w, sgu_b, sconv_w, cconv_w, cconv_b,
                   cnorm_g, cnorm_b, b_out, ln1_g, ln1_b, b_ff1, b_ff2, ln2_g, ln2_b)
    in_maps = []
    for c in range(NCORES):
        xin = np.concatenate([x_prompt[2 * c:2 * c + 2].reshape(-1, D), x_sample[4 * c:4 * c + 4].reshape(-1, D)], axis=0)
        in_maps.append(dict(aux, xin=np.ascontiguousarray(xin), wpk32=wpk32))
    if "nc" not in _NC_CACHE:
        _NC_CACHE["nc"] = build_program()
    res = run_bass_kernel_spmd(_NC_CACHE["nc"], in_maps, core_ids=list(range(NCORES)))
    y_prompt = np.empty((16, 4096, D), np.float32)
    y_sample = np.empty((32, 2048, D), np.float32)
    for c in range(NCORES):
        y = np.asarray(res.results[c]["yout"])
        y_prompt[2 * c:2 * c + 2] = y[:8192].reshape(2, 4096, D)
        y_sample[4 * c:4 * c + 4] = y[8192:].reshape(4, 2048, D)
    return (y_prompt, y_sample)
```

```python
import contextlib
import numpy as np
import concourse.bass as bass
import concourse.mybir as mybir
from concourse.bass_utils import run_bass_kernel_spmd

F32 = mybir.dt.float32
BF16 = mybir.dt.bfloat16
AF = mybir.ActivationFunctionType
ALU = mybir.AluOpType

D = 1024
NCORES = 8
T = 512
HALO = 15
WIN = T + 2 * HALO
NBLK = 52
NSLOT = 3
ALPHA = float(4 ** 0.25)
EPS = 1e-5
TOK_PER_CORE = 16384
SEQS = [(0, 8), (4096, 8), (8192, 4), (10240, 4), (12288, 4), (14336, 4)]

SEG_U, SEG_V, SEG_BG, SEG_CG, SEG_H, SEG_GA, SEG_GB, SEG_G0 = 0, 1, 2, 3, 4, 5, 6, 7

CF_BIN = 0
CF_BFF1 = 80
CF_CCB = 112
CF_CNG = 120
CF_CNB = 128
CF_W3 = 136
CF_N = 160


def layer_schedule():
    sch = []
    cb = []
    for m in range(8):
        cb += [(SEG_CG, m), (SEG_H, m), (SEG_BG, m)]
    B = lambda kind, p: sch.append(("blk", kind, p))
    A = lambda name: sch.append(("act", name, None))
    B("u", 0); B("u", 1); B("v", 0); B("v", 1); A("haloT"); A("tail")
    B("cc", 0); B("cc", 1); A("vln0"); B("dg", 0); B("dg", 1)
    B("cc", 2); A("vln1"); B("dg", 2); B("dg", 3)
    B("cc", 3); A("vln2"); B("dg", 4); B("dg", 5)
    B("cb", cb[0:4]); A("vln3"); B("dg", 6); B("dg", 7); A("cnorm")
    for i in range(1, 6):
        B("cb", cb[4 * i:4 * i + 4])
    A("flush"); A("sgu")
    for mb in range(2):
        for n in range(3):
            B("gate", (n, mb)); B("br", (n, mb))
    B("out", 0); B("out", 1); A("ln1")
    for mb in range(8):
        B("ff1", mb)
    for h in range(2):
        for kb in range(4):
            B("ff2", (h, kb))
    A("ln2")
    return sch


SCHED = layer_schedule()
PLAN = [(k, p) for (t, k, p) in SCHED if t == "blk"]
assert len(PLAN) == NBLK


def in_groups(kind, p):
    if kind == "u":
        return [(SEG_U, 4 * p + j) for j in range(4)]
    if kind == "v":
        return [(SEG_V, 4 * p + j) for j in range(4)]
    if kind == "cc":
        return [(SEG_GB, 2 * p), (SEG_GA, 2 * p), (SEG_GB, 2 * p + 1), (SEG_GA, 2 * p + 1)]
    if kind == "cb":
        return p
    if kind == "gate":
        n, mb = p
        return [(SEG_G0 + n, 4 * mb + j) for j in range(4)]
    return None


def _pack(W):
    return np.ascontiguousarray(W.reshape(8, 128, 512).transpose(1, 0, 2)).reshape(128, 4096)


def pack_weights(w_in, w_branch, w_out, w_ff1, w_ff2, cconv_w):
    out = np.empty((2, NBLK, 128, 4096), np.float32)
    ar = np.arange(128)
    for l in range(2):
        for b, (kind, p) in enumerate(PLAN):
            grp = in_groups(kind, p)
            if grp is not None:
                cols = np.concatenate([np.arange(sg * 1024 + m * 128, sg * 1024 + m * 128 + 128) for (sg, m) in grp])
                W = w_in[l][:, cols]
            elif kind == "dg":
                blk = np.zeros((128, 32, 128), np.float32)
                for j in range(31):
                    blk[ar, j, ar] = cconv_w[l, j, p * 128:(p + 1) * 128]
                out[l, b] = blk.reshape(128, 4096)
                continue
            elif kind == "br":
                n, mb = p
                W = w_branch[l, n][:, mb * 512:(mb + 1) * 512]
            elif kind == "out":
                W = w_out[l][:, p * 512:(p + 1) * 512]
            elif kind == "ff1":
                W = w_ff1[l][:, p * 512:(p + 1) * 512]
            else:
                h, kb = p
                W = w_ff2[l][kb * 1024:(kb + 1) * 1024, h * 512:(h + 1) * 512]
            out[l, b] = _pack(W)
    return out


class Buf:
    __slots__ = ("w", "r")

    def __init__(self):
        self.w = None
        self.r = {}


class Sched:
    ENG = ("sp", "act", "dve", "pool", "pe")

    def __init__(self):
        self.streams = {k: [] for k in self.ENG}
        self.cnt = {}
        self.seen = {k: {} for k in self.ENG}
        self.lazy = []

    def _waits(self, eng, reads, writes):
        deps = {}

        def add(tok, raw):
            k, v = tok
            if k == eng and eng == "pe":
                return
            if k not in self.ENG:
                v = self.cnt[k]
            if deps.get(k, 0) < v:
                deps[k] = v
        for b in reads:
            if b.w is not None:
                add(b.w, True)
        for b in writes:
            if b.w is not None:
                add(b.w, False)
            for k, v in b.r.items():
                add((k, v), False)
        seen = self.seen[eng]
        for k, v in deps.items():
            if seen.get(k, 0) < v:
                self.streams[eng].append(("wait", k, v))
                seen[k] = v

    def _mark(self, tok, reads, writes):
        k, v = tok
        for b in reads:
            if b.r.get(k, 0) < v:
                b.r[k] = v
        for b in writes:
            b.w = tok
            b.r = {}

    def op(self, eng, fn, reads=(), writes=(), signal=True):
        self._waits(eng, reads, writes)
        c = self.cnt.get(eng, 0)
        if signal:
            c += 1
            self.cnt[eng] = c
            self.streams[eng].append(("op", fn, eng, 1))
            tok = (eng, c)
        else:
            self.streams[eng].append(("op", fn, None, 0))
            tok = (eng, c + 1)
        self._mark(tok, reads, writes)

    def dma(self, eng, fn, dsem, reads=(), writes=()):
        self._waits(eng, reads, writes)
        c = self.cnt.get(dsem, 0) + 16
        self.cnt[dsem] = c
        self.streams[eng].append(("op", fn, dsem, 16))
        self._mark((dsem, c), reads, writes)

    def defer(self, eng, fn, reads=(), writes=()):
        self.lazy.append((eng, fn, list(reads), list(writes)))

    def pump(self, n=None):
        while self.lazy and (n is None or n > 0):
            eng, fn, r, w = self.lazy.pop(0)
            self.op(eng, fn, reads=r, writes=w)
            if n is not None:
                n -= 1

    def wait_all(self, eng, keys):
        for k in keys:
            v = self.cnt.get(k, 0)
            if v and self.seen[eng].get(k, 0) < v:
                self.streams[eng].append(("wait", k, v))
                self.seen[eng][k] = v


def build_program(SEQS=SEQS):
    TOK_PER_CORE = sum(n for _, n in SEQS) * T
    nc = bass.Bass("TRN2", target_bir_lowering=False)
    xin = nc.dram_tensor("xin", [TOK_PER_CORE, D], F32, kind="ExternalInput").ap()
    wpk32 = nc.dram_tensor("wpk32", [2, NBLK, 128, 4096], F32, kind="ExternalInput").ap()
    cfm = nc.dram_tensor("cfm", [128, 2 * CF_N], F32, kind="ExternalInput").ap()
    brow32 = nc.dram_tensor("brow32", [2, 4, D], F32, kind="ExternalInput").ap()
    lnp = nc.dram_tensor("lnp", [14, 128, D], F32, kind="ExternalInput").ap()
    wst32 = nc.dram_tensor("wst32", [2, 128, 8 * 128], F32, kind="ExternalInput").ap()
    ident = nc.dram_tensor("ident", [128, 128], F32, kind="ExternalInput").ap()
    yout = nc.dram_tensor("yout", [TOK_PER_CORE, D], F32, kind="ExternalOutput").ap()
    wpk = nc.dram_tensor("wpk", [2, NBLK, 128, 4096], BF16, kind="Internal").ap()

    S = Sched()
    es = contextlib.ExitStack()

    def sb(name, shape, dt):
        return es.enter_context(nc.sbuf_tensor(name, shape, dt))

    with es:
        XT = [sb(f"xt{i}", [128, 4, D], F32) for i in range(3)]
        XF = [sb(f"xf{i}", [128, 8, WIN], BF16) for i in range(2)]
        WR = [sb(f"wr{i}", [128, 8, 512], BF16) for i in range(NSLOT)]
        R = sb("rbig", [128, 32, 512], BF16)
        CP = sb("cp", [128, 8, 512], BF16)
        SQ = sb("sq", [128, 8, 512], BF16)
        SGT = [sb(f"sgt{i}", [128, 4, 512], BF16) for i in range(2)]
        MG = sb("mg", [128, 4, 512], F32)
        CG = [sb(f"cg{i}", [128, WIN], F32) for i in range(2)]
        CGH = [sb(f"cgh{i}", [128, WIN], F32) for i in range(2)]
        SG = CG
        CCB = [sb(f"ccb{i}", [128, WIN], BF16) for i in range(4)]
        ACC = [sb(f"acc{i}", [128, 512], F32) for i in range(2)]
        TM1 = [sb(f"tm1{i}", [128, 512], F32) for i in range(2)]
        RR = [sb(f"rr{i}", [128, 512], F32) for i in range(2)]
        ZT = RR
        MEAN = TM1[0]
        E2 = sb("e2", [128, 512], F32)
        RSTD = TM1[1]
        NEGH = sb("negh", [128, 1], F32)
        GBT = [[sb(f"gb{i}{j}", [128, D], F32) for j in range(2)] for i in range(1)]
        LNT = sb("lnt", [128, D], F32)
        HB = sb("hb", [128, 8, 2 * HALO], BF16)
        XH = MG[:, 0:2, :].rearrange("p a c -> p (a c)")
        CF = sb("cf", [128, 2 * CF_N], F32)
        IDT = sb("idt", [128, 128], F32)
        ONES = sb("ones", [128, 128], BF16)
        BROW = sb("brow", [128, 2, 2 * D], BF16)
        WST = sb("wst", [128, 2, 8 * 128], BF16)
        NLN = 8
        ST = [sb(f"st{i}", [128, 12], F32) for i in range(NLN)]
        MV = [sb(f"mv{i}", [128, 2], F32) for i in range(NLN)]
        VE = [sb(f"ve{i}", [128, 1], F32) for i in range(NLN)]
        RS = [sb(f"rs{i}", [128, 1], F32) for i in range(NLN)]
        EPST = sb("epst", [128, 1], F32)
        PP = [es.enter_context(nc.psum_tensor(f"pp{i}", [128, 1024], F32)) for i in range(4)]

        keys = ["pe", "act", "dve", "pool", "misc", "xh", "gb0", "gb1"] + [f"w{i}" for i in range(NSLOT)] + \
               [f"c{l}_{q}" for l in range(2) for q in range(NBLK // 4)] + \
               [f"{a}{s_}{g}" for a in "xo" for s_ in range(3) for g in range(4)]
        sems = {k: es.enter_context(nc.semaphore(k)) for k in keys}

        bXT = [[Buf() for _ in range(4)] for _ in range(3)]
        bXFc = [Buf() for _ in range(2)]
        bXFh = [Buf() for _ in range(2)]
        bWR = [Buf() for _ in range(NSLOT)]
        bR = [Buf() for _ in range(32)]
        bCP = [Buf() for _ in range(8)]
        bSQ = [Buf() for _ in range(8)]
        bSGT = [Buf() for _ in range(2)]
        bMG = [Buf() for _ in range(4)]
        bXHL = [bMG[0], bMG[1]]
        bCG = [Buf() for _ in range(2)]
        bCGH = [Buf() for _ in range(2)]
        bSG = bCG
        bCCB = [Buf() for _ in range(4)]
        bACC = [Buf() for _ in range(2)]
        bTM1 = [Buf() for _ in range(2)]
        bRR = [Buf() for _ in range(2)]
        bZT = bRR
        bE2 = Buf()
        bSTATL = [bTM1[0], bTM1[1], bE2]
        bGB = [Buf() for _ in range(2)]
        bLNT = Buf()
        bHB = Buf()
        bXH = None
        bCONST = Buf()
        bLN = [Buf() for _ in range(8)]
        bPB = [[Buf(), Buf()] for _ in range(4)]
        bWPK = [[Buf() for _ in range(NBLK // 4)] for _ in range(2)]

        st = {"pp": 0, "misc": 0, "ln": 0, "gb": 0, "tmp": 0, "gbcur": -1}

        def next_bank():
            p = st["pp"]
            st["pp"] = (p + 1) % 8
            return p // 2, p % 2

        def next_pair():
            p = st["pp"]
            if p % 2:
                p = (p + 1) % 8
            st["pp"] = (p + 2) % 8
            return p // 2

        def misc_bank():
            h = st["misc"]
            st["misc"] = (h + 1) % 4
            return 2 + h // 2, h % 2

        def bank_ap(i, h, n=512):
            return PP[i][:, h * 512:h * 512 + n]

        castq = [(l, q) for l in range(2) for q in range(NBLK // 4)]

        def pump_cast(n=None, upto=None):
            while castq and (n is None or n > 0):
                if upto is not None and castq[0] > upto:
                    break
                l, q = castq.pop(0)
                S.dma("pool", lambda e, l=l, q=q: e.dma_start(
                    out=wpk[l, 4 * q:4 * q + 4].rearrange("b p c -> (b p) c").rearrange("(a r) c -> a (r c)", a=16),
                    in_=wpk32[l, 4 * q:4 * q + 4].rearrange("b p c -> (b p) c").rearrange("(a r) c -> a (r c)", a=16)),
                    f"c{l}_{q}", writes=[bWPK[l][q]])
                if n is not None:
                    n -= 1
        S.dma("sp", lambda e: e.dma_start(out=CF[:], in_=cfm[:, :]), "misc", writes=[bCONST])
        S.dma("sp", lambda e: e.dma_start(out=IDT[:], in_=ident[:, :]), "misc", writes=[bCONST])
        S.wait_all("sp", ["misc"])
        for l in range(2):
            for r in range(3):
                S.dma("sp", lambda e, l=l, r=r: e.dma_start(out=XH[32 * r:32 * r + 1, :], in_=brow32[l, r:r + 1, :]),
                      "misc", writes=bXHL)
            S.wait_all("sp", ["misc"])
            for r in range(3):
                S.op("dve", lambda e, l=l, r=r: e.tensor_copy(out=BROW[32 * r:32 * r + 1, l, 0:D],
                                                              in_=XH[32 * r:32 * r + 1, :]),
                     reads=bXHL, writes=[bCONST])
            S.dma("sp", lambda e, l=l: e.dma_start(out=XH[0:1, :], in_=brow32[l, 3:4, :]), "misc", writes=bXHL)
            S.wait_all("sp", ["misc"])
            S.op("dve", lambda e, l=l: e.tensor_copy(out=BROW[0:1, l, D:2 * D], in_=XH[0:1, :]),
                 reads=bXHL, writes=[bCONST])
            S.dma("sp", lambda e, l=l: e.dma_start(out=LNT[:], in_=wst32[l]), "misc", writes=[bLNT])
            S.wait_all("sp", ["misc"])
            S.op("dve", lambda e, l=l: e.tensor_copy(out=WST[:, l, :], in_=LNT[:]), reads=[bLNT], writes=[bCONST])
        S.op("dve", lambda e: e.memset(ONES[:], 1.0), writes=[bCONST])
        S.op("dve", lambda e: e.memset(NEGH[:], -0.5), writes=[bCONST])
        S.op("dve", lambda e: e.memset(EPST[:], EPS), writes=[bCONST])
        S.op("dve", lambda e: e.memset(XH[:], 0.0), reads=[], writes=bXHL)

        tile_layers = []
        for si, (row0, ntile) in enumerate(SEQS):
            for j in range(ntile):
                tile_layers.append(("L1", si, j))
                if j >= 1:
                    tile_layers.append(("L2", si, j - 1))
            tile_layers.append(("L2", si, ntile - 1))
        gblocks = []
        for (kind, si, j) in tile_layers:
            l = 0 if kind == "L1" else 1
            for b in range(NBLK):
                gblocks.append((l, b))
        wst_ = {"issued": 0, "used": 0}

        def issue_loads(upto):
            while wst_["issued"] < min(upto, len(gblocks)):
                i = wst_["issued"]
                l, b = gblocks[i]
                slot = i % NSLOT
                pump_cast(upto=(l, b // 4))
                S.dma("sp", lambda e, l=l, b=b, slot=slot: e.dma_start(
                    out=WR[slot][:].rearrange("p k c -> p (k c)"), in_=wpk[l, b]),
                    f"w{slot}", reads=[bWPK[l][b // 4]], writes=[bWR[slot]])
                wst_["issued"] += 1

        def next_block(prefetch=True):
            i = wst_["used"]
            issue_loads(i + NSLOT if prefetch else i + 1)
            if castq and i % (2 if castq[0][0] == 0 else 4) == 0:
                pump_cast(1)
            wst_["used"] += 1
            return i % NSLOT

        def load_gb(idx):
            s = 0
            st["gbcur"] = idx
            S.dma("pool", lambda e: e.dma_start(out=GBT[s][0][:], in_=lnp[idx]), f"gb{s}", writes=[bGB[s]])
            S.dma("pool", lambda e: e.dma_start(out=GBT[s][1][:], in_=lnp[idx + 1]), f"gb{s}", writes=[bGB[s]])
            return s

        def ln_stats(src, rbufs, npart=128):
            k = st["ln"]
            st["ln"] = (k + 1) % NLN
            P = slice(0, npart)
            S.op("dve", lambda e: e.bn_stats(out=ST[k][P, 0:6], in_=src[:, 0:512]), reads=rbufs, writes=[bLN[k]])
            S.op("dve", lambda e: e.bn_stats(out=ST[k][P, 6:12], in_=src[:, 512:1024]), reads=rbufs, writes=[bLN[k]])
            S.op("dve", lambda e: e.bn_aggr(out=MV[k][P, :], in_=ST[k][P, :]), reads=[bLN[k]], writes=[bLN[k]])
            S.op("pool", lambda e: e.tensor_tensor(out=VE[k][P, :], in0=MV[k][P, 1:2], in1=EPST[P, :], op=ALU.add),
                 reads=[bLN[k], bCONST], writes=[bLN[k]])
            S.op("pool", lambda e: e.tensor_tensor(out=RS[k][P, :], in0=VE[k][P, :], in1=NEGH[P, 0:1], op=ALU.pow),
                 reads=[bLN[k], bCONST], writes=[bLN[k]])
            return k

        def ln_apply(k, src, dst, rbufs, wbufs, gs, npart=128):
            P = slice(0, npart)
            S.op("dve", lambda e: e.scalar_tensor_tensor(out=LNT[P, :], in0=src, scalar=MV[k][P, 0:1],
                                                         in1=GBT[gs][0][P, :], op0=ALU.subtract, op1=ALU.mult),
                 reads=list(rbufs) + [bLN[k], bGB[gs]], writes=[bLNT])
            S.op("dve", lambda e: e.scalar_tensor_tensor(out=dst, in0=LNT[P, :], scalar=RS[k][P, 0:1],
                                                         in1=GBT[gs][1][P, :], op0=ALU.mult, op1=ALU.add),
                 reads=[bLNT, bLN[k], bGB[gs]], writes=wbufs)

        def ln_tm(src, dst, rbufs, wbufs, gs, npart=128):
            k = ln_stats(src, rbufs, npart)
            ln_apply(k, src, dst, rbufs, wbufs, gs, npart)

        def ln_groups(s, gs, pre=None):
            ks = []
            for g in range(4):
                if pre is not None:
                    pre(g)
                ks.append(ln_stats(XT[s][:, g, :], [bXT[s][g]]))
                if g >= 1:
                    ln_apply(ks[g - 1], XT[s][:, g - 1, :], XT[s][:, g - 1, :], [bXT[s][g - 1]], [bXT[s][g - 1]], gs)
            ln_apply(ks[3], XT[s][:, 3, :], XT[s][:, 3, :], [bXT[s][3]], [bXT[s][3]], gs)

        def to_fm(s, xs, dstF, dst_bufs, col0):
            for g in range(4):
                for mq in range(2):
                    i, h = misc_bank()
                    for mm in range(4):
                        m = mq * 4 + mm
                        S.op("pe", lambda e, i=i, h=h, g=g, m=m, mm=mm: e.transpose(
                            out=PP[i][:, h * 512 + mm * 128:h * 512 + (mm + 1) * 128],
                            in_=XT[xs][:, g, m * 128:(m + 1) * 128], identity=IDT[:]),
                            reads=[bXT[xs][g], bCONST], writes=[bPB[i][h]], signal=(mm == 3))
                    S.op("act", lambda e, i=i, h=h, g=g, mq=mq: e.activation(
                        out=dstF[:, mq * 4:mq * 4 + 4, col0 + g * 128:col0 + (g + 1) * 128],
                        in_=bank_ap(i, h).rearrange("p (m c) -> p m c", c=128), func=AF.Copy),
                        reads=[bPB[i][h]], writes=dst_bufs)

        tcount = {"n": 0}
        slot_of = {}

        def cfc(l, base, idx):
            c = l * CF_N + base + idx
            return CF[:, c:c + 1]

        tiles = [(si, j) for si, (row0, ntile) in enumerate(SEQS) for j in range(ntile)]
        gidx = {t: i for i, t in enumerate(tiles)}

        def stage_a(si, j):
            row0, ntile = SEQS[si]
            r0 = row0 + j * T
            s = gidx[(si, j)] % 3
            first, last = (j == 0), (j == ntile - 1)
            for g in range(4):
                S.dma("sp", lambda e, g=g: e.dma_start(out=XT[s][:, g, :], in_=xin[r0 + g * 128:r0 + (g + 1) * 128, :]),
                      f"x{s}{g}", writes=[bXT[s][g]])
            if not first:
                S.dma("sp", lambda e: e.dma_start(out=XH[0:HALO, :], in_=xin[r0 - HALO:r0, :]), "xh", writes=bXHL)
            if not last:
                S.dma("sp", lambda e: e.dma_start(out=XH[HALO:2 * HALO, :], in_=xin[r0 + T:r0 + T + HALO, :]), "xh",
                      writes=bXHL)
            gs = load_gb(0)
            NP = 2 * HALO
            ln_tm(XH[0:NP, :], XH[0:NP, :], bXHL, bXHL, gs, npart=NP)
            ln_groups(s, gs)
            pendingH.append(1)

        def halo_t():
            if not pendingH:
                return
            pendingH.pop()
            NP = 2 * HALO
            i, h = misc_bank()
            for m in range(8):
                S.op("pe", lambda e, i=i, h=h, m=m: e.transpose(out=PP[i][:, h * 512 + m * NP:h * 512 + (m + 1) * NP],
                                                                  in_=XH[0:NP, m * 128:(m + 1) * 128],
                                                                  identity=IDT[0:NP, 0:NP]),
                     reads=bXHL + [bCONST], writes=[bPB[i][h]], signal=(m == 7))
            pv = PP[i][:, h * 512:h * 512 + 8 * NP].rearrange("p (m c) -> p m c", c=NP)
            S.op("act", lambda e: e.activation(out=HB[:], in_=pv, func=AF.Copy), reads=[bPB[i][h]], writes=[bHB])

        def stage_b(si, j):
            s = gidx[(si, j)] % 3
            sf = gidx[(si, j)] % 2
            halo_t()
            to_fm(s, s, XF[sf], [bXFc[sf]], HALO)
            S.op("pool", lambda e: e.tensor_copy(out=XF[sf][:, :, 0:HALO], in_=HB[:, :, 0:HALO]),
                 reads=[bHB], writes=[bXFh[sf]])
            S.op("pool", lambda e: e.tensor_copy(out=XF[sf][:, :, HALO + T:WIN], in_=HB[:, :, HALO:2 * HALO]),
                 reads=[bHB], writes=[bXFh[sf]])

        pendingB = []
        pendingT = []
        pendingH = []

        def layer(l, si, j):
            row0, ntile = SEQS[si]
            s = gidx[(si, j)] % 3
            sf = gidx[(si, j)] % 2
            first, last = (j == 0), (j == ntile - 1)
            xf, bxf, bxfh = XF[sf], bXFc[sf], bXFh[sf]
            GU = lambda m: R[:, m, :]
            YA = lambda m: R[:, 8 + m, :]
            YB = lambda m: R[:, 16 + m, :]
            VNH = lambda g, hh: R[:, 24 + 2 * g + hh, :]
            bGU, bYA, bYB = bR[0:8], bR[8:16], bR[16:24]
            bVN = lambda g: bR[24 + 2 * g:26 + 2 * g]
            LNB = 2 + 6 * l

            def fm_mm(slot, cgi, rhs_of_k, rbufs, halo):
                if halo:
                    i = next_pair()
                    for k in range(8):
                        S.op("pe", lambda e, k=k, i=i: e.matmul(PP[i][:, 0:512], lhsT=WR[slot][:, k, cgi * 128:(cgi + 1) * 128],
                                                                  rhs=xf[:, k, 0:512], start=(k == 0), stop=(k == 7)),
                             reads=[bWR[slot]] + rbufs, writes=[bPB[i][0]], signal=False)
                    for k in range(8):
                        S.op("pe", lambda e, k=k, i=i: e.matmul(PP[i][:, 512:WIN], lhsT=WR[slot][:, k, cgi * 128:(cgi + 1) * 128],
                                                                  rhs=xf[:, k, 512:WIN], start=(k == 0), stop=(k == 7)),
                             reads=[bWR[slot]] + rbufs, writes=[bPB[i][1]], signal=(k == 7))
                    return PP[i][:, 0:WIN], [bPB[i][0], bPB[i][1]]
                i, h = next_bank()
                for k in range(8):
                    S.op("pe", lambda e, k=k, i=i, h=h: e.matmul(bank_ap(i, h), lhsT=WR[slot][:, k, cgi * 128:(cgi + 1) * 128],
                                                                  rhs=rhs_of_k(k), start=(k == 0), stop=(k == 7)),
                         reads=[bWR[slot]] + rbufs, writes=[bPB[i][h]], signal=(k == 7))
                return bank_ap(i, h), [bPB[i][h]]

            def tm_mm(slot, lhs_of_k, lbufs, brow_r, col0, nk=8):
                outs = []
                for g in range(4):
                    i, h = next_bank()
                    S.op("pe", lambda e, i=i, h=h: e.matmul(bank_ap(i, h), lhsT=ONES[32 * brow_r:32 * brow_r + 1, :],
                                                             rhs=BROW[32 * brow_r:32 * brow_r + 1, l, col0:col0 + 512],
                                                             start=True, stop=False, skip_group_check=True),
                         reads=[bCONST], writes=[bPB[i][h]], signal=False)
                    for k in range(nk):
                        S.op("pe", lambda e, i=i, h=h, k=k, g=g: e.matmul(bank_ap(i, h), lhsT=lhs_of_k(k, g),
                                                                           rhs=WR[slot][:, k, :], start=False, stop=(k == nk - 1),
                                                                           skip_group_check=True),
                             reads=[bWR[slot]] + lbufs(k), writes=[bPB[i][h]], signal=(k == nk - 1))
                    outs.append((i, h))
                return outs

            tmpi = {"c": 0}

            def rot():
                tmpi["c"] ^= 1
                return tmpi["c"]

            WRF = lambda slot: WR[slot][:].rearrange("p k c -> p (k c)")
            acc_of = {}
            cstate = {}

            def blk_u(p, slot):
                for cgi, (seg, m) in enumerate(in_groups("u", p)):
                    ps, pbufs = fm_mm(slot, cgi, lambda k: xf[:, k, HALO:HALO + T], [bxf], False)
                    S.op("act", lambda e, ps=ps, m=m: e.activation(out=GU(m), in_=ps, func=AF.Gelu,
                                                                    bias=cfc(l, CF_BIN, SEG_U * 8 + m), scale=1.0),
                         reads=pbufs + [bCONST], writes=[bGU[m]])

            def blk_cc(p, slot):
                for cgi, (seg, m) in enumerate(in_groups("cc", p)):
                    ps, pbufs = fm_mm(slot, cgi, None, [bxf, bxfh], True)
                    if seg == SEG_GB:
                        q = rot()
                        S.op("act", lambda e, ps=ps, q=q, m=m: e.activation(out=SG[q][:], in_=ps, func=AF.Sigmoid,
                                                                             bias=cfc(l, CF_BIN, SEG_GB * 8 + m), scale=1.0),
                             reads=pbufs + [bCONST], writes=[bSG[q]])
                        cstate["sgq"] = q
                    else:
                        sgq = cstate["sgq"]
                        c4 = m % 4
                        S.op("dve", lambda e, ps=ps, c4=c4, m=m, sgq=sgq: e.scalar_tensor_tensor(
                            out=CCB[c4][:], in0=ps, scalar=cfc(l, CF_BIN, SEG_GA * 8 + m), in1=SG[sgq][:],
                            op0=ALU.add, op1=ALU.mult), reads=pbufs + [bSG[sgq], bCONST], writes=[bCCB[c4]])
                        if first:
                            S.op("dve", lambda e, c4=c4: e.memset(CCB[c4][:, 0:HALO], 0.0), writes=[bCCB[c4]])
                        if last:
                            S.op("dve", lambda e, c4=c4: e.memset(CCB[c4][:, HALO + T:WIN], 0.0), writes=[bCCB[c4]])

            def blk_dg(m, slot):
                c4 = m % 4
                i, h = next_bank()
                for jt in range(31):
                    S.op("pe", lambda e, i=i, h=h, jt=jt, c4=c4, slot=slot: e.matmul(
                        bank_ap(i, h), lhsT=WRF(slot)[:, jt * 128:(jt + 1) * 128], rhs=CCB[c4][:, jt:jt + T],
                        start=(jt == 0), stop=(jt == 30)), reads=[bWR[slot], bCCB[c4]], writes=[bPB[i][h]], signal=(jt == 30))
                S.op("act", lambda e, i=i, h=h, m=m: e.activation(out=CP[:, m, :], in_=bank_ap(i, h), func=AF.Identity,
                                                                   bias=cfc(l, CF_CCB, m), scale=1.0),
                     reads=[bPB[i][h], bCONST], writes=[bCP[m]])
                S.op("act", lambda e, m=m: e.activation(out=SQ[:, m, :], in_=CP[:, m, :], func=AF.Square),
                     reads=[bCP[m]], writes=[bSQ[m]])

            def blk_v(hh, slot):
                outs = tm_mm(slot, lambda k, g: xf[:, k, HALO + g * 128:HALO + (g + 1) * 128], lambda k: [bxf], 0, hh * 512)
                for g, (i, h) in enumerate(outs):
                    S.op("act", lambda e, i=i, h=h, g=g, hh=hh: e.activation(out=VNH(g, hh), in_=bank_ap(i, h), func=AF.Gelu),
                         reads=[bPB[i][h]], writes=[bVN(g)[hh]])

            def act_vln(g):
                if g == 0:
                    cstate["vgs"] = load_gb(LNB + 0)
                vfull = R[:, 24 + 2 * g:26 + 2 * g, :].rearrange("p a c -> p (a c)")
                ln_tm(vfull, vfull, bVN(g), bVN(g), cstate["vgs"])

            def act_sgu():
                for hd in range(8):
                    i, h = misc_bank()
                    for g in range(4):
                        S.op("pe", lambda e, i=i, h=h, g=g, hd=hd: e.matmul(
                            PP[i][:, h * 512 + g * 128:h * 512 + (g + 1) * 128], lhsT=ONES[0:1, :],
                            rhs=BROW[0:1, l, D + hd * 128:D + (hd + 1) * 128], start=(g == 0), stop=False, skip_group_check=True),
                            reads=[bCONST], writes=[bPB[i][h]], signal=False)
                    for g in range(4):
                        S.op("pe", lambda e, i=i, h=h, g=g, hd=hd: e.matmul(
                            PP[i][:, h * 512 + g * 128:h * 512 + (g + 1) * 128],
                            lhsT=R[:, 24 + 2 * g + hd // 4, (hd % 4) * 128:(hd % 4 + 1) * 128],
                            rhs=WST[:, l, hd * 128:(hd + 1) * 128], start=False, stop=(g == 3), skip_group_check=True),
                            reads=[bCONST] + bVN(g), writes=[bPB[i][h]], signal=(g == 3))
                    S.op("dve", lambda e, i=i, h=h, hd=hd: e.tensor_tensor(out=YA(hd), in0=bank_ap(i, h), in1=GU(hd), op=ALU.mult),
                         reads=[bPB[i][h], bGU[hd]], writes=[bYA[hd]])

            def blk_cb(groups, slot):
                for cgi, (seg, m) in enumerate(groups):
                    if seg == SEG_CG:
                        ps, pbufs = fm_mm(slot, cgi, None, [bxf, bxfh], True)
                        q = rot()
                        S.op("act", lambda e, ps=ps, q=q, m=m: e.activation(out=CG[q][:], in_=ps, func=AF.Identity,
                                                                             bias=cfc(l, CF_BIN, SEG_CG * 8 + m), scale=1.0),
                             reads=pbufs + [bCONST], writes=[bCG[q]])
                        cstate["cgq"] = q
                    elif seg == SEG_H:
                        cgq = cstate["cgq"]
                        ps, pbufs = fm_mm(slot, cgi, None, [bxf, bxfh], True)
                        q = rot()
                        S.op("dve", lambda e, ps=ps, q=q, m=m, cgq=cgq: e.scalar_tensor_tensor(
                            out=CGH[q][:], in0=ps, scalar=cfc(l, CF_BIN, SEG_H * 8 + m), in1=CG[cgq][:],
                            op0=ALU.add, op1=ALU.mult), reads=pbufs + [bCG[cgq], bCONST], writes=[bCGH[q]])
                        if first:
                            S.op("dve", lambda e, q=q: e.memset(CGH[q][:, 0:HALO], 0.0), writes=[bCGH[q]])
                        if last:
                            S.op("dve", lambda e, q=q: e.memset(CGH[q][:, HALO + T:WIN], 0.0), writes=[bCGH[q]])
                        a = rot()
                        S.op("dve", lambda e, q=q, a=a, m=m: e.tensor_scalar(
                            out=ACC[a][:], in0=CGH[q][:, HALO - 1:HALO - 1 + T], scalar1=cfc(l, CF_W3, m), scalar2=None,
                            op0=ALU.mult), reads=[bCGH[q], bCONST], writes=[bACC[a]])
                        for jt in (1, 2):
                            S.op("dve", lambda e, q=q, a=a, m=m, jt=jt: e.scalar_tensor_tensor(
                                out=ACC[a][:], in0=CGH[q][:, HALO - 1 + jt:HALO - 1 + jt + T],
                                scalar=cfc(l, CF_W3, jt * 8 + m), in1=ACC[a][:], op0=ALU.mult, op1=ALU.add),
                                reads=[bCGH[q], bACC[a], bCONST], writes=[bACC[a]])
                        acc_of[m] = a
                        S.pump(3)
                    else:
                        ps, pbufs = fm_mm(slot, cgi, lambda k: xf[:, k, HALO:HALO + T], [bxf], False)
                        a = acc_of[m]
                        S.op("dve", lambda e, ps=ps, a=a, m=m: e.scalar_tensor_tensor(
                            out=YB(m), in0=ps, scalar=cfc(l, CF_BIN, SEG_BG * 8 + m), in1=ACC[a][:],
                            op0=ALU.add, op1=ALU.mult), reads=pbufs + [bACC[a], bCONST], writes=[bYB[m]])

            def act_cnorm():
                i1, h1 = misc_bank()
                i2, h2 = misc_bank()
                for k in range(8):
                    S.op("pe", lambda e, k=k: e.matmul(bank_ap(i1, h1), lhsT=ONES[:, :], rhs=CP[:, k, :], start=(k == 0), stop=(k == 7)),
                         reads=[bCONST, bCP[k]], writes=[bPB[i1][h1]], signal=(k == 7))
                for k in range(8):
                    S.op("pe", lambda e, k=k: e.matmul(bank_ap(i2, h2), lhsT=ONES[:, :], rhs=SQ[:, k, :], start=(k == 0), stop=(k == 7)),
                         reads=[bCONST, bSQ[k]], writes=[bPB[i2][h2]], signal=(k == 7))
                S.op("act", lambda e: e.activation(out=MEAN[:], in_=bank_ap(i1, h1), func=AF.Identity, scale=1.0 / D),
                     reads=[bPB[i1][h1]], writes=bSTATL)
                S.op("act", lambda e: e.activation(out=E2[:], in_=bank_ap(i2, h2), func=AF.Identity, scale=1.0 / D),
                     reads=[bPB[i2][h2]], writes=bSTATL)
                S.op("dve", lambda e: e.scalar_tensor_tensor(out=RSTD[:], in0=MEAN[:], scalar=-1.0, in1=MEAN[:],
                                                             op0=ALU.mult, op1=ALU.mult), reads=bSTATL, writes=bSTATL)
                S.op("dve", lambda e: e.scalar_tensor_tensor(out=E2[:], in0=E2[:], scalar=EPS, in1=RSTD[:],
                                                             op0=ALU.add, op1=ALU.add), reads=bSTATL, writes=bSTATL)
                S.op("act", lambda e: e.activation(out=RSTD[:], in_=E2[:], func=AF.Sqrt), reads=bSTATL, writes=bSTATL)
                S.op("dve", lambda e: e.reciprocal(out=RSTD[:], in_=RSTD[:]), reads=bSTATL, writes=bSTATL)
                for m in range(8):
                    q = m % 2
                    S.defer("pool", lambda e, m=m, q=q: e.tensor_tensor(out=ZT[q][:], in0=CP[:, m, :], in1=MEAN[:], op=ALU.subtract),
                            reads=[bCP[m]] + bSTATL, writes=[bZT[q]])
                    S.defer("pool", lambda e, m=m, q=q: e.tensor_tensor(out=ZT[q][:], in0=ZT[q][:], in1=RSTD[:], op=ALU.mult),
                            reads=[bZT[q]] + bSTATL, writes=[bZT[q]])
                    S.defer("act", lambda e, m=m, q=q: e.activation(out=CP[:, m, :], in_=ZT[q][:], func=AF.Silu,
                                                                     bias=cfc(l, CF_CNB, m), scale=cfc(l, CF_CNG, m)),
                            reads=[bZT[q], bCONST], writes=[bCP[m]])

            Ysrc = [(YA, bYA), (YB, bYB), (lambda m: CP[:, m, :], bCP)]

            def blk_gate(p, slot):
                n, mb = p
                r = (mb * 3 + n) % 2
                for cgi, (seg, m) in enumerate(in_groups("gate", p)):
                    ps, pbufs = fm_mm(slot, cgi, lambda k: xf[:, k, HALO:HALO + T], [bxf], False)
                    S.op("act", lambda e, ps=ps, m=m, cgi=cgi, r=r, seg=seg: e.activation(
                        out=SGT[r][:, cgi, :], in_=ps, func=AF.Sigmoid, bias=cfc(l, CF_BIN, seg * 8 + m), scale=1.0),
                        reads=pbufs + [bCONST], writes=[bSGT[r]])

            def blk_br(p, slot):
                n, mb = p
                r = (mb * 3 + n) % 2
                Yn, bYn = Ysrc[n]
                for cgi in range(4):
                    m = mb * 4 + cgi
                    ps, pbufs = fm_mm(slot, cgi, lambda k, Yn=Yn: Yn(k), list(bYn), False)
                    if n == 0:
                        S.op("dve", lambda e, ps=ps, cgi=cgi, r=r: e.tensor_tensor(out=MG[:, cgi, :], in0=ps, in1=SGT[r][:, cgi, :], op=ALU.mult),
                             reads=pbufs + [bSGT[r]], writes=[bMG[cgi]])
                    else:
                        q = rot()
                        S.op("dve", lambda e, ps=ps, q=q, cgi=cgi, r=r: e.tensor_tensor(out=TM1[q][:], in0=ps, in1=SGT[r][:, cgi, :], op=ALU.mult),
                             reads=pbufs + [bSGT[r]], writes=[bTM1[q]])
                        if n == 1:
                            S.op("pool", lambda e, q=q, cgi=cgi: e.tensor_tensor(out=MG[:, cgi, :], in0=MG[:, cgi, :], in1=TM1[q][:], op=ALU.add),
                                 reads=[bMG[cgi], bTM1[q]], writes=[bMG[cgi]])
                        else:
                            S.op("pool", lambda e, q=q, m=m, cgi=cgi: e.tensor_tensor(out=SQ[:, m, :], in0=MG[:, cgi, :], in1=TM1[q][:], op=ALU.add),
                                 reads=[bMG[cgi], bTM1[q]], writes=[bSQ[m]])

            if pendingT and pendingT[0][0] == gidx[(si, j)]:
                while pendingT:
                    pendingT.pop(0)[1]()
            handlers = {"u": blk_u, "cc": blk_cc, "dg": blk_dg, "v": blk_v, "cb": blk_cb, "gate": blk_gate, "br": blk_br}
            def run_tails():
                while pendingT:
                    pendingT.pop(0)[1]()

            def act_stageb():
                while pendingB:
                    stage_b(*pendingB.pop(0))
            acts = {"vln0": lambda: act_vln(0), "vln1": lambda: act_vln(1), "vln2": lambda: act_vln(2),
                    "vln3": lambda: act_vln(3), "sgu": act_sgu, "cnorm": act_cnorm, "stageB": act_stageb,
                    "flush": lambda: S.pump(), "haloT": halo_t, "tail": run_tails}
            si_ = 0
            while SCHED[si_][1] != "out":
                t_, k_, p_ = SCHED[si_]
                if t_ == "blk":
                    handlers[k_](p_, next_block())
                else:
                    acts[k_]()
                si_ += 1
            MGB, bMGB = SQ, bSQ
            gs = load_gb(LNB + 2)
            wslots = [next_block(), next_block(prefetch=False)]

            def wout_group(g):
                banks = []
                for hh in range(2):
                    i, h = next_bank()
                    banks.append((i, h))
                    S.op("pe", lambda e, i=i, h=h, hh=hh: e.matmul(bank_ap(i, h), lhsT=ONES[32:33, :],
                                                                    rhs=BROW[32:33, l, hh * 512:(hh + 1) * 512],
                                                                    start=True, stop=False, skip_group_check=True),
                         reads=[bCONST], writes=[bPB[i][h]], signal=False)
                    for k in range(8):
                        S.op("pe", lambda e, i=i, h=h, k=k, g=g, hh=hh: e.matmul(
                            bank_ap(i, h), lhsT=MGB[:, k, g * 128:(g + 1) * 128], rhs=WR[wslots[hh]][:, k, :],
                            start=False, stop=(k == 7), skip_group_check=True),
                            reads=[bWR[wslots[hh]], bMGB[k]], writes=[bPB[i][h]], signal=(k == 7))
                for hh, (i, h) in enumerate(banks):
                    S.op("dve", lambda e, i=i, h=h, g=g, hh=hh: e.scalar_tensor_tensor(
                        out=XT[s][:, g, hh * 512:(hh + 1) * 512], in0=XT[s][:, g, hh * 512:(hh + 1) * 512], scalar=ALPHA,
                        in1=bank_ap(i, h), op0=ALU.mult, op1=ALU.add), reads=[bPB[i][h], bXT[s][g]], writes=[bXT[s][g]])
            ln_groups(s, gs, pre=wout_group)
            act_stageb()
            X1F, bX1F = CP, bCP
            to_fm(s, s, X1F, list(bX1F), 0)
            for mb in range(8):
                slot = next_block()
                for cgi in range(4):
                    c = mb * 4 + cgi
                    ps, pbufs = fm_mm(slot, cgi, lambda k: X1F[:, k, :], list(bX1F), False)
                    q = rot()
                    S.op("act", lambda e, ps=ps, q=q, c=c: e.activation(out=RR[q][:], in_=ps, func=AF.Relu,
                                                                         bias=cfc(l, CF_BFF1, c), scale=1.0),
                         reads=pbufs + [bCONST], writes=[bRR[q]])
                    S.op("pool", lambda e, q=q, c=c: e.tensor_tensor(out=R[:, c, :], in0=RR[q][:], in1=RR[q][:], op=ALU.mult),
                         reads=[bRR[q]], writes=[bR[c]])
            gs = load_gb(LNB + 4)
            ln_ks = []
            for hh in range(2):
                banks = [next_bank() for _ in range(4)]
                for kb in range(4):
                    slot = next_block()
                    for g in range(4):
                        i, h = banks[g]
                        if kb == 0:
                            S.op("pe", lambda e, i=i, h=h, hh=hh: e.matmul(bank_ap(i, h), lhsT=ONES[64:65, :],
                                                                            rhs=BROW[64:65, l, hh * 512:(hh + 1) * 512],
                                                                            start=True, stop=False, skip_group_check=True),
                                 reads=[bCONST], writes=[bPB[i][h]], signal=False)
                        for k in range(8):
                            kk = kb * 8 + k
                            lastmm = (kb == 3 and k == 7)
                            S.op("pe", lambda e, i=i, h=h, g=g, k=k, kk=kk, lastmm=lastmm, slot=slot: e.matmul(
                                bank_ap(i, h), lhsT=R[:, kk, g * 128:(g + 1) * 128], rhs=WR[slot][:, k, :],
                                start=False, stop=lastmm, skip_group_check=True),
                                reads=[bWR[slot], bR[kk]], writes=[bPB[i][h]], signal=(k == 7))
                        if kb == 3:
                            S.op("dve", lambda e, i=i, h=h, g=g, hh=hh: e.scalar_tensor_tensor(
                                out=XT[s][:, g, hh * 512:(hh + 1) * 512], in0=XT[s][:, g, hh * 512:(hh + 1) * 512], scalar=ALPHA,
                                in1=bank_ap(i, h), op0=ALU.mult, op1=ALU.add), reads=[bPB[i][h], bXT[s][g]], writes=[bXT[s][g]])
                            if hh == 1:
                                ln_ks.append(ln_stats(XT[s][:, g, :], [bXT[s][g]]))
                                if g >= 1:
                                    ln_apply(ln_ks[g - 1], XT[s][:, g - 1, :], XT[s][:, g - 1, :], [bXT[s][g - 1]], [bXT[s][g - 1]], gs)
            ln_apply(ln_ks[3], XT[s][:, 3, :], XT[s][:, 3, :], [bXT[s][3]], [bXT[s][3]], gs)
            if l == 0:
                def tail():
                    to_fm(s, s, XF[sf], [bXFc[sf]], HALO)
                    if not first:
                        sp_ = 1 - sf
                        S.op("pool", lambda e: e.tensor_copy(out=XF[sf][:, :, 0:HALO], in_=XF[sp_][:, :, T:T + HALO]),
                             reads=[bXFc[sp_]], writes=[bXFh[sf]])
                        S.op("pool", lambda e: e.tensor_copy(out=XF[sp_][:, :, HALO + T:WIN], in_=XF[sf][:, :, HALO:2 * HALO]),
                             reads=[bXFc[sf]], writes=[bXFh[sp_]])
                pendingT.append((gidx[(si, j)], tail))
            else:
                r0 = row0 + j * T
                for g in range(4):
                    S.dma("pool", lambda e, g=g: e.dma_start(out=yout[r0 + g * 128:r0 + (g + 1) * 128, :], in_=XT[s][:, g, :]),
                          f"o{s}{g}", reads=[bXT[s][g]])

        stage_a(*tiles[0])
        stage_b(*tiles[0])
        pump_cast(3)
        for n_, (kind, si, j) in enumerate(tile_layers):
            if kind == "L1":
                layer(0, si, j)
                nxt = gidx[(si, j)] + 1
                if nxt < len(tiles):
                    stage_a(*tiles[nxt])
                    pendingB.append(tiles[nxt])
                    if not (n_ + 1 < len(tile_layers) and tile_layers[n_ + 1][0] == "L2"):
                        stage_b(*pendingB.pop(0))
            else:
                layer(1, si, j)
        S.wait_all("sp", [f"o{s_}{g}" for s_ in range(3) for g in range(4)] + ["pe", "act", "dve", "pool"])

        engs = {"sp": "sync", "act": "scalar", "dve": "vector", "pool": "gpsimd", "pe": "tensor"}
        with nc.Block() as block:
            def make(key):
                def body(e):
                    for it in S.streams[key]:
                        if it[0] == "wait":
                            e.wait_ge(sems[it[1]], it[2])
                        else:
                            ins = it[1](e)
                            if it[3]:
                                ins.then_inc(sems[it[2]], it[3])
                return body
            for key, attr in engs.items():
                getattr(block, attr)(make(key))
    return nc


def pack_aux(ln_in_g, ln_in_b, b_in, sgu_ln_g, sgu_ln_b, sgu_w, sgu_b, sconv_w, cconv_w, cconv_b,
             cnorm_g, cnorm_b, b_out, ln1_g, ln1_b, b_ff1, b_ff2, ln2_g, ln2_b):
    f = lambda a: np.asarray(a, dtype=np.float32)
    cfm = np.empty((128, 2 * CF_N), np.float32)
    for l in range(2):
        o = l * CF_N
        cfm[:, o + CF_BIN:o + CF_BIN + 80] = f(b_in)[l].reshape(80, 128).T
        cfm[:, o + CF_BFF1:o + CF_BFF1 + 32] = f(b_ff1)[l].reshape(32, 128).T
        cfm[:, o + CF_CCB:o + CF_CCB + 8] = f(cconv_b)[l].reshape(8, 128).T
        cfm[:, o + CF_CNG:o + CF_CNG + 8] = f(cnorm_g)[l].reshape(8, 128).T
        cfm[:, o + CF_CNB:o + CF_CNB + 8] = f(cnorm_b)[l].reshape(8, 128).T
        cfm[:, o + CF_W3:o + CF_W3 + 24] = f(sconv_w)[l].reshape(3, 8, 128).transpose(2, 0, 1).reshape(128, 24)
    brow32 = np.empty((2, 4, D), np.float32)
    for l in range(2):
        brow32[l, 0] = f(b_in)[l, SEG_V * 1024:(SEG_V + 1) * 1024]
        brow32[l, 1] = f(b_out)[l]
        brow32[l, 2] = f(b_ff2)[l]
        brow32[l, 3] = f(sgu_b)[l].reshape(-1)
    lnp = np.empty((14, 128, D), np.float32)
    vecs = [f(ln_in_g), f(ln_in_b)]
    for l in range(2):
        vecs += [f(sgu_ln_g)[l], f(sgu_ln_b)[l], f(ln1_g)[l], f(ln1_b)[l], f(ln2_g)[l], f(ln2_b)[l]]
    for i, v in enumerate(vecs):
        lnp[i] = np.broadcast_to(v[None, :], (128, D))
    wst32 = np.ascontiguousarray(f(sgu_w).transpose(0, 3, 1, 2)).reshape(2, 128, 8 * 128)
    ident = np.eye(128, dtype=np.float32)

    return {"cfm": cfm, "brow32": brow32, "lnp": lnp, "wst32": wst32, "ident": ident}


_NC_CACHE = {}


def kernel(x_prompt, x_sample, ln_in_g, ln_in_b, w_in, b_in, sgu_ln_g, sgu_ln_b, sgu_w, sgu_b,
           sconv_w, cconv_w, cconv_b, cnorm_g, cnorm_b, w_branch, w_out, b_out,
           ln1_g, ln1_b, w_ff1, b_ff1, w_ff2, b_ff2, ln2_g, ln2_b):
    f = lambda a: np.asarray(a, dtype=np.float32)
    x_prompt, x_sample = f(x_prompt), f(x_sample)
    w_in, w_branch, w_out, w_ff1, w_ff2 = f(w_in), f(w_branch), f(w_out), f(w_ff1), f(w_ff2)
    wpk32 = pack_weights(w_in, w_branch, w_out, w_ff1, w_ff2, f(cconv_w))
    aux = pack_aux(ln_in_g, ln_in_b, b_in, sgu_ln_g, sgu_ln_b, sgu_w, sgu_b, sconv_w, cconv_w, cconv_b,
                   cnorm_g, cnorm_b, b_out, ln1_g, ln1_b, b_ff1, b_ff2, ln2_g, ln2_b)
    in_maps = []
    for c in range(NCORES):
        xin = np.concatenate([x_prompt[2 * c:2 * c + 2].reshape(-1, D), x_sample[4 * c:4 * c + 4].reshape(-1, D)], axis=0)
        in_maps.append(dict(aux, xin=np.ascontiguousarray(xin), wpk32=wpk32))
    if "nc" not in _NC_CACHE:
        _NC_CACHE["nc"] = build_program()
    res = run_bass_kernel_spmd(_NC_CACHE["nc"], in_maps, core_ids=list(range(NCORES)))
    y_prompt = np.empty((16, 4096, D), np.float32)
    y_sample = np.empty((32, 2048, D), np.float32)
    for c in range(NCORES):
        y = np.asarray(res.results[c]["yout"])
        y_prompt[2 * c:2 * c + 2] = y[:8192].reshape(2, 4096, D)
        y_sample[4 * c:4 * c + 4] = y[8192:].reshape(4, 2048, D)
    return (y_prompt, y_sample)
```
